# Optimizing a Trainium2 kernel written in Bass

```python
import math
import jax, jax.numpy as jnp
from jax import lax
import numpy as np

D_MODEL = 1024
BATCH = 8
SEQ = 2048
DEPTH = 2
DEC_BATCH = 32
DEC_SEQ = 4
PAST_LEN = 8192
PAGE_SIZE = 128

HEAD_DIM = 128
H_A = D_MODEL // (2 * HEAD_DIM)
DK_A = HEAD_DIM
DV_A = HEAD_DIM
H_B = D_MODEL // (2 * HEAD_DIM)
DK_B = HEAD_DIM
DV_B = HEAD_DIM
H_C = D_MODEL // (2 * HEAD_DIM)
DH_C = HEAD_DIM
D_FF = 4 * D_MODEL
N_EVEN = (DEPTH + 1) // 2
N_ODD = DEPTH // 2
CHUNK_A = 16
CHUNK_B = 128
Q_BLOCK = 128
ROPE_BASE = 10000.0
NORM_EPS = 1e-6
AB_WIDTHS = (H_A * DK_A, H_A * DK_A, H_A * DV_A, H_A * DV_A,
             H_B * DK_B, H_B * DK_B, H_B * DV_B, H_B * DV_B)
D_IN_AB = sum(AB_WIDTHS)
D_MIX_AB = H_A * DV_A + H_B * DV_B
C_WIDTHS = (H_C * 2 * DH_C, H_C * 2 * DH_C, H_C * 2 * DH_C)
D_IN_C = sum(C_WIDTHS)
D_MIX_C = H_C * 2 * DH_C

kernel_name = 'hgrn2_retention_diffattn_adaln_decoder_step'


def rmsnorm(x, w=None):
    xf = x.astype(jnp.float32)
    y = xf * lax.rsqrt(jnp.mean(xf * xf, axis=-1, keepdims=True) + NORM_EPS)
    if w is not None:
        y = y * w.astype(jnp.float32)
    return y.astype(x.dtype)


def split_cols(x, widths):
    out, off = [], 0
    for w in widths:
        out.append(x[..., off:off + w])
        off += w
    return out


def rotary(x, pos):
    d = x.shape[-1]
    inv = 1.0 / (ROPE_BASE ** jnp.linspace(0.0, 1.0, d // 2, dtype=jnp.float32))
    ang = pos.astype(jnp.float32)[:, None] * inv[None, :]
    cos = jnp.cos(ang)[None, :, None, :]
    sin = jnp.sin(ang)[None, :, None, :]
    xp = x.astype(jnp.float32).reshape(*x.shape[:-1], d // 2, 2)
    x0, x1 = xp[..., 0], xp[..., 1]
    out = jnp.stack([x0 * cos - x1 * sin, x1 * cos + x0 * sin], axis=-1)
    return out.reshape(x.shape).astype(x.dtype)


def chunked_gated_linear_attn(q, k, v, log_f, state0, chunk):
    f32 = jnp.float32
    B, T, H, K = q.shape
    n = T // chunk
    scalar = log_f.shape[-1] == 1

    def blocks(a):
        return jnp.moveaxis(a.astype(f32).reshape(B, n, chunk, *a.shape[2:]), 1, 0)

    causal = jnp.tril(jnp.ones((chunk, chunk), dtype=bool))[None, :, :, None, None]

    def step(S, inp):
        qc, kc, vc, lc = inp
        G = jnp.cumsum(lc, axis=1)
        rel = jnp.exp(jnp.where(causal, G[:, :, None] - G[:, None, :], -jnp.inf))
        if scalar:
            A = jnp.einsum('bthk,bshk->btsh', qc, kc) * rel[..., 0]
        else:
            A = jnp.einsum('bthk,bshk,btshk->btsh', qc, kc, rel)
        o = jnp.einsum('btsh,bshv->bthv', A, vc) + jnp.einsum('bthk,bhkv->bthv', qc * jnp.exp(G), S)
        G_end = G[:, -1]
        k_dec = kc * jnp.exp(G_end[:, None] - G)
        S = S * jnp.exp(G_end)[..., None] + jnp.einsum('bshk,bshv->bhkv', k_dec, vc)
        return S, o

    S, o = lax.scan(step, state0.astype(f32), (blocks(q), blocks(k), blocks(v), blocks(log_f)))
    o = jnp.moveaxis(o, 0, 1).reshape(B, T, H, v.shape[-1])
    return o.astype(v.dtype), S.astype(state0.dtype)


def hgrn2_retention_mixer(h, pos0, s_a0, s_b0, w_in, w_out, lb, norm_a):
    B, T, _ = h.shape
    f32 = jnp.float32
    qa, fa, ia, ga, qb, kb, vb, gb = split_cols(h @ w_in, AB_WIDTHS)
    fz = lb + (1.0 - lb) * jax.nn.sigmoid(fa.astype(f32))
    log_fa = jnp.log(fz).reshape(B, T, H_A, DK_A)
    ka = (1.0 - fz).reshape(B, T, H_A, DK_A)
    qa = jax.nn.silu(qa).reshape(B, T, H_A, DK_A) * DK_A ** -0.5
    oa, s_a = chunked_gated_linear_attn(qa, ka, ia.reshape(B, T, H_A, DV_A), log_fa, s_a0,
                                        math.gcd(T, CHUNK_A))
    oa = rmsnorm(oa, norm_a) * jax.nn.silu(ga.reshape(B, T, H_A, DV_A))
    pos = pos0 + jnp.arange(T)
    qb = rotary(qb.reshape(B, T, H_B, DK_B), pos)
    kb = rotary(kb.reshape(B, T, H_B, DK_B), pos) * DK_B ** -0.5
    log_gamma = jnp.log(1.0 - 2.0 ** (-5.0 - jnp.arange(H_B, dtype=f32)))
    log_g = jnp.broadcast_to(log_gamma[:, None], (B, T, H_B, 1))
    ob, s_b = chunked_gated_linear_attn(qb, kb, vb.reshape(B, T, H_B, DV_B), log_g, s_b0,
                                        math.gcd(T, CHUNK_B))
    ob = rmsnorm(ob) * jax.nn.silu(gb.reshape(B, T, H_B, DV_B))
    o = jnp.concatenate([oa.reshape(B, T, -1), ob.reshape(B, T, -1)], axis=-1).astype(h.dtype)
    return o @ w_out, s_a, s_b


def diff_attn_causal(q, k, v, lam):
    B, T, H, _, dh = q.shape
    qb = math.gcd(T, Q_BLOCK)
    nb = T // qb
    scale = DH_C ** -0.5
    q_blocks = jnp.moveaxis(q.reshape(B, nb, qb, H, 2, dh), 1, 0)
    kpos = jnp.arange(T)

    def block(args):
        qi, i = args
        s = jnp.einsum('bqhjd,bkhjd->bhjqk', qi, k).astype(jnp.float32) * scale
        qpos = i * qb + jnp.arange(qb)
        s = jnp.where(kpos[None, :] <= qpos[:, None], s, -jnp.inf)
        p = jax.nn.softmax(s, axis=-1).astype(v.dtype)
        o = jnp.einsum('bhjqk,bkhe->bqhje', p, v)
        return o[:, :, :, 0] - lam * o[:, :, :, 1]

    o = lax.map(block, (q_blocks, jnp.arange(nb)))
    return jnp.moveaxis(o, 0, 1).reshape(B, T, H, v.shape[-1])


def diff_attn_with_past(q, k, v, k_past, v_past, lam):
    T = q.shape[1]
    P = k_past.shape[1]
    scale = DH_C ** -0.5
    s_past = jnp.einsum('bqhjd,bkhjd->bhjqk', q, k_past).astype(jnp.float32) * scale
    s_new = jnp.einsum('bqhjd,bkhjd->bhjqk', q, k).astype(jnp.float32) * scale
    s_new = jnp.where(jnp.tril(jnp.ones((T, T), dtype=bool)), s_new, -jnp.inf)
    p = jax.nn.softmax(jnp.concatenate([s_past, s_new], axis=-1), axis=-1).astype(v.dtype)
    o = (jnp.einsum('bhjqk,bkhe->bqhje', p[..., :P], v_past)
         + jnp.einsum('bhjqk,bkhe->bqhje', p[..., P:], v))
    return o[:, :, :, 0] - lam * o[:, :, :, 1]


def diff_attention(h, w_in, w_out, lam_p, subln_w, lam_init, k_past, v_past):
    B, T, _ = h.shape
    q, k, v = split_cols(h @ w_in, C_WIDTHS)
    q = q.reshape(B, T, H_C, 2, DH_C)
    k = k.reshape(B, T, H_C, 2, DH_C)
    v = v.reshape(B, T, H_C, 2 * DH_C)
    lp = lam_p.astype(jnp.float32)
    lam = jnp.exp(jnp.sum(lp[0] * lp[1])) - jnp.exp(jnp.sum(lp[2] * lp[3])) + lam_init
    if k_past is None:
        o = diff_attn_causal(q, k, v, lam)
    else:
        o = diff_attn_with_past(q, k, v, k_past, v_past, lam)
    o = rmsnorm(o, subln_w) * (1.0 - lam_init)
    return o.reshape(B, T, D_MIX_C).astype(h.dtype) @ w_out, k, v


def squared_relu_mlp(h, w_up, w_down):
    u = jax.nn.relu(h @ w_up)
    return (u * u) @ w_down


def trunk(x, c, pos0, init_hgrn, init_ret, cache_k, cache_v, page_table,
          w_ada, b_ada, norm_w, w_in_ab, w_out_ab, hgrn_lb_logits, hgrn_norm_w,
          w_in_c, w_out_c, diff_lambda, diff_subln_w, w_mlp_up, w_mlp_down, final_norm_w):
    B, T, _ = x.shape
    lb_all = jnp.cumsum(jax.nn.softmax(hgrn_lb_logits.astype(jnp.float32), axis=0), axis=0)
    sc = jax.nn.silu(c)
    hg_out, rt_out, k_out, v_out = [], [], [], []
    for l in range(DEPTH):
        mod = sc @ w_ada[l] + b_ada[l]
        sh1, sc1, g1, sh2, sc2, g2 = [m[:, None, :] for m in jnp.split(mod, 6, axis=-1)]
        h = rmsnorm(x, norm_w[l, 0]) * (1.0 + sc1) + sh1
        if l % 2 == 0:
            e = l // 2
            y, s_a, s_b = hgrn2_retention_mixer(h, pos0, init_hgrn[e], init_ret[e], w_in_ab[e],
                                                w_out_ab[e], lb_all[e], hgrn_norm_w[e])
            hg_out.append(s_a)
            rt_out.append(s_b)
        else:
            o = l // 2
            lam_init = 0.8 - 0.6 * math.exp(-0.3 * l)
            if cache_k is None:
                k_past, v_past = None, None
            else:
                k_past = cache_k[o, page_table].reshape(B, -1, H_C, 2, DH_C)
                v_past = cache_v[o, page_table].reshape(B, -1, H_C, 2 * DH_C)
            y, k_new, v_new = diff_attention(h, w_in_c[o], w_out_c[o], diff_lambda[o],
                                             diff_subln_w[o], lam_init, k_past, v_past)
            k_out.append(k_new)
            v_out.append(v_new)
        x = x + g1 * y
        h = rmsnorm(x, norm_w[l, 1]) * (1.0 + sc2) + sh2
        x = x + g2 * squared_relu_mlp(h, w_mlp_up[l], w_mlp_down[l])
    return (rmsnorm(x, final_norm_w), jnp.stack(hg_out), jnp.stack(rt_out),
            jnp.stack(k_out), jnp.stack(v_out))


def setup_inputs(seed: int = 0) -> dict:
    key = jax.random.key(seed)
    ks = jax.random.split(key, 24)
    f32 = jnp.float32

    def nrm(k, shape, s):
        return jax.random.normal(k, shape, f32) * s

    n_pages = PAST_LEN // PAGE_SIZE
    n_used = DEC_BATCH * n_pages
    n_phys = n_used + max(1, n_used // 4)
    page_table = jax.random.permutation(ks[6], n_phys)[:n_used].reshape(DEC_BATCH, n_pages).astype(jnp.int32)
    return {
        'x_prompt': nrm(ks[0], (BATCH, SEQ, D_MODEL), 1.0),
        'x_sample': nrm(ks[1], (DEC_BATCH, DEC_SEQ, D_MODEL), 1.0),
        'state_hgrn': nrm(ks[2], (N_EVEN, DEC_BATCH, H_A, DK_A, DV_A), 0.5),
        'state_ret': nrm(ks[3], (N_EVEN, DEC_BATCH, H_B, DK_B, DV_B), 1.0),
        'cache_k': nrm(ks[4], (N_ODD, n_phys, PAGE_SIZE, H_C, 2, DH_C), 1.0),
        'cache_v': nrm(ks[5], (N_ODD, n_phys, PAGE_SIZE, H_C, 2 * DH_C), 1.0),
        'page_table': page_table,
        'c_prompt': nrm(ks[7], (BATCH, D_MODEL), 1.0),
        'c_sample': nrm(ks[8], (DEC_BATCH, D_MODEL), 1.0),
        'w_ada': nrm(ks[9], (DEPTH, D_MODEL, 6 * D_MODEL), 0.5 * D_MODEL ** -0.5),
        'b_ada': nrm(ks[10], (DEPTH, 6 * D_MODEL), 0.02),
        'norm_w': 1.0 + nrm(ks[11], (DEPTH, 2, D_MODEL), 0.02),
        'w_in_ab': nrm(ks[12], (N_EVEN, D_MODEL, D_IN_AB), D_MODEL ** -0.5),
        'w_out_ab': nrm(ks[13], (N_EVEN, D_MIX_AB, D_MODEL), D_MIX_AB ** -0.5),
        'hgrn_lb_logits': nrm(ks[14], (N_EVEN + 1, H_A * DK_A), 0.1),
        'hgrn_norm_w': 1.0 + nrm(ks[15], (N_EVEN, DV_A), 0.02),
        'w_in_c': nrm(ks[16], (N_ODD, D_MODEL, D_IN_C), D_MODEL ** -0.5),
        'w_out_c': nrm(ks[17], (N_ODD, D_MIX_C, D_MODEL), D_MIX_C ** -0.5),
        'diff_lambda': nrm(ks[18], (N_ODD, 4, DH_C), 0.1),
        'diff_subln_w': 1.0 + nrm(ks[19], (N_ODD, 2 * DH_C), 0.02),
        'w_mlp_up': nrm(ks[20], (DEPTH, D_MODEL, D_FF), D_MODEL ** -0.5),
        'w_mlp_down': nrm(ks[21], (DEPTH, D_FF, D_MODEL), D_FF ** -0.5),
        'final_norm_w': 1.0 + nrm(ks[22], (D_MODEL,), 0.02),
    }


def reference(x_prompt, x_sample, state_hgrn, state_ret, cache_k, cache_v, page_table,
              c_prompt, c_sample, w_ada, b_ada, norm_w, w_in_ab, w_out_ab, hgrn_lb_logits,
              hgrn_norm_w, w_in_c, w_out_c, diff_lambda, diff_subln_w, w_mlp_up, w_mlp_down,
              final_norm_w):
    B, T, _ = x_prompt.shape
    zeros_a = jnp.zeros((N_EVEN, B, H_A, DK_A, DV_A), x_prompt.dtype)
    zeros_b = jnp.zeros((N_EVEN, B, H_B, DK_B, DV_B), x_prompt.dtype)
    y_prompt, hg_p, rt_p, k_p, v_p = trunk(
        x_prompt, c_prompt, 0, zeros_a, zeros_b, None, None, None,
        w_ada, b_ada, norm_w, w_in_ab, w_out_ab, hgrn_lb_logits, hgrn_norm_w,
        w_in_c, w_out_c, diff_lambda, diff_subln_w, w_mlp_up, w_mlp_down, final_norm_w)
    past_len = page_table.shape[1] * cache_k.shape[2]
    y_sample, hg_s, rt_s, k_s, v_s = trunk(
        x_sample, c_sample, past_len, state_hgrn, state_ret, cache_k, cache_v, page_table,
        w_ada, b_ada, norm_w, w_in_ab, w_out_ab, hgrn_lb_logits, hgrn_norm_w,
        w_in_c, w_out_c, diff_lambda, diff_subln_w, w_mlp_up, w_mlp_down, final_norm_w)
    k_p = k_p.reshape(N_ODD, B, T // PAGE_SIZE, PAGE_SIZE, H_C, 2, DH_C)
    v_p = v_p.reshape(N_ODD, B, T // PAGE_SIZE, PAGE_SIZE, H_C, 2 * DH_C)
    return (y_prompt, y_sample, hg_p, rt_p, k_p, v_p, hg_s, rt_s, k_s, v_s)
```

```python
import math
import numpy as np
import concourse.bass as bass
import concourse.mybir as mybir
from concourse.bass_utils import run_bass_kernel_spmd

F32 = mybir.dt.float32
BF16 = mybir.dt.bfloat16
I32 = mybir.dt.int32
AF = mybir.ActivationFunctionType
ALU = mybir.AluOpType

ENGS = ("pe", "act", "dve", "pool", "sp")
D = 1024
NCH = 8
DFF = 4096
EPS = 1e-6
LAM_INIT = 0.8 - 0.6 * math.exp(-0.3 * 1)
DK_SCALE = 128.0 ** -0.5


class Cfg:
    def __init__(self, S=2048, NPG=64, NPHYS=2560, n_cores=8, debug=False):
        self.S = S
        self.NPG = NPG
        self.NPHYS = NPHYS
        self.n_cores = n_cores
        self.NT = S + 16
        self.POS0 = NPG * 128
        self.debug = debug
        self.tiles = [(t0, 512) for t0 in range(0, S, 512)] + [(S, 16)]


class KB:
    def __init__(self, nc):
        self.nc = nc
        self.q = {e: [] for e in ENGS}
        self.sems = {e: nc.alloc_semaphore("sem_" + e) for e in ENGS}
        self.cnt = {e: 0 for e in ENGS}
        self.seen = {e: {} for e in ENGS}
        self.lastw = {}
        self.readers = {}

    def dsem(self, name):
        if name not in self.sems:
            self.sems[name] = self.nc.alloc_semaphore("dsem_" + name)
            self.cnt[name] = 0
        return name

    def _deps(self, eng, r, w, skip_sem=None):
        deps = {}

        def need(x):
            if x is None:
                return
            sk, v = x
            if v > deps.get(sk, 0):
                deps[sk] = v
        for b in r:
            need(self.lastw.get(b))
        for b in w:
            need(self.lastw.get(b))
            for rd in self.readers.get(b, ()):
                need(rd)
        waits = []
        for sk, v in deps.items():
            if sk == skip_sem:
                continue
            if self.seen[eng].get(sk, 0) >= v:
                continue
            self.seen[eng][sk] = v
            waits.append((sk, v))
        return waits

    def _commit(self, me, r, w):
        for b in w:
            self.lastw[b] = me
            self.readers[b] = []
        for b in r:
            self.readers.setdefault(b, []).append(me)

    @staticmethod
    def _is_psum(k):
        return len(k) == 2 and k[0] == "P" and k[1].isdigit()

    def op(self, eng, fn, r=(), w=()):
        w = tuple(w) + tuple(k for k in r if self._is_psum(k))
        r = tuple(k for k in r if not self._is_psum(k))
        waits = self._deps(eng, r, w, skip_sem="pe" if eng == "pe" else None)
        self.cnt[eng] += 1
        me = (eng, self.cnt[eng])
        self.q[eng].append((waits, fn, eng, 1))
        self._commit(me, r, w)

    def dma(self, eng, fn, sem, r=(), w=(), group=False):
        r = tuple(r); w = tuple(w)
        self.dsem(sem)
        waits = self._deps(eng, r, w, skip_sem=sem if group else None)
        self.cnt[sem] += 16
        me = (sem, self.cnt[sem])
        self.q[eng].append((waits, fn, sem, 16))
        self._commit(me, r, w)

    def barrier(self):
        for e in ENGS:
            waits = []
            for sk, v in self.cnt.items():
                if v > self.seen[e].get(sk, 0):
                    self.seen[e][sk] = v
                    waits.append((sk, v))
            if waits:
                self.q[e].append((waits, None, None, 0))

    def emit(self):
        nc = self.nc
        with nc.Block() as block:
            def mk(ename):
                def body(e):
                    for waits, fn, incsem, incv in self.q[ename]:
                        for sk, v in waits:
                            e.wait_ge(self.sems[sk], v)
                        if fn is not None:
                            fn(e).then_inc(self.sems[incsem], incv)
                return body
            block.tensor(mk("pe"))
            block.scalar(mk("act"))
            block.vector(mk("dve"))
            block.gpsimd(mk("pool"))
            block.sync(mk("sp"))


def I(name, *a, **k):
    return lambda e: getattr(e, name)(*a, **k)


class Arena:
    def __init__(self, nc):
        self.nc = nc
        self.off = (int(nc.sbuf_base) + 63) // 64 * 64
        self.limit = int(nc.sbuf_top)
        self.n = 0

    def alloc(self, name, shape, dtype):
        esz = 2 if dtype == BF16 else 4
        nbytes = int(np.prod(shape[1:])) * esz
        self.off = (self.off + 31) // 32 * 32
        assert self.off + nbytes <= self.limit, f"SBUF overflow allocating {name}: {self.off}+{nbytes}>{self.limit}"
        self.n += 1
        t = self.nc.alloc_sbuf_tensor_at(f"{name}_{self.n}", list(shape), dtype, offset=self.off)
        self.off += nbytes
        return t

    def mark(self):
        return self.off

    def reset(self, m):
        self.off = m


CF_IDENT, CF_MASK, CF_M0, CF_M1, CF_SEL2 = 0, 128, 256, 257, 258
NCF = 274
CF_DT, CF_G1, CF_KD128, CF_KD4 = 0, 512, 1024, 1028
NCFB = 1032


def make_consts(cfg):
    cf = np.zeros((128, NCF), np.float64)
    cfb = np.zeros((128, NCFB), np.float64)
    cf[:, CF_IDENT:CF_IDENT + 128] = np.eye(128)
    s = np.arange(128)[:, None]
    t = np.arange(128)[None, :]
    cf[:, CF_MASK:CF_MASK + 128] = (s <= t)
    for h in range(4):
        g = 1.0 - 2.0 ** (-5.0 - h)
        cfb[:, CF_DT + h * 128:CF_DT + (h + 1) * 128] = np.where(t >= s, g ** np.maximum(t - s, 0), 0.0)
        cfb[:, CF_G1 + h * 128:CF_G1 + (h + 1) * 128] = g ** (t + 1.0)
        cfb[:, CF_KD128 + h] = g ** (127.0 - s[:, 0])
        cfb[:4, CF_KD4 + h] = g ** (3.0 - s[:4, 0])
        cf[32 * h:32 * h + 4, CF_M0] = 1.0
        cf[32 * h + 4:32 * h + 8, CF_M1] = 1.0
        for j in range(2):
            for q in range(4):
                cf[32 * h + j * 4 + q, CF_SEL2 + h * 4 + q] = 1.0
    d = 128
    inv = 1.0 / (10000.0 ** np.linspace(0.0, 1.0, d // 2, dtype=np.float32).astype(np.float64))
    pos = np.concatenate([np.arange(cfg.S), np.tile(cfg.POS0 + np.arange(4), 4)]).astype(np.float64)
    ang = np.repeat(inv, 2)[:, None] * pos[None, :]
    rope = np.stack([np.cos(ang), np.sin(ang)]).astype(np.float32)
    return cf.astype(np.float32), cfb.astype(np.float32), rope


def build_program(cfg):
    nc = bass.Bass("TRN2", target_bir_lowering=False)
    S, NT, NPG = cfg.S, cfg.NT, cfg.NPG
    kb = KB(nc)

    def din(name, shape, dt=F32):
        return nc.dram_tensor(name, list(shape), dt, kind="ExternalInput").ap()

    def dout(name, shape, dt=F32):
        return nc.dram_tensor(name, list(shape), dt, kind="ExternalOutput").ap()

    xp_d = din("xp", [S, D]); xs_d = din("xs", [16, D])
    sth_d = din("st_h", [4, 4, 128, 128]); str_d = din("st_r", [4, 4, 128, 128])
    ck_d = din("ck", [cfg.NPHYS, 128, 1024]); cv_d = din("cv", [cfg.NPHYS, 128, 1024])
    pt_d = din("pt", [1, 4 * NPG], I32)
    cvec_d = din("cvec", [5, D])
    wada_d = din("w_ada", [2, D, 6 * D]); bada_d = din("b_ada", [96, 128])
    normw_d = din("norm_w", [32, 128])
    winab_d = din("w_in_ab", [8, D, 512]); woutab_d = din("w_out_ab", [D, D])
    lbl_d = din("lb_logits", [8, 128]); hnw_d = din("hgrn_norm_w", [1, 128])
    winc_d = din("w_in_c", [4, D, 768]); woutc_d = din("w_out_c", [D, D])
    dlam_d = din("diff_lambda", [1, 512]); subln_d = din("subln_w", [1, 256])
    wup_d = din("w_up", [2, D, DFF]); wdn_d = din("w_down", [2, DFF, D])
    fnw_d = din("final_norm_w", [1, D])
    cf_d = din("cf", [128, NCF]); cfb_d = din("cfb", [128, NCFB]); rope_d = din("rope", [2, 128, NT])

    yp_d = dout("y_p", [S, D]); ys_d = dout("y_s", [16, D])
    hgp_d = dout("hg_p", [4, 128, 128]); rtp_d = dout("rt_p", [4, 128, 128])
    kp_d = dout("k_p", [S, 1024]); vp_d = dout("v_p", [S, 1024])
    hgs_d = dout("hg_s", [4, 4, 128, 128]); rts_d = dout("rt_s", [4, 4, 128, 128])
    ks_d = dout("k_s", [16, 1024]); vs_d = dout("v_s", [16, 1024])
    out_keys = []

    ar = Arena(nc)
    A = ar.alloc
    xT = A("xT", [128, NCH, NT], F32)
    hT = A("hT", [128, NCH, NT], BF16)
    cf = A("cf", [128, NCF], F32)
    identb = A("identb", [128, 128], BF16)
    maskb = A("maskb", [128, 128], BF16)
    ones_f = A("ones_f", [128, 128], F32)
    zeros_f = A("zeros_f", [128, 64], F32)
    modT = A("modT", [128, 2 * 48 * 5], F32)
    wmT = A("wmT", [128, 4 * 8 * 5], F32)
    nwT = A("nwT", [128, 32], F32)
    bT = A("bT", [128, 96], F32)
    scT = A("scT", [128, NCH, 5], BF16)
    lbv = A("lbv", [128, 12], F32)
    na = A("na", [128, 1], F32)
    neglam = A("neglam", [128, 1], F32)
    sgn8 = A("sgn8", [128, 1], F32)
    wsub = A("wsub", [128, 256], F32)
    ident = cf[:, CF_IDENT:CF_IDENT + 128]
    maskf = cf[:, CF_MASK:CF_MASK + 128]
    sqb = [A(f"sqb{i}", [128, 512], F32) for i in range(2)]
    rstd = A("rstd", [128, 512], F32)
    tnrm = [A(f"tnrm{i}", [128, 512], F32) for i in range(2)]
    phase_base = ar.mark()

    P = [nc.alloc_psum_tensor(f"P{i}", [128, 512], F32) for i in range(8)]
    Pb = [p[:].bitcast(BF16) for p in P]

    def mod_ap(l, j, c, b0, nb=1):
        o = ((l * 6 + j) * 8 + c) * 5 + b0
        return modT[:, o:o + nb]

    def wm_ap(l, i, c, b0):
        o = ((l * 2 + i) * 8 + c) * 5 + b0
        return wmT[:, o:o + 1]

    def bsegs(t0, n):
        if t0 < S:
            return [(t0, n, 0)]
        return [(S + 4 * i, 4, 1 + i) for i in range(4)]

    setup_m = ar.mark()
    c5 = A("c5", [5, D], F32)
    s5 = A("s5", [5, D], F32)
    ldrow = A("ldrow", [128, 128], F32)
    lam_t = A("lam_t", [1, 520], F32)
    wada_sl = [A(f"wada{i}", [128, NCH, 512], BF16) for i in range(2)]

    kb.dma("sp", I("dma_start", out=cf[:], in_=cf_d[:]), "ld_cf", w=["cf"])
    kb.dma("sp", I("dma_start", out=c5[:], in_=cvec_d[:]), "ld_c5", w=["c5"])
    kb.op("dve", I("memset", ones_f[:], 1.0), w=["ones_f"])
    kb.op("dve", I("memset", zeros_f[:], 0.0), w=["zeros_f"])
    kb.op("dve", I("tensor_copy", identb[:], ident), r=["cf"], w=["identb"])
    kb.op("dve", I("tensor_copy", maskb[:], maskf), r=["cf"], w=["maskb"])

    kb.op("act", I("activation", out=s5[:], in_=c5[:], func=AF.Silu), r=["c5"], w=["s5"])
    for c in range(NCH):
        kb.op("pe", I("transpose", P[0][:, c * 5:(c + 1) * 5], s5[:, c * 128:(c + 1) * 128], cf[0:5, 0:5]),
              r=["s5", "cf"], w=["P0"])
    kb.op("dve", I("tensor_copy", scT[:].rearrange("p c b -> p (c b)"), P[0][:, 0:40]), r=["P0"], w=["scT"])

    def load_cols(src_ap, nrows, dst_ap, key, tag):
        kb.dma("sp", I("dma_start", out=ldrow[0:nrows, :], in_=src_ap), "ld_row", w=["ldrow"])
        kb.op("pe", I("transpose", P[1][:, 0:nrows], ldrow[0:nrows, :], cf[0:nrows, 0:nrows]),
              r=["ldrow", "cf"], w=["P1"])
        kb.op("dve", I("tensor_copy", dst_ap, P[1][:, 0:nrows]), r=["P1"], w=[key])

    load_cols(bada_d[:], 96, bT[:], "bT", "b")
    load_cols(normw_d[:], 32, nwT[:], "nwT", "n")
    lbt = A("lbt", [128, 8], F32)
    load_cols(lbl_d[:], 8, lbt[:], "lbt", "l")
    load_cols(hnw_d[:], 1, na[:], "na", "h")
    kb.op("dve", I("tensor_tensor", out=lbt[:, 0:4], in0=lbt[:, 0:4], in1=lbt[:, 4:8], op=ALU.subtract),
          r=["lbt"], w=["lbt"])
    kb.op("act", I("activation", out=lbv[:, 0:4], in_=lbt[:, 0:4], func=AF.Sigmoid), r=["lbt"], w=["lbv"])
    kb.op("dve", I("tensor_scalar", out=lbv[:, 4:8], in0=lbv[:, 0:4], scalar1=-1.0, scalar2=1.0,
                                           op0=ALU.mult, op1=ALU.add), r=["lbv"], w=["lbv"])
    kb.op("dve", I("tensor_scalar", out=lbv[:, 8:12], in0=lbv[:, 0:4], scalar1=-1.0, scalar2=None,
                                           op0=ALU.add), r=["lbv"], w=["lbv"])

    kb.dma("sp", I("dma_start", out=lam_t[:, 0:512], in_=dlam_d[:]), "ld_lam", w=["lam_t"])
    kb.op("dve", I("tensor_tensor", out=lam_t[:, 0:128], in0=lam_t[:, 0:128], in1=lam_t[:, 128:256], op=ALU.mult),
          r=["lam_t"], w=["lam_t"])
    kb.op("dve", I("tensor_tensor", out=lam_t[:, 256:384], in0=lam_t[:, 256:384], in1=lam_t[:, 384:512], op=ALU.mult),
          r=["lam_t"], w=["lam_t"])
    kb.op("dve", I("reduce_sum", out=lam_t[:, 512:513], in_=lam_t[:, 0:128], axis=mybir.AxisListType.X),
          r=["lam_t"], w=["lam_t"])
    kb.op("dve", I("reduce_sum", out=lam_t[:, 513:514], in_=lam_t[:, 256:384], axis=mybir.AxisListType.X),
          r=["lam_t"], w=["lam_t"])
    kb.op("act", I("activation", out=lam_t[:, 514:516], in_=lam_t[:, 512:514], func=AF.Exp), r=["lam_t"], w=["lam_t"])
    kb.op("dve", I("tensor_tensor", out=lam_t[:, 516:517], in0=lam_t[:, 515:516], in1=lam_t[:, 514:515], op=ALU.subtract),
          r=["lam_t"], w=["lam_t"])
    kb.op("dve", I("tensor_scalar", out=lam_t[:, 518:519], in0=lam_t[:, 516:517], scalar1=-LAM_INIT,
                                           scalar2=None, op0=ALU.add), r=["lam_t"], w=["lam_t"])
    kb.op("pe", I("matmul", P[1][:, 0:1], lhsT=ones_f[0:1, :], rhs=lam_t[:, 518:519], start=True, stop=True),
          r=["lam_t", "ones_f"], w=["P1"])
    kb.op("dve", I("tensor_copy", neglam[:], P[1][:, 0:1]), r=["P1"], w=["neglam"])
    kb.op("dve", I("scalar_tensor_tensor", out=sgn8[:, :], in0=cf[:, CF_M1:CF_M1 + 1], scalar=neglam[:, 0:1],
                                                  in1=cf[:, CF_M0:CF_M0 + 1], op0=ALU.mult, op1=ALU.add),
          r=["neglam", "cf"], w=["sgn8"])
    kb.dma("sp", I("dma_start", out=wsub[:], in_=subln_d[:].partition_broadcast(128)), "ld_bc", w=["wsub"])
    kb.op("dve", I("tensor_scalar", out=wsub[:], in0=wsub[:], scalar1=1.0 - LAM_INIT, scalar2=None, op0=ALU.mult),
          r=["wsub"], w=["wsub"])

    for l in range(2):
        for g in range(12):
            sl = (l * 12 + g) % 2
            kb.dma("pool", I("dma_start",
                out=wada_sl[sl][:], in_=wada_d[l, :, g * 512:(g + 1) * 512].rearrange("(c p) n -> p c n", p=128)),
                f"ld_wada{sl}", w=[f"wada{sl}"])
            pb = P[2 + sl]
            for blk in range(4):
                for c in range(NCH):
                    kb.op("pe", I("matmul",
                        pb[:, blk * 5:(blk + 1) * 5], lhsT=wada_sl[sl][:, c, blk * 128:(blk + 1) * 128], rhs=scT[:, c, :],
                        start=(c == 0), stop=(c == NCH - 1)), r=[f"wada{sl}", "scT"], w=[f"P{2 + sl}"])
            for blk in range(4):
                bi = l * 48 + g * 4 + blk
                kb.op("dve", I("tensor_scalar",
                    out=modT[:, bi * 5:(bi + 1) * 5], in0=pb[:, blk * 5:(blk + 1) * 5], scalar1=bT[:, bi:bi + 1], scalar2=None,
                    op0=ALU.add), r=[f"P{2 + sl}", "bT"], w=["modT"])
    for l in range(2):
        for i in range(2):
            for c in range(NCH):
                o = ((l * 2 + i) * 8 + c)
                kb.op("dve", I("tensor_scalar",
                    out=wmT[:, o * 5:(o + 1) * 5], in0=mod_ap(l, 1 + 3 * i, c, 0, 5), scalar1=1.0, scalar2=nwT[:, o:o + 1],
                    op0=ALU.add, op1=ALU.mult), r=["modT", "nwT"], w=["wmT"])

    xld = [A(f"xld{i}", [128, D], F32) for i in range(2)]
    n_xt = S // 128
    for it in range(n_xt + 1):
        sl = it % 2
        rows = 128 if it < n_xt else 16
        src = xp_d[it * 128:(it + 1) * 128, :] if it < n_xt else xs_d[:]
        kb.dma("sp", I("dma_start", out=xld[sl][0:rows, :], in_=src),
               f"ld_x{sl}", w=[f"xld{sl}"])
        for half in range(2):
            pb = P[4 + 2 * sl + half]
            pk = f"P{4 + 2 * sl + half}"
            for cc in range(4):
                c = half * 4 + cc
                kb.op("pe", I("transpose",
                    pb[:, cc * 128:cc * 128 + rows], xld[sl][0:rows, c * 128:(c + 1) * 128], cf[0:rows, 0:rows]),
                    r=[f"xld{sl}", "cf"], w=[pk])
            t0 = it * 128
            eng = "act" if half == 0 else "dve"
            if eng == "act":
                kb.op("act", I("copy",
                    xT[:, half * 4:(half + 1) * 4, t0:t0 + rows],
                    pb[:].rearrange("p (c t) -> p c t", c=4)[:, :, 0:rows]), r=[pk], w=["xT"])
            else:
                kb.op("dve", I("tensor_copy",
                    xT[:, half * 4:(half + 1) * 4, t0:t0 + rows],
                    pb[:].rearrange("p (c t) -> p c t", c=4)[:, :, 0:rows]), r=[pk], w=["xT"])
    kb.barrier()
    ar.reset(setup_m)

    def prenorm(l, i):
        jsh = 3 * i
        for ti, (t0, n) in enumerate(cfg.tiles):
            pk = f"P{ti % 2}"
            pb = P[ti % 2]
            for c in range(NCH):
                sq = sqb[c % 2]
                kb.op("act", I("activation", out=sq[:, 0:n], in_=xT[:, c, t0:t0 + n], func=AF.Square),
                      r=["xT"], w=[f"sqb{c % 2}"])
                kb.op("pe", I("matmul", pb[:, 0:n], lhsT=ones_f[:], rhs=sq[:, 0:n],
                                                                      start=(c == 0), stop=(c == NCH - 1)),
                      r=[f"sqb{c % 2}", "ones_f"], w=[pk])
            kb.op("act", I("activation", out=rstd[:, 0:n], in_=pb[:, 0:n], func=AF.Sqrt, scale=1.0 / D, bias=EPS),
                  r=[pk], w=["rstd"])
            kb.op("dve", I("reciprocal", rstd[:, 0:n], rstd[:, 0:n]), r=["rstd"], w=["rstd"])
            for c in range(NCH):
                tn = tnrm[c % 2]
                kb.op("dve", I("tensor_tensor", out=tn[:, 0:n], in0=xT[:, c, t0:t0 + n], in1=rstd[:, 0:n],
                                                                              op=ALU.mult), r=["xT", "rstd"], w=[f"tnrm{c % 2}"])
                for (s0, sn, b) in bsegs(t0, n):
                    kb.op("act", I("activation",
                        out=hT[:, c, s0:s0 + sn], in_=tn[:, s0 - t0:s0 - t0 + sn], func=AF.Identity,
                        scale=wm_ap(l, i, c, b), bias=mod_ap(l, jsh, c, b)),
                        r=[f"tnrm{c % 2}", "wmT", "modT"], w=["hT"])

    def resid_add(pb, pk, cc, t0, n, l, jg):
        for (s0, sn, b) in bsegs(t0, n):
            kb.op("dve", I("scalar_tensor_tensor",
                out=xT[:, cc, s0:s0 + sn], in0=pb[:, s0 - t0:s0 - t0 + sn], scalar=mod_ap(l, jg, cc, b),
                in1=xT[:, cc, s0:s0 + sn], op0=ALU.mult, op1=ALU.add), r=[pk, "modT", "xT"], w=["xT"])

    def mlp(l):
        m = ar.mark()
        wup = [A(f"wup{i}", [128, NCH, 1024], BF16) for i in range(2)]
        wdn = [A(f"wdn{i}", [128, NCH, 1024], BF16) for i in range(2)]
        uT = [A(f"uT{i}", [128, NCH, 512], BF16) for i in range(2)]
        rl = [A(f"rl{i}", [128, 512], BF16) for i in range(2)]
        prenorm(l, 1)
        it = 0
        nup = 0
        ndn = 0
        for g in range(4):
            sl = g % 2
            kb.dma("pool", I("dma_start",
                out=wup[sl][:], in_=wup_d[l, :, g * 1024:(g + 1) * 1024].rearrange("(c p) n -> p c n", p=128)),
                f"ld_wup{sl}", w=[f"wup{sl}"])
            kb.dma("pool", I("dma_start",
                out=wdn[sl][:], in_=wdn_d[l, g * 1024:(g + 1) * 1024, :].rearrange("(f p) n -> p f n", p=128)),
                f"ld_wdn{sl}", w=[f"wdn{sl}"])
            for (t0, n) in cfg.tiles:
                us = it % 2
                it += 1
                for f in range(NCH):
                    pi = nup % 4
                    nup += 1
                    pb, pk = P[pi], f"P{pi}"
                    for c in range(NCH):
                        kb.op("pe", I("matmul",
                            pb[:, 0:n], lhsT=wup[sl][:, c, f * 128:(f + 1) * 128], rhs=hT[:, c, t0:t0 + n],
                            start=(c == 0), stop=(c == NCH - 1)), r=[f"wup{sl}", "hT"], w=[pk])
                    rs = f % 2
                    kb.op("act", I("activation", out=rl[rs][:, 0:n], in_=pb[:, 0:n], func=AF.Relu),
                          r=[pk], w=[f"rl{rs}"])
                    kb.op("dve", I("tensor_tensor",
                        out=uT[us][:, f, 0:n], in0=pb[:, 0:n], in1=rl[rs][:, 0:n], op=ALU.mult),
                        r=[pk, f"rl{rs}"], w=[f"uT{us}"])
                for cc in range(NCH):
                    pi = 4 + ndn % 4
                    ndn += 1
                    pb, pk = P[pi], f"P{pi}"
                    for f in range(NCH):
                        kb.op("pe", I("matmul",
                            pb[:, 0:n], lhsT=wdn[sl][:, f, cc * 128:(cc + 1) * 128], rhs=uT[us][:, f, 0:n],
                            start=(f == 0), stop=(f == NCH - 1)), r=[f"wdn{sl}", f"uT{us}"], w=[pk])
                    resid_add(pb, pk, cc, t0, n, l, 5)
        kb.barrier()
        ar.reset(m)

    def final_out():
        m = ar.mark()
        ybuf = [A(f"ybuf{i}", [128, D], F32) for i in range(2)]
        finw = A("finw", [128, D], F32)
        kb.dma("sp", I("dma_start", out=finw[:], in_=fnw_d[:].partition_broadcast(128)), "ld_bc2", w=["finw"])
        ssq = A("ssq", [128, 8], F32)
        junk = A("junk", [128, 512], F32)
        n_t = S // 128
        for it in range(n_t + 1):
            sl = it % 2
            rows = 128 if it < n_t else 16
            t0 = it * 128
            for half in range(2):
                pi = 2 * sl + half
                pb, pk = P[pi], f"P{pi}"
                for cc in range(4):
                    c = half * 4 + cc
                    kb.op("pe", I("transpose",
                        pb[0:rows, cc * 128:(cc + 1) * 128], xT[:, c, t0:t0 + rows], ident), r=["xT", "cf"], w=[pk])
                kb.op("act", I("activation",
                    out=junk[0:rows, :], in_=pb[0:rows, :], func=AF.Square, accum_out=ssq[0:rows, 2 * sl + half:2 * sl + half + 1]),
                    r=[pk], w=["junk", f"ssq{sl}{half}"])
            kb.op("dve", I("tensor_tensor",
                out=ssq[0:rows, 4 + sl:5 + sl], in0=ssq[0:rows, 2 * sl:2 * sl + 1], in1=ssq[0:rows, 2 * sl + 1:2 * sl + 2], op=ALU.add),
                r=[f"ssq{sl}0", f"ssq{sl}1"], w=[f"ssqs{sl}"])
            kb.op("act", I("activation",
                out=ssq[0:rows, 6 + sl:7 + sl], in_=ssq[0:rows, 4 + sl:5 + sl], func=AF.Sqrt, scale=1.0 / D, bias=EPS),
                r=[f"ssqs{sl}"], w=[f"ssqr{sl}"])
            kb.op("dve", I("reciprocal", ssq[0:rows, 6 + sl:7 + sl], ssq[0:rows, 6 + sl:7 + sl]),
                  r=[f"ssqr{sl}"], w=[f"ssqr{sl}"])
            for half in range(2):
                pi = 2 * sl + half
                pb, pk = P[pi], f"P{pi}"
                kb.op("dve", I("scalar_tensor_tensor",
                    out=ybuf[sl][0:rows, half * 512:(half + 1) * 512], in0=pb[0:rows, :], scalar=ssq[0:rows, 6 + sl:7 + sl],
                    in1=finw[0:rows, half * 512:(half + 1) * 512], op0=ALU.mult, op1=ALU.mult),
                    r=[pk, f"ssqr{sl}", "finw"], w=[f"ybuf{sl}"])
            dst = yp_d[t0:t0 + 128, :] if it < n_t else ys_d[:]
            key = f"y{it}"
            kb.dma("sp", I("dma_start", out=dst, in_=ybuf[sl][0:rows, :]),
                   f"st_y{sl}", r=[f"ybuf{sl}"], w=[key])
            out_keys.append(key)
        ar.reset(m)

    from_layers(cfg, nc, kb, ar, A, P, Pb, locals())
    return nc


def from_layers(cfg, nc, kb, ar, A, P, Pb, env):
    stages = getattr(cfg, "stages", ("mix0", "mlp0", "mix1", "mlp1"))
    if "mix0" in stages:
        layer0_mixer(cfg, nc, kb, ar, A, P, Pb, env)
    if "mlp0" in stages:
        env["mlp"](0)
    if "mix1" in stages:
        layer1_mixer(cfg, nc, kb, ar, A, P, Pb, env)
    if "mlp1" in stages:
        env["mlp"](1)
    env["final_out"]()
    waits = kb._deps("sp", tuple(env["out_keys"]), ())
    kb.q["sp"].append((waits, None, None, 0))
    kb.emit()


def prep_shared(cfg, inp):
    f = lambda a: np.ascontiguousarray(np.asarray(a))
    w4 = np.asarray(inp["w_in_ab"])[0].reshape(D, 8, 4, 128)
    units = []
    for u in range(4):
        units.append(np.concatenate([w4[:, 0, u], w4[:, 1, u], w4[:, 3, u], w4[:, 2, u]], axis=1))
    for u in range(4):
        units.append(np.concatenate([w4[:, 4, u], w4[:, 5, u], w4[:, 7, u], w4[:, 6, u]], axis=1))
    wc = np.asarray(inp["w_in_c"])[0]
    heads = []
    for h in range(4):
        heads.append(np.concatenate([wc[:, h * 256:(h + 1) * 256], wc[:, 1024 + h * 256:1024 + (h + 1) * 256],
                                     wc[:, 2048 + h * 256:2048 + (h + 1) * 256]], axis=1))
    cf, cfb, rope = make_consts(cfg)
    return {
        "ck": f(np.asarray(inp["cache_k"])[0].reshape(cfg.NPHYS, 128, 1024)),
        "cv": f(np.asarray(inp["cache_v"])[0].reshape(cfg.NPHYS, 128, 1024)),
        "w_ada": f(inp["w_ada"]), "b_ada": f(np.asarray(inp["b_ada"]).reshape(96, 128)),
        "norm_w": f(np.asarray(inp["norm_w"]).reshape(32, 128)),
        "w_in_ab": f(np.stack(units)), "w_out_ab": f(np.asarray(inp["w_out_ab"])[0]),
        "lb_logits": f(np.asarray(inp["hgrn_lb_logits"]).reshape(8, 128)),
        "hgrn_norm_w": f(np.asarray(inp["hgrn_norm_w"]).reshape(1, 128)),
        "w_in_c": f(np.stack(heads)), "w_out_c": f(np.asarray(inp["w_out_c"])[0]),
        "diff_lambda": f(np.asarray(inp["diff_lambda"]).reshape(1, 512)),
        "subln_w": f(np.asarray(inp["diff_subln_w"]).reshape(1, 256)),
        "w_up": f(inp["w_mlp_up"]), "w_down": f(inp["w_mlp_down"]),
        "final_norm_w": f(np.asarray(inp["final_norm_w"]).reshape(1, D)),
        "cf": cf, "cfb": cfb, "rope": rope,
    }


def prep_core(cfg, inp, shared, c):
    f = lambda a: np.ascontiguousarray(np.asarray(a))
    m = dict(shared)
    m["xp"] = f(np.asarray(inp["x_prompt"])[c])
    m["xs"] = f(np.asarray(inp["x_sample"])[4 * c:4 * c + 4].reshape(16, D))
    m["st_h"] = f(np.asarray(inp["state_hgrn"])[0, 4 * c:4 * c + 4])
    m["st_r"] = f(np.asarray(inp["state_ret"])[0, 4 * c:4 * c + 4])
    m["pt"] = f(np.asarray(inp["page_table"])[4 * c:4 * c + 4].reshape(1, 4 * cfg.NPG).astype(np.int32))
    m["cvec"] = f(np.concatenate([np.asarray(inp["c_prompt"])[c:c + 1], np.asarray(inp["c_sample"])[4 * c:4 * c + 4]], axis=0))
    return m


def assemble(cfg, res):
    n = cfg.n_cores
    S = cfg.S
    g = lambda k: [np.asarray(r[k]) for r in res]
    y_p = np.stack(g("y_p"))
    y_s = np.concatenate([a.reshape(4, 4, D) for a in g("y_s")], axis=0)
    hg_p = np.stack(g("hg_p"))[None]
    rt_p = np.stack(g("rt_p"))[None]
    k_p = np.stack([a.reshape(S // 128, 128, 4, 2, 128) for a in g("k_p")])[None]
    v_p = np.stack([a.reshape(S // 128, 128, 4, 256) for a in g("v_p")])[None]
    hg_s = np.concatenate(g("hg_s"), axis=0)[None]
    rt_s = np.concatenate(g("rt_s"), axis=0)[None]
    k_s = np.concatenate([a.reshape(4, 4, 4, 2, 128) for a in g("k_s")], axis=0)[None]
    v_s = np.concatenate([a.reshape(4, 4, 4, 256) for a in g("v_s")], axis=0)[None]
    return tuple(np.ascontiguousarray(a.astype(np.float32)) for a in (y_p, y_s, hg_p, rt_p, k_p, v_p, hg_s, rt_s, k_s, v_s))


_NC_CACHE = {}


def kernel(**inputs):
    cfg = Cfg()
    if "nc" not in _NC_CACHE:
        _NC_CACHE["nc"] = build_program(cfg)
    nc = _NC_CACHE["nc"]
    shared = prep_shared(cfg, inputs)
    in_maps = [prep_core(cfg, inputs, shared, c) for c in range(cfg.n_cores)]
    res = run_bass_kernel_spmd(nc, in_maps, core_ids=list(range(cfg.n_cores)))
    return assemble(cfg, res.results)


def layer0_mixer(cfg, nc, kb, ar, A, P, Pb, env):
    S, NT = cfg.S, cfg.NT
    xT, hT, cf, identb, ones_f, zeros_f, lbv, na = (env[k] for k in
                                                    ("xT", "hT", "cf", "identb", "ones_f", "zeros_f", "lbv", "na"))
    sqb, rstd, tnrm = env["sqb"], env["rstd"], env["tnrm"]
    maskf = env["maskf"]
    d = env
    winab_d, woutab_d, rope_d = d["winab_d"], d["woutab_d"], d["rope_d"]
    sth_d, str_d, hgp_d, rtp_d, hgs_d, rts_d = d["sth_d"], d["str_d"], d["hgp_d"], d["rtp_d"], d["hgs_d"], d["rts_d"]
    out_keys = d["out_keys"]
    m0 = ar.mark()
    oT = A("oT", [128, NCH, NT], BF16)
    cfb = A("cfb", [128, NCFB], F32)
    kb.dma("sp", I("dma_start", out=cfb[:], in_=d["cfb_d"][:]), "ld_cfb", w=["cfb"])
    m_w = ar.mark()
    wu = [A(f"wu{i}", [128, NCH, 512], BF16) for i in range(2)]
    wR = A("wR", [128, NCH, 256], BF16)
    f32t = {k: A(k, [128, 512], F32) for k in ("qs", "sg", "om", "bb", "rb")}
    qTt = A("qTt", [128, 512], BF16); kTt = A("kTt", [128, 512], BF16)
    gs = A("gs", [128, 512], BF16); vTt = A("vTt", [128, 512], BF16)
    cosT = A("cosT", [128, 512], F32); sinT = A("sinT", [128, 512], F32)
    Am = A("Am", [128, 128], BF16); kvtok = A("kvtok", [128, 256], BF16); qd = A("qd", [128, 128], BF16)
    U = A("U", [128, 128], F32); Sbf = A("Sbf", [128, 128], BF16); Sfin = A("Sfin", [128, 128], F32)
    belast = A("belast", [128, 1], F32)
    qs, sg, om, bb, rb = (f32t[k] for k in ("qs", "sg", "om", "bb", "rb"))

    env["prenorm"](0, 0)

    def proj(ws_ap_fn, t0, n, pi, rkeys):
        pb, pk = P[pi], f"P{pi}"
        for c in range(NCH):
            kb.op("pe", I("matmul", pb[:, 0:n], lhsT=ws_ap_fn(c), rhs=hT[:, c, t0:t0 + n],
                                                start=(c == 0), stop=(c == NCH - 1)), r=list(rkeys) + ["hT"], w=[pk])
        return pb, pk

    for u in range(8):
        ret = u >= 4
        h = u % 4
        sl = u % 2
        wk = f"wu{sl}"
        kb.dma("pool", I("dma_start", out=wu[sl][:], in_=winab_d[u].rearrange("(c p) n -> p c n", p=128)),
               f"ld_wu{sl}", w=[wk])
        if ret:
            wuv = wu[sl][:, :, 0:256].rearrange("p c (i two) -> p c i two", two=2)
            wRv = wR[:].rearrange("p c (i two) -> p c i two", two=2)
            kb.op("pool", I("tensor_scalar", out=wRv[:, :, :, 0], in0=wuv[:, :, :, 1], scalar1=-1.0, scalar2=None,
                                                                     op0=ALU.mult), r=[wk], w=["wR"])
            kb.op("pool", I("tensor_copy", wRv[:, :, :, 1], wuv[:, :, :, 0]), r=[wk], w=["wR"])
            g_h = 1.0 - 2.0 ** (-5.0 - h)
        st_in = str_d if ret else sth_d
        st_out_p = rtp_d if ret else hgp_d
        st_out_s = rts_d if ret else hgs_d
        for (t0, n) in cfg.tiles:
            is_s = t0 >= S
            W = wu[sl]
            if not ret:
                pb, pk = proj(lambda c: W[:, c, 0:128], t0, n, 0, [wk])
                kb.op("act", I("activation", out=qs[:, 0:n], in_=pb[:, 0:n], func=AF.Silu), r=[pk], w=["qs"])
                pb, pk = proj(lambda c: W[:, c, 128:256], t0, n, 1, [wk])
                kb.op("act", I("activation", out=sg[:, 0:n], in_=pb[:, 0:n], func=AF.Sigmoid), r=[pk], w=["sg"])
                pb, pk = proj(lambda c: W[:, c, 256:384], t0, n, 2, [wk])
                kb.op("act", I("activation", out=gs[:, 0:n], in_=pb[:, 0:n], func=AF.Silu), r=[pk], w=["gs"])
                pb, pk = proj(lambda c: W[:, c, 384:512], t0, n, 0, [wk])
                kb.op("act", I("copy", vTt[:, 0:n], pb[:, 0:n]), r=[pk], w=["vTt"])
                kb.op("dve", I("tensor_scalar", out=om[:, 0:n], in0=sg[:, 0:n], scalar1=lbv[:, 8 + h:9 + h], scalar2=lbv[:, 4 + h:5 + h],
                                                       op0=ALU.mult, op1=ALU.add), r=["sg", "lbv"], w=["om"])
                kb.op("dve", I("tensor_scalar", out=sg[:, 0:n], in0=sg[:, 0:n], scalar1=lbv[:, 4 + h:5 + h], scalar2=lbv[:, h:h + 1],
                                                       op0=ALU.mult, op1=ALU.add), r=["sg", "lbv"], w=["sg"])
                C = 4 if is_s else 64
                for c0 in range(0, n, C):
                    kb.op("dve", I("tensor_tensor_scan",
                        out=bb[:, c0:c0 + C], data0=sg[:, c0:c0 + C], data1=zeros_f[:, 0:C], initial=1.0, op0=ALU.mult, op1=ALU.add),
                        r=["sg", "zeros_f"], w=["bb"])
                kb.op("dve", I("reciprocal", rb[:, 0:n], bb[:, 0:n]), r=["bb"], w=["rb"])
                kb.op("dve", I("scalar_tensor_tensor", out=qTt[:, 0:n], in0=qs[:, 0:n], scalar=DK_SCALE, in1=bb[:, 0:n],
                                                              op0=ALU.mult, op1=ALU.mult), r=["qs", "bb"], w=["qTt"])
                kb.op("dve", I("tensor_tensor", out=kTt[:, 0:n], in0=om[:, 0:n], in1=rb[:, 0:n], op=ALU.mult),
                      r=["om", "rb"], w=["kTt"])
            else:
                kb.dma("sp", I("dma_start", out=cosT[:, 0:n], in_=rope_d[0, :, t0:t0 + n]), "ld_cos", w=["cosT"])
                kb.dma("sp", I("dma_start", out=sinT[:, 0:n], in_=rope_d[1, :, t0:t0 + n]), "ld_sin", w=["sinT"])
                pa, pka = proj(lambda c: W[:, c, 0:128], t0, n, 0, [wk])
                pr, pkr = proj(lambda c: wR[:, c, 0:128], t0, n, 1, ["wR"])
                kb.op("dve", I("tensor_tensor", out=qs[:, 0:n], in0=pa[:, 0:n], in1=cosT[:, 0:n], op=ALU.mult),
                      r=[pka, "cosT"], w=["qs"])
                kb.op("dve", I("tensor_tensor", out=sg[:, 0:n], in0=pr[:, 0:n], in1=sinT[:, 0:n], op=ALU.mult),
                      r=[pkr, "sinT"], w=["sg"])
                kb.op("pool", I("tensor_tensor", out=qTt[:, 0:n], in0=qs[:, 0:n], in1=sg[:, 0:n], op=ALU.add),
                      r=["qs", "sg"], w=["qTt"])
                pa, pka = proj(lambda c: W[:, c, 128:256], t0, n, 2, [wk])
                pr, pkr = proj(lambda c: wR[:, c, 128:256], t0, n, 0, ["wR"])
                kb.op("dve", I("scalar_tensor_tensor", out=om[:, 0:n], in0=pa[:, 0:n], scalar=DK_SCALE, in1=cosT[:, 0:n],
                                                                     op0=ALU.mult, op1=ALU.mult), r=[pka, "cosT"], w=["om"])
                kb.op("dve", I("scalar_tensor_tensor", out=rb[:, 0:n], in0=pr[:, 0:n], scalar=DK_SCALE, in1=sinT[:, 0:n],
                                                                     op0=ALU.mult, op1=ALU.mult), r=[pkr, "sinT"], w=["rb"])
                kb.op("pool", I("tensor_tensor", out=kTt[:, 0:n], in0=om[:, 0:n], in1=rb[:, 0:n], op=ALU.add),
                      r=["om", "rb"], w=["kTt"])
                pb, pk = proj(lambda c: W[:, c, 256:384], t0, n, 1, [wk])
                kb.op("act", I("activation", out=gs[:, 0:n], in_=pb[:, 0:n], func=AF.Silu), r=[pk], w=["gs"])
                pb, pk = proj(lambda c: W[:, c, 384:512], t0, n, 2, [wk])
                kb.op("act", I("copy", vTt[:, 0:n], pb[:, 0:n]), r=[pk], w=["vTt"])
                C = 4 if is_s else 128

            nchunks = n // C
            for ci in range(nchunks):
                c0 = ci * C
                seq = ci if is_s else None
                state_zero = (not is_s) and t0 == 0 and ci == 0
                if is_s:
                    kb.dma("sp", I("dma_start", out=U[:], in_=st_in[seq, h]), "ld_U", w=["U"])
                    kb.op("dve", I("tensor_copy", Sbf[:], U[:]), r=["U"], w=["Sbf"])
                kb.op("pe", I("matmul", P[4][0:C, 0:C], lhsT=kTt[:, c0:c0 + C], rhs=qTt[:, c0:c0 + C], start=True, stop=True),
                      r=["kTt", "qTt"], w=["P4"])
                kb.op("pe", I("transpose", Pb[6][0:C, 0:128], kTt[:, c0:c0 + C], identb[:]), r=["kTt", "identb"], w=["P6"])
                kb.op("pe", I("transpose", Pb[6][0:C, 128:256], vTt[:, c0:c0 + C], identb[:]), r=["vTt", "identb"], w=["P6"])
                if not ret:
                    mk_ap = maskf[0:C, 0:C]
                else:
                    mk_ap = cfb[0:C, CF_DT + h * 128:CF_DT + h * 128 + C]
                kb.op("dve", I("tensor_tensor", out=Am[0:C, 0:C], in0=P[4][0:C, 0:C], in1=mk_ap, op=ALU.mult),
                      r=["P4", "cf", "cfb"], w=["Am"])
                if not ret:
                    kb.op("act", I("copy", kvtok[0:C, :], Pb[6][0:C, 0:256]), r=["P6"], w=["kvtok"])
                else:
                    kd = cfb[0:C, (CF_KD4 if is_s else CF_KD128) + h:(CF_KD4 if is_s else CF_KD128) + h + 1]
                    kb.op("act", I("activation", out=kvtok[0:C, 0:128], in_=Pb[6][0:C, 0:128], func=AF.Identity, scale=kd),
                          r=["P6", "cfb"], w=["kvtok"])
                    kb.op("act", I("copy", kvtok[0:C, 128:256], Pb[6][0:C, 128:256]), r=["P6"], w=["kvtok"])
                    if not state_zero:
                        kb.op("pool", I("tensor_tensor",
                            out=qd[:, 0:C], in0=qTt[:, c0:c0 + C], in1=cfb[:, CF_G1 + h * 128:CF_G1 + h * 128 + C], op=ALU.mult),
                            r=["qTt", "cfb"], w=["qd"])
                kb.op("pe", I("matmul", P[7][:, c0:c0 + C], lhsT=kvtok[0:C, 128:256], rhs=Am[0:C, 0:C],
                                                                        start=True, stop=state_zero), r=["kvtok", "Am"], w=["P7"])
                if not state_zero:
                    q_in = qd[:, 0:C] if ret else qTt[:, c0:c0 + C]
                    kb.op("pe", I("matmul", P[7][:, c0:c0 + C], lhsT=Sbf[:], rhs=q_in, start=False, stop=True),
                          r=["Sbf", "qd", "qTt"], w=["P7"])
                kb.op("pe", I("matmul", P[5][:, 0:128], lhsT=kvtok[0:C, 0:128], rhs=kvtok[0:C, 128:256], start=True, stop=True),
                      r=["kvtok"], w=["P5"])
                if state_zero:
                    kb.op("dve", I("tensor_copy", U[:], P[5][:, 0:128]), r=["P5"], w=["U"])
                else:
                    if ret:
                        sc_prev = g_h ** C
                    elif is_s:
                        sc_prev = 1.0
                    elif ci == 0:
                        sc_prev = belast[:, 0:1]
                    else:
                        sc_prev = bb[:, c0 - 1:c0]
                    kb.op("dve", I("scalar_tensor_tensor", out=U[:], in0=U[:], scalar=sc_prev, in1=P[5][:, 0:128],
                                                                                 op0=ALU.mult, op1=ALU.add),
                          r=["U", "P5", "bb", "belast"], w=["U"])
                last_of_seq = is_s or (t0 + n == S and ci == nchunks - 1)
                be_cur = bb[:, c0 + C - 1:c0 + C]
                if not last_of_seq:
                    if ret:
                        kb.op("act", I("copy", Sbf[:], U[:]), r=["U"], w=["Sbf"])
                    else:
                        kb.op("dve", I("tensor_scalar", out=Sbf[:], in0=U[:], scalar1=be_cur, scalar2=None, op0=ALU.mult),
                              r=["U", "bb"], w=["Sbf"])
                        if ci == nchunks - 1:
                            kb.op("dve", I("tensor_copy", belast[:], be_cur), r=["bb"], w=["belast"])
                else:
                    dst = st_out_s[seq, h] if is_s else st_out_p[h]
                    key = f"st_{u}_{seq}"
                    if ret:
                        kb.op("act", I("copy", Sfin[:], U[:]), r=["U"], w=["Sfin"])
                    else:
                        kb.op("dve", I("tensor_scalar", out=Sfin[:], in0=U[:], scalar1=be_cur, scalar2=None, op0=ALU.mult),
                              r=["U", "bb"], w=["Sfin"])
                    kb.dma("sp", I("dma_start", out=dst, in_=Sfin[:]), "st_S", r=["Sfin"], w=[key])
                    out_keys.append(key)

            kb.op("act", I("activation", out=sqb[0][:, 0:n], in_=P[7][:, 0:n], func=AF.Square), r=["P7"], w=["sqb0"])
            kb.op("pe", I("matmul", P[3][:, 0:n], lhsT=ones_f[:], rhs=sqb[0][:, 0:n], start=True, stop=True),
                  r=["sqb0", "ones_f"], w=["P3"])
            kb.op("act", I("activation", out=rstd[:, 0:n], in_=P[3][:, 0:n], func=AF.Sqrt, scale=1.0 / 128, bias=EPS),
                  r=["P3"], w=["rstd"])
            kb.op("dve", I("reciprocal", rstd[:, 0:n], rstd[:, 0:n]), r=["rstd"], w=["rstd"])
            kb.op("dve", I("tensor_tensor", out=tnrm[0][:, 0:n], in0=P[7][:, 0:n], in1=rstd[:, 0:n], op=ALU.mult),
                  r=["P7", "rstd"], w=["tnrm0"])
            nsc = 1.0 if ret else na[:, 0:1]
            kb.op("dve", I("scalar_tensor_tensor", out=oT[:, u, t0:t0 + n], in0=tnrm[0][:, 0:n], scalar=nsc, in1=gs[:, 0:n],
                                                                  op0=ALU.mult, op1=ALU.mult), r=["tnrm0", "gs", "na"], w=["oT"])

    kb.barrier()
    m_end = ar.mark()
    ar.reset(m_w)
    wo = A("wo", [128, NCH, 1024], BF16)
    kb.dma("pool", I("dma_start", out=wo[:], in_=woutab_d[:].rearrange("(c p) n -> p c n", p=128)), "ld_wo", w=["wo"])
    npj = 0
    for (t0, n) in cfg.tiles:
        for cc in range(NCH):
            pi = npj % 4
            npj += 1
            pb, pk = P[pi], f"P{pi}"
            for c in range(NCH):
                kb.op("pe", I("matmul", pb[:, 0:n], lhsT=wo[:, c, cc * 128:(cc + 1) * 128], rhs=oT[:, c, t0:t0 + n],
                                                                  start=(c == 0), stop=(c == NCH - 1)), r=["wo", "oT"], w=[pk])
            env["resid_add"](pb, pk, cc, t0, n, 0, 2)
    kb.barrier()
    ar.reset(m0)


def layer1_mixer(cfg, nc, kb, ar, A, P, Pb, env):
    S, NT, NPG = cfg.S, cfg.NT, cfg.NPG
    xT, hT, cf, identb, maskb, neglam, sgn8, wsub = (env[k] for k in ("xT", "hT", "cf", "identb", "maskb", "neglam", "sgn8", "wsub"))
    sqb, rstd, tnrm = env["sqb"], env["rstd"], env["tnrm"]
    d = env
    winc_d, woutc_d, ck_d, cv_d, pt_d = d["winc_d"], d["woutc_d"], d["ck_d"], d["cv_d"], d["pt_d"]
    kp_d, vp_d, ks_d, vs_d = d["kp_d"], d["vp_d"], d["ks_d"], d["vs_d"]
    out_keys = d["out_keys"]
    NVT = S // 128
    NQT = S // 512
    VW = 264

    env["prenorm"](1, 0)
    kb.barrier()
    m0 = ar.mark()
    oTs = A("oTs", [128, NCH, 16], BF16)
    QTs = A("QTs", [128, 8, 16], BF16)
    KTs = A("KTs", [128, 8, 16], BF16)
    Vs = A("Vs", [4, 16, VW], BF16)
    small = A("small", [128, 8], F32)
    ssm = A("ssm", [128, 8], F32)
    oTh = A("oTh", [128, 2, S], BF16)
    wqkv = A("wqkv", [128, NCH, 768], BF16)
    woh = A("woh", [128, 2, 1024], BF16)
    QT = A("QT", [128, 2, S], BF16)
    KT = A("KT", [128, 2, S], BF16)
    Vtok = A("Vtok", [128, NVT, VW], BF16)
    stg = [A(f"stg{i}", [128, 512], F32) for i in range(2)]
    PT = [A(f"PT{i}", [128, 512], BF16) for i in range(2)]
    n1b = A("n1b", [128, 2, 256], F32)
    og = A("og", [128, 256], BF16)
    n1a = rstd[:].rearrange("p (a b) -> p a b", a=2)
    n1 = lambda qs: (n1a if qs < 2 else n1b)[:, qs % 2, :]
    dd = tnrm[0][:, 0:256]
    junk = tnrm[0][:, 256:512]
    NSL = 3
    Kpg = [A(f"Kpg{i}", [128, 1024], BF16) for i in range(NSL)]
    Vpg = [A(f"Vpg{i}", [128, 1024], BF16) for i in range(NSL)]
    KTpg = [A(f"KTpg{i}", [128, 8, 128], BF16) for i in range(2)]
    PTs = [A(f"PTs{i}", [128, 32], BF16) for i in range(2)]
    PTn = A("PTn", [4, 32], BF16)
    ogs = A("ogs", [16, 256], BF16)
    ones_b = A("ones_b", [128, 8], BF16)
    zeros_b = A("zeros_b", [128, 128], BF16)
    ptb = A("ptb", [128, 4 * NPG], I32)
    pid = A("pid", [128, 1], I32)
    pidx = A("pidx", [128, 4 * NPG], I32)
    acc = sqb[0][:, 0:260]
    nsg = sqb[1][:, 0:256]
    junk2 = tnrm[1][0:16, 0:256]
    dsm = tnrm[1][0:16, 256:512]

    kb.op("pool", I("memset", Vtok[:, :, 256:257], 1.0), w=["Vtok"])
    kb.op("pool", I("memset", Vs[:, :, 256:257], 1.0), w=["Vs"])
    kb.op("dve", I("memset", ones_b[:], 1.0), w=["ones_b"])
    kb.op("dve", I("memset", zeros_b[:], 0.0), w=["zeros_b"])
    kb.dma("sp", I("dma_start", out=ptb[:], in_=pt_d[:].partition_broadcast(128)), "ld_pts", w=["ptb"])
    kb.op("pool", I("iota", pid[:], pattern=[[0, 1]], base=0, channel_multiplier=1), w=["pid"])
    kb.op("dve", I("tensor_scalar", out=pidx[:], in0=ptb[:], scalar1=128, scalar2=pid[:, 0:1], op0=ALU.mult, op1=ALU.add),
          r=["ptb", "pid"], w=["pidx"])
    ck_rows = ck_d.rearrange("a p f -> (a p) f")
    cv_rows = cv_d.rearrange("a p f -> (a p) f")

    nproj = [0]

    def pbank():
        pi = 2 + nproj[0] % 2
        nproj[0] += 1
        return P[pi], f"P{pi}"

    nstg = [0]

    def load_head_weights(h, with_out):
        kb.dma("pool", I("dma_start", out=wqkv[:], in_=winc_d[h].rearrange("(c p) n -> p c n", p=128)), "ld_wqkv", w=["wqkv"])
        if with_out:
            kb.dma("pool", I("dma_start", out=woh[:], in_=woutc_d[h * 256:(h + 1) * 256, :].rearrange("(c p) n -> p c n", p=128)),
                   "ld_woh", w=["woh"])

    def project(h, t0, n):
        is_s = t0 >= S
        for j in range(2):
            pb, pk = pbank()
            for c in range(NCH):
                kb.op("pe", I("matmul", pb[:, 0:n], lhsT=wqkv[:, c, j * 128:(j + 1) * 128], rhs=hT[:, c, t0:t0 + n],
                              start=(c == 0), stop=(c == NCH - 1)), r=["wqkv", "hT"], w=[pk])
            if is_s:
                kb.op("act", I("activation", out=QTs[:, h * 2 + j, :], in_=pb[:, 0:16], func=AF.Identity, scale=DK_SCALE), r=[pk], w=["QTs"])
            else:
                kb.op("act", I("activation", out=QT[:, j, t0:t0 + n], in_=pb[:, 0:n], func=AF.Identity, scale=DK_SCALE), r=[pk], w=["QT"])
        for j in range(2):
            pb, pk = pbank()
            for c in range(NCH):
                kb.op("pe", I("matmul", pb[:, 0:n], lhsT=wqkv[:, c, 256 + j * 128:256 + (j + 1) * 128], rhs=hT[:, c, t0:t0 + n],
                              start=(c == 0), stop=(c == NCH - 1)), r=["wqkv", "hT"], w=[pk])
            if is_s:
                kb.op("dve", I("tensor_copy", KTs[:, h * 2 + j, :], pb[:, 0:16]), r=[pk], w=["KTs"])
            else:
                kb.op("dve", I("tensor_copy", KT[:, j, t0:t0 + n], pb[:, 0:n]), r=[pk], w=["KT"])
        subs = [(S + 4 * i, 4, i) for i in range(4)] if is_s else [(t0 + s * 128, 128, None) for s in range(n // 128)]
        for (c0, rows, seq) in subs:
            pb, pk = pbank()
            for c in range(NCH):
                kb.op("pe", I("matmul", pb[0:rows, 0:512], lhsT=hT[:, c, c0:c0 + rows], rhs=wqkv[:, c, 256:768],
                              start=(c == 0), stop=(c == NCH - 1)), r=["wqkv", "hT"], w=[pk])
            si = nstg[0] % 2
            nstg[0] += 1
            kb.op("act", I("copy", stg[si][0:rows, :], pb[0:rows, 0:512]), r=[pk], w=[f"stg{si}"])
            if seq is None:
                kdst = kp_d[c0:c0 + rows, h * 256:(h + 1) * 256]
                vdst = vp_d[c0:c0 + rows, h * 256:(h + 1) * 256]
                kb.op("dve", I("tensor_copy", Vtok[:, c0 // 128, 0:256], stg[si][:, 256:512]), r=[f"stg{si}"], w=["Vtok"])
            else:
                r0 = c0 - S
                kdst = ks_d[r0:r0 + rows, h * 256:(h + 1) * 256]
                vdst = vs_d[r0:r0 + rows, h * 256:(h + 1) * 256]
                kb.op("dve", I("tensor_copy", Vs[0:4, seq * 4 + h, 0:256], stg[si][0:4, 256:512]), r=[f"stg{si}"], w=["Vs"])
            key = f"kv_{h}_{c0}"
            kb.dma("sp", I("dma_start", out=kdst, in_=stg[si][0:rows, 0:256]), f"st_k{si}", r=[f"stg{si}"], w=[key + "k"])
            kb.dma("sp", I("dma_start", out=vdst, in_=stg[si][0:rows, 256:512]), f"st_v{si}", r=[f"stg{si}"], w=[key + "v"])
            out_keys.extend([key + "k", key + "v"])

    def sample_gen():
        pages = [(i, p) for i in range(4) for p in range(NPG)]
        NP = len(pages)

        def stageA(n):
            sl, s2 = n % NSL, n % 2
            for hj in range(8):
                kb.op("pe", I("transpose", Pb[0][:, hj * 128:(hj + 1) * 128], Kpg[sl][:, hj * 128:(hj + 1) * 128], identb[:]),
                      r=[f"Kpg{sl}", "identb"], w=["P0"])
            if s2 == 0:
                kb.op("dve", I("tensor_copy", KTpg[s2][:].rearrange("p a b -> p (a b)"), Pb[0][:, 0:1024]), r=["P0"], w=[f"KTpg{s2}"])
            else:
                kb.op("act", I("copy", KTpg[s2][:].rearrange("p a b -> p (a b)"), Pb[0][:, 0:1024]), r=["P0"], w=[f"KTpg{s2}"])

        def stageB(n):
            i, p = pages[n]
            s2 = n % 2
            for hj in range(8):
                kb.op("pe", I("matmul", P[1][:, hj * 4:(hj + 1) * 4], lhsT=KTpg[s2][:, hj, :], rhs=QTs[:, hj, 4 * i:4 * i + 4], start=True, stop=True),
                      r=[f"KTpg{s2}", "QTs"], w=["P1"])
            kb.op("act", I("activation", out=PTs[s2][:], in_=P[1][:, 0:32], func=AF.Exp), r=["P1"], w=[f"PTs{s2}"])

        def stageC(n):
            i, p = pages[n]
            sl, s2 = n % NSL, n % 2
            if p == 0:
                kb.op("pool", I("memset", acc, 0.0), w=["acc"])
            for h in range(4):
                kb.op("pe", I("matmul", P[1][32 * h:32 * h + 8, 64:320], lhsT=PTs[s2][:, h * 8:(h + 1) * 8], rhs=Vpg[sl][:, h * 256:(h + 1) * 256],
                              start=True, stop=False, tile_position=(0, 32 * h)), r=[f"PTs{s2}", f"Vpg{sl}"], w=["P1"])
                kb.op("pe", I("matmul", P[1][32 * h:32 * h + 8, 320:321], lhsT=PTs[s2][:, h * 8:(h + 1) * 8], rhs=ones_b[:, 0:1],
                              start=False, stop=True, tile_position=(0, 32 * h)), r=[f"PTs{s2}", "ones_b"], w=["P1"])
            kb.op("dve", I("tensor_tensor", out=acc[:, 0:257], in0=P[1][:, 64:321], in1=acc[:, 0:257], op=ALU.add), r=["P1", "acc"], w=["acc"])
            if p == NPG - 1:
                finish_seq(i)

        def finish_seq(i):
            for hj in range(8):
                kb.op("pe", I("matmul", P[1][0:4, hj * 4:(hj + 1) * 4], lhsT=KTs[:, hj, 4 * i:4 * i + 4], rhs=QTs[:, hj, 4 * i:4 * i + 4],
                              start=True, stop=True), r=["KTs", "QTs"], w=["P1"])
            kb.op("act", I("activation", out=PTn[:], in_=P[1][0:4, 0:32], func=AF.Exp), r=["P1"], w=["PTn"])
            kb.op("dve", I("tensor_tensor", out=PTn[:].rearrange("p (a b) -> p a b", a=8), in0=PTn[:].rearrange("p (a b) -> p a b", a=8),
                           in1=maskb[0:4, 0:4].unsqueeze(1).to_broadcast([4, 8, 4]), op=ALU.mult), r=["PTn", "maskb"], w=["PTn"])
            for h in range(4):
                kb.op("pe", I("matmul", P[1][32 * h:32 * h + 8, 64:321], lhsT=PTn[0:4, h * 8:(h + 1) * 8], rhs=Vs[0:4, i * 4 + h, 0:257],
                              start=True, stop=True, tile_position=(0, 32 * h)), r=["PTn", "Vs"], w=["P1"])
            kb.op("dve", I("tensor_tensor", out=acc[:, 0:257], in0=P[1][:, 64:321], in1=acc[:, 0:257], op=ALU.add), r=["P1", "acc"], w=["acc"])
            kb.op("dve", I("tensor_scalar", out=ssm[:, 0:1], in0=acc[:, 256:257], scalar1=1e-30, scalar2=None, op0=ALU.add), r=["acc"], w=["ssm0"])
            kb.op("dve", I("reciprocal", ssm[:, 1:2], ssm[:, 0:1]), r=["ssm0"], w=["ssm1"])
            kb.op("dve", I("tensor_tensor", out=ssm[:, 2:3], in0=ssm[:, 1:2], in1=sgn8[:, 0:1], op=ALU.mult), r=["ssm1", "sgn8"], w=["ssm2"])
            kb.op("dve", I("tensor_scalar", out=nsg, in0=acc[:, 0:256], scalar1=ssm[:, 2:3], scalar2=None, op0=ALU.mult), r=["acc", "ssm2"], w=["nsg"])
            kb.op("pe", I("matmul", P[0][0:16, 0:256], lhsT=cf[:, CF_SEL2:CF_SEL2 + 16], rhs=nsg, start=True, stop=True), r=["nsg", "cf"], w=["P0"])
            kb.op("act", I("copy", dsm, P[0][0:16, 0:256]), r=["P0"], w=["dsm"])
            kb.op("act", I("activation", out=junk2, in_=dsm, func=AF.Square, accum_out=ssm[0:16, 3:4]), r=["dsm"], w=["junk2", "ssm3"])
            kb.op("act", I("activation", out=ssm[0:16, 4:5], in_=ssm[0:16, 3:4], func=AF.Sqrt, scale=1.0 / 256, bias=EPS), r=["ssm3"], w=["ssm4"])
            kb.op("dve", I("reciprocal", ssm[0:16, 5:6], ssm[0:16, 4:5]), r=["ssm4"], w=["ssm5"])
            kb.op("dve", I("scalar_tensor_tensor", out=ogs[:], in0=dsm, scalar=ssm[0:16, 5:6], in1=wsub[0:16, :], op0=ALU.mult, op1=ALU.mult),
                  r=["dsm", "ssm5", "wsub"], w=["ogs"])
            for half in range(2):
                kb.op("pe", I("transpose", Pb[0][:, half * 16:(half + 1) * 16], ogs[:, half * 128:(half + 1) * 128], identb[0:16, 0:16]),
                      r=["ogs", "identb"], w=["P0"])
            for half in range(2):
                kb.op("dve", I("tensor_copy", oTs[:].rearrange("p (h two) t -> p two h t", two=2)[:, half, :, 4 * i:4 * i + 4],
                               Pb[0][:, half * 16:(half + 1) * 16].rearrange("p (h q) -> p h q", h=4)), r=["P0"], w=["oTs"])

        kb.op("pe", I("matmul", P[1][:, 64:321], lhsT=zeros_b[:], rhs=Vtok[:, 0, 0:257], start=True, stop=True), r=["zeros_b", "Vtok"], w=["P1"])
        kdma = lambda n: kb.dma("pool", I("indirect_dma_start", out=Kpg[n % NSL][:], out_offset=None, in_=ck_rows,
                                          in_offset=bass.IndirectOffsetOnAxis(ap=pidx[:, pages[n][0] * NPG + pages[n][1]:pages[n][0] * NPG + pages[n][1] + 1], axis=0)),
                                f"ld_kpg{n % NSL}", r=["pidx"], w=[f"Kpg{n % NSL}"])
        vdma = lambda n: kb.dma("pool", I("indirect_dma_start", out=Vpg[n % NSL][:], out_offset=None, in_=cv_rows,
                                          in_offset=bass.IndirectOffsetOnAxis(ap=pidx[:, pages[n][0] * NPG + pages[n][1]:pages[n][0] * NPG + pages[n][1] + 1], axis=0)),
                                f"ld_vpg{n % NSL}", r=["pidx"], w=[f"Vpg{n % NSL}"])
        for n in range(min(2, NP)):
            kdma(n)
        vdma(0)
        for sstep in range(NP + 2):
            if sstep < NP:
                stageA(sstep)
            if 0 <= sstep - 1 < NP:
                stageB(sstep - 1)
            if 0 <= sstep - 2 < NP:
                stageC(sstep - 2)
            if sstep + 2 < NP:
                kdma(sstep + 2)
            if sstep + 1 < NP:
                vdma(sstep + 1)
            yield

    for h in range(4):
        load_head_weights(h, False)
        project(h, S, 16)
    gen = sample_gen()
    PAGES_PER_KB = getattr(cfg, "pages_per_kb", 1)

    for h in range(4):
        load_head_weights(h, True)
        for (t0, n) in cfg.tiles:
            if t0 < S:
                project(h, t0, n)
        nst = [0]
        for qt in range(NQT):
            for j in range(2):
                nk = 4 * qt + 4
                bufs = {}

                def scores(kbi):
                    n0 = max(0, kbi - 4 * qt)
                    sb = 2 + nst[0] % 2
                    ps = nst[0] % 2
                    nst[0] += 1
                    bufs[kbi] = ps
                    q0 = qt * 512 + n0 * 128
                    kb.op("pe", I("matmul", P[sb][:, n0 * 128:512], lhsT=KT[:, j, kbi * 128:(kbi + 1) * 128], rhs=QT[:, j, q0:(qt + 1) * 512],
                                  start=True, stop=True), r=["KT", "QT"], w=[f"P{sb}"])
                    kb.op("act", I("activation", out=PT[ps][:, n0 * 128:512], in_=P[sb][:, n0 * 128:512], func=AF.Exp),
                          r=[f"P{sb}"], w=[f"PT{ps}"])
                    if kbi >= 4 * qt:
                        kb.op("dve", I("tensor_tensor", out=PT[ps][:, n0 * 128:(n0 + 1) * 128], in0=PT[ps][:, n0 * 128:(n0 + 1) * 128],
                                       in1=maskb[:], op=ALU.mult), r=[f"PT{ps}", "maskb"], w=[f"PT{ps}"])

                scores(0)
                for kbi in range(nk):
                    n0 = max(0, kbi - 4 * qt)
                    if kbi + 1 < nk:
                        scores(kbi + 1)
                    ps = bufs[kbi]
                    for qs in range(n0, 4):
                        kb.op("pe", I("matmul", P[4 + qs][:, 0:257], lhsT=PT[ps][:, qs * 128:(qs + 1) * 128], rhs=Vtok[:, kbi, 0:257],
                                      start=(kbi == 0), stop=(kbi == 4 * qt + qs)), r=[f"PT{ps}", "Vtok"], w=[f"P{4 + qs}"])
                    for _ in range(PAGES_PER_KB):
                        next(gen, None)
                for qs in range(4):
                    ob, ok = P[4 + qs], f"P{4 + qs}"
                    tok0 = qt * 512 + qs * 128
                    if j == 0:
                        kb.op("dve", I("reciprocal", small[:, 0:1], ob[:, 256:257]), r=[ok], w=["sm0"])
                        kb.op("act", I("activation", out=n1(qs), in_=ob[:, 0:256], func=AF.Identity, scale=small[:, 0:1]),
                              r=[ok, "sm0"], w=[f"n1{qs}"])
                    else:
                        kb.op("dve", I("reciprocal", small[:, 1:2], ob[:, 256:257]), r=[ok], w=["sm1"])
                        kb.op("dve", I("tensor_tensor", out=small[:, 2:3], in0=small[:, 1:2], in1=neglam[:, 0:1], op=ALU.mult),
                              r=["sm1", "neglam"], w=["sm2"])
                        kb.op("dve", I("scalar_tensor_tensor", out=dd, in0=ob[:, 0:256], scalar=small[:, 2:3], in1=n1(qs),
                                       op0=ALU.mult, op1=ALU.add), r=[ok, "sm2", f"n1{qs}"], w=["dd"])
                        kb.op("act", I("activation", out=junk, in_=dd, func=AF.Square, accum_out=small[:, 3:4]), r=["dd"], w=["junk1", "sm3"])
                        kb.op("act", I("activation", out=small[:, 4:5], in_=small[:, 3:4], func=AF.Sqrt, scale=1.0 / 256, bias=EPS),
                              r=["sm3"], w=["sm4"])
                        kb.op("dve", I("reciprocal", small[:, 5:6], small[:, 4:5]), r=["sm4"], w=["sm5"])
                        kb.op("dve", I("scalar_tensor_tensor", out=og[:], in0=dd, scalar=small[:, 5:6], in1=wsub[:], op0=ALU.mult, op1=ALU.mult),
                              r=["dd", "sm5", "wsub"], w=["og"])
                        pb, pk = pbank()
                        pbb = pb[:].bitcast(BF16)
                        for half in range(2):
                            kb.op("pe", I("transpose", pbb[:, half * 128:(half + 1) * 128], og[:, half * 128:(half + 1) * 128], identb[:]),
                                  r=["og", "identb"], w=[pk])
                        kb.op("act", I("copy", oTh[:, :, tok0:tok0 + 128], pbb[:, 0:256].rearrange("p (c t) -> p c t", c=2)), r=[pk], w=["oTh"])
        for (t0, n) in cfg.tiles:
            if t0 >= S:
                continue
            for cc in range(NCH):
                pb, pk = pbank()
                for c2 in range(2):
                    kb.op("pe", I("matmul", pb[:, 0:n], lhsT=woh[:, c2, cc * 128:(cc + 1) * 128], rhs=oTh[:, c2, t0:t0 + n],
                                  start=(c2 == 0), stop=(c2 == 1)), r=["woh", "oTh"], w=[pk])
                env["resid_add"](pb, pk, cc, t0, n, 1, 2)
                next(gen, None)
    for _ in gen:
        pass

    for h in range(4):
        kb.dma("pool", I("dma_start", out=woh[:], in_=woutc_d[h * 256:(h + 1) * 256, :].rearrange("(c p) n -> p c n", p=128)),
               "ld_woh", w=["woh"])
        for cc in range(NCH):
            pb, pk = pbank()
            for c2 in range(2):
                kb.op("pe", I("matmul", pb[:, 0:16], lhsT=woh[:, c2, cc * 128:(cc + 1) * 128], rhs=oTs[:, 2 * h + c2, :],
                              start=(c2 == 0), stop=(c2 == 1)), r=["woh", "oTs"], w=[pk])
            env["resid_add"](pb, pk, cc, S, 16, 1, 2)
    kb.barrier()
    ar.reset(m0)
```

```python
import math
import numpy as np
import concourse.bass as bass
import concourse.mybir as mybir
from concourse.bass_utils import run_bass_kernel_spmd

F32 = mybir.dt.float32
BF16 = mybir.dt.bfloat16
I32 = mybir.dt.int32
AF = mybir.ActivationFunctionType
ALU = mybir.AluOpType

ENGS = ("pe", "act", "dve", "pool", "sp")
D = 1024
NCH = 8
DFF = 4096
EPS = 1e-6
LAM_INIT = 0.8 - 0.6 * math.exp(-0.3 * 1)
DK_SCALE = 128.0 ** -0.5


class Cfg:
    def __init__(self, S=2048, NPG=64, NPHYS=2560, n_cores=8, debug=False):
        self.S = S
        self.NPG = NPG
        self.NPHYS = NPHYS
        self.n_cores = n_cores
        self.NT = S + 16
        self.POS0 = NPG * 128
        self.debug = debug
        self.tiles = [(t0, 512) for t0 in range(0, S, 512)] + [(S, 16)]


class KB:
    def __init__(self, nc):
        self.nc = nc
        self.q = {e: [] for e in ENGS}
        self.sems = {e: nc.alloc_semaphore("sem_" + e) for e in ENGS}
        self.cnt = {e: 0 for e in ENGS}
        self.seen = {e: {} for e in ENGS}
        self.lastw = {}
        self.readers = {}

    def dsem(self, name):
        if name not in self.sems:
            self.sems[name] = self.nc.alloc_semaphore("dsem_" + name)
            self.cnt[name] = 0
        return name

    def _deps(self, eng, r, w, skip_sem=None):
        deps = {}

        def need(x):
            if x is None:
                return
            sk, v = x
            if v > deps.get(sk, 0):
                deps[sk] = v
        for b in r:
            need(self.lastw.get(b))
        for b in w:
            need(self.lastw.get(b))
            for rd in self.readers.get(b, ()):
                need(rd)
        waits = []
        for sk, v in deps.items():
            if sk == skip_sem:
                continue
            if self.seen[eng].get(sk, 0) >= v:
                continue
            self.seen[eng][sk] = v
            waits.append((sk, v))
        return waits

    def _commit(self, me, r, w):
        for b in w:
            self.lastw[b] = me
            self.readers[b] = []
        for b in r:
            self.readers.setdefault(b, []).append(me)

    @staticmethod
    def _is_psum(k):
        return len(k) == 2 and k[0] == "P" and k[1].isdigit()

    def op(self, eng, fn, r=(), w=()):
        w = tuple(w) + tuple(k for k in r if self._is_psum(k))
        r = tuple(k for k in r if not self._is_psum(k))
        waits = self._deps(eng, r, w, skip_sem="pe" if eng == "pe" else None)
        self.cnt[eng] += 1
        me = (eng, self.cnt[eng])
        self.q[eng].append((waits, fn, eng, 1))
        self._commit(me, r, w)

    def dma(self, eng, fn, sem, r=(), w=(), group=False):
        r = tuple(r); w = tuple(w)
        self.dsem(sem)
        waits = self._deps(eng, r, w, skip_sem=sem if group else None)
        self.cnt[sem] += 16
        me = (sem, self.cnt[sem])
        self.q[eng].append((waits, fn, sem, 16))
        self._commit(me, r, w)

    def barrier(self):
        for e in ENGS:
            waits = []
            for sk, v in self.cnt.items():
                if v > self.seen[e].get(sk, 0):
                    self.seen[e][sk] = v
                    waits.append((sk, v))
            if waits:
                self.q[e].append((waits, None, None, 0))

    def emit(self):
        nc = self.nc
        with nc.Block() as block:
            def mk(ename):
                def body(e):
                    for waits, fn, incsem, incv in self.q[ename]:
                        for sk, v in waits:
                            e.wait_ge(self.sems[sk], v)
                        if fn is not None:
                            fn(e).then_inc(self.sems[incsem], incv)
                return body
            block.tensor(mk("pe"))
            block.scalar(mk("act"))
            block.vector(mk("dve"))
            block.gpsimd(mk("pool"))
            block.sync(mk("sp"))


def I(name, *a, **k):
    return lambda e: getattr(e, name)(*a, **k)


class Arena:
    def __init__(self, nc):
        self.nc = nc
        self.off = (int(nc.sbuf_base) + 63) // 64 * 64
        self.limit = int(nc.sbuf_top)
        self.n = 0

    def alloc(self, name, shape, dtype):
        esz = 2 if dtype == BF16 else 4
        nbytes = int(np.prod(shape[1:])) * esz
        self.off = (self.off + 31) // 32 * 32
        assert self.off + nbytes <= self.limit, f"SBUF overflow allocating {name}: {self.off}+{nbytes}>{self.limit}"
        self.n += 1
        t = self.nc.alloc_sbuf_tensor_at(f"{name}_{self.n}", list(shape), dtype, offset=self.off)
        self.off += nbytes
        return t

    def mark(self):
        return self.off

    def reset(self, m):
        self.off = m


CF_IDENT, CF_MASK, CF_M0, CF_M1, CF_SEL2 = 0, 128, 256, 257, 258
NCF = 274
CF_DT, CF_G1, CF_KD128, CF_KD4 = 0, 512, 1024, 1028
NCFB = 1032


def make_consts(cfg):
    cf = np.zeros((128, NCF), np.float64)
    cfb = np.zeros((128, NCFB), np.float64)
    cf[:, CF_IDENT:CF_IDENT + 128] = np.eye(128)
    s = np.arange(128)[:, None]
    t = np.arange(128)[None, :]
    cf[:, CF_MASK:CF_MASK + 128] = (s <= t)
    for h in range(4):
        g = 1.0 - 2.0 ** (-5.0 - h)
        cfb[:, CF_DT + h * 128:CF_DT + (h + 1) * 128] = np.where(t >= s, g ** np.maximum(t - s, 0), 0.0)
        cfb[:, CF_G1 + h * 128:CF_G1 + (h + 1) * 128] = g ** (t + 1.0)
        cfb[:, CF_KD128 + h] = g ** (127.0 - s[:, 0])
        cfb[:4, CF_KD4 + h] = g ** (3.0 - s[:4, 0])
        cf[32 * h:32 * h + 4, CF_M0] = 1.0
        cf[32 * h + 4:32 * h + 8, CF_M1] = 1.0
        for j in range(2):
            for q in range(4):
                cf[32 * h + j * 4 + q, CF_SEL2 + h * 4 + q] = 1.0
    d = 128
    inv = 1.0 / (10000.0 ** np.linspace(0.0, 1.0, d // 2, dtype=np.float32).astype(np.float64))
    pos = np.concatenate([np.arange(cfg.S), np.tile(cfg.POS0 + np.arange(4), 4)]).astype(np.float64)
    ang = np.repeat(inv, 2)[:, None] * pos[None, :]
    rope = np.stack([np.cos(ang), np.sin(ang)]).astype(np.float32)
    return cf.astype(np.float32), cfb.astype(np.float32), rope


def build_program(cfg):
    nc = bass.Bass("TRN2", target_bir_lowering=False)
    S, NT, NPG = cfg.S, cfg.NT, cfg.NPG
    kb = KB(nc)

    def din(name, shape, dt=F32):
        return nc.dram_tensor(name, list(shape), dt, kind="ExternalInput").ap()

    def dout(name, shape, dt=F32):
        return nc.dram_tensor(name, list(shape), dt, kind="ExternalOutput").ap()

    xp_d = din("xp", [S, D]); xs_d = din("xs", [16, D])
    sth_d = din("st_h", [4, 4, 128, 128]); str_d = din("st_r", [4, 4, 128, 128])
    ck_d = din("ck", [cfg.NPHYS, 128, 1024]); cv_d = din("cv", [cfg.NPHYS, 128, 1024])
    pt_d = din("pt", [1, 4 * NPG], I32)
    cvec_d = din("cvec", [5, D])
    wada_d = din("w_ada", [2, D, 6 * D]); bada_d = din("b_ada", [96, 128])
    normw_d = din("norm_w", [32, 128])
    winab_d = din("w_in_ab", [8, D, 512]); woutab_d = din("w_out_ab", [D, D])
    lbl_d = din("lb_logits", [8, 128]); hnw_d = din("hgrn_norm_w", [1, 128])
    winc_d = din("w_in_c", [4, D, 768]); woutc_d = din("w_out_c", [D, D])
    dlam_d = din("diff_lambda", [1, 512]); subln_d = din("subln_w", [1, 256])
    wup_d = din("w_up", [2, D, DFF]); wdn_d = din("w_down", [2, DFF, D])
    fnw_d = din("final_norm_w", [1, D])
    cf_d = din("cf", [128, NCF]); cfb_d = din("cfb", [128, NCFB]); rope_d = din("rope", [2, 128, NT])

    yp_d = dout("y_p", [S, D]); ys_d = dout("y_s", [16, D])
    hgp_d = dout("hg_p", [4, 128, 128]); rtp_d = dout("rt_p", [4, 128, 128])
    kp_d = dout("k_p", [S, 1024]); vp_d = dout("v_p", [S, 1024])
    hgs_d = dout("hg_s", [4, 4, 128, 128]); rts_d = dout("rt_s", [4, 4, 128, 128])
    ks_d = dout("k_s", [16, 1024]); vs_d = dout("v_s", [16, 1024])
    out_keys = []

    ar = Arena(nc)
    A = ar.alloc
    xT = A("xT", [128, NCH, NT], F32)
    hT = A("hT", [128, NCH, NT], BF16)
    cf = A("cf", [128, NCF], F32)
    identb = A("identb", [128, 128], BF16)
    maskb = A("maskb", [128, 128], BF16)
    ones_f = A("ones_f", [128, 128], F32)
    zeros_f = A("zeros_f", [128, 64], F32)
    modT = A("modT", [128, 2 * 48 * 5], F32)
    wmT = A("wmT", [128, 4 * 8 * 5], F32)
    nwT = A("nwT", [128, 32], F32)
    bT = A("bT", [128, 96], F32)
    scT = A("scT", [128, NCH, 5], BF16)
    lbv = A("lbv", [128, 12], F32)
    na = A("na", [128, 1], F32)
    neglam = A("neglam", [128, 1], F32)
    sgn8 = A("sgn8", [128, 1], F32)
    wsub = A("wsub", [128, 256], F32)
    ident = cf[:, CF_IDENT:CF_IDENT + 128]
    maskf = cf[:, CF_MASK:CF_MASK + 128]
    sqb = [A(f"sqb{i}", [128, 512], F32) for i in range(2)]
    rstd = A("rstd", [128, 512], F32)
    tnrm = [A(f"tnrm{i}", [128, 512], F32) for i in range(2)]
    phase_base = ar.mark()

    P = [nc.alloc_psum_tensor(f"P{i}", [128, 512], F32) for i in range(8)]
    Pb = [p[:].bitcast(BF16) for p in P]

    def mod_ap(l, j, c, b0, nb=1):
        o = ((l * 6 + j) * 8 + c) * 5 + b0
        return modT[:, o:o + nb]

    def wm_ap(l, i, c, b0):
        o = ((l * 2 + i) * 8 + c) * 5 + b0
        return wmT[:, o:o + 1]

    def bsegs(t0, n):
        if t0 < S:
            return [(t0, n, 0)]
        return [(S + 4 * i, 4, 1 + i) for i in range(4)]

    setup_m = ar.mark()
    c5 = A("c5", [5, D], F32)
    s5 = A("s5", [5, D], F32)
    ldrow = A("ldrow", [128, 128], F32)
    lam_t = A("lam_t", [1, 520], F32)
    wada_sl = [A(f"wada{i}", [128, NCH, 512], BF16) for i in range(2)]

    kb.dma("sp", I("dma_start", out=cf[:], in_=cf_d[:]), "ld_cf", w=["cf"])
    kb.dma("sp", I("dma_start", out=c5[:], in_=cvec_d[:]), "ld_c5", w=["c5"])
    kb.op("dve", I("memset", ones_f[:], 1.0), w=["ones_f"])
    kb.op("dve", I("memset", zeros_f[:], 0.0), w=["zeros_f"])
    kb.op("dve", I("tensor_copy", identb[:], ident), r=["cf"], w=["identb"])
    kb.op("dve", I("tensor_copy", maskb[:], maskf), r=["cf"], w=["maskb"])

    kb.op("act", I("activation", out=s5[:], in_=c5[:], func=AF.Silu), r=["c5"], w=["s5"])
    for c in range(NCH):
        kb.op("pe", I("transpose", P[0][:, c * 5:(c + 1) * 5], s5[:, c * 128:(c + 1) * 128], cf[0:5, 0:5]),
              r=["s5", "cf"], w=["P0"])
    kb.op("dve", I("tensor_copy", scT[:].rearrange("p c b -> p (c b)"), P[0][:, 0:40]), r=["P0"], w=["scT"])

    def load_cols(src_ap, nrows, dst_ap, key, tag):
        kb.dma("sp", I("dma_start", out=ldrow[0:nrows, :], in_=src_ap), "ld_row", w=["ldrow"])
        kb.op("pe", I("transpose", P[1][:, 0:nrows], ldrow[0:nrows, :], cf[0:nrows, 0:nrows]),
              r=["ldrow", "cf"], w=["P1"])
        kb.op("dve", I("tensor_copy", dst_ap, P[1][:, 0:nrows]), r=["P1"], w=[key])

    load_cols(bada_d[:], 96, bT[:], "bT", "b")
    load_cols(normw_d[:], 32, nwT[:], "nwT", "n")
    lbt = A("lbt", [128, 8], F32)
    load_cols(lbl_d[:], 8, lbt[:], "lbt", "l")
    load_cols(hnw_d[:], 1, na[:], "na", "h")
    kb.op("dve", I("tensor_tensor", out=lbt[:, 0:4], in0=lbt[:, 0:4], in1=lbt[:, 4:8], op=ALU.subtract),
          r=["lbt"], w=["lbt"])
    kb.op("act", I("activation", out=lbv[:, 0:4], in_=lbt[:, 0:4], func=AF.Sigmoid), r=["lbt"], w=["lbv"])
    kb.op("dve", I("tensor_scalar", out=lbv[:, 4:8], in0=lbv[:, 0:4], scalar1=-1.0, scalar2=1.0,
                                           op0=ALU.mult, op1=ALU.add), r=["lbv"], w=["lbv"])
    kb.op("dve", I("tensor_scalar", out=lbv[:, 8:12], in0=lbv[:, 0:4], scalar1=-1.0, scalar2=None,
                                           op0=ALU.add), r=["lbv"], w=["lbv"])

    kb.dma("sp", I("dma_start", out=lam_t[:, 0:512], in_=dlam_d[:]), "ld_lam", w=["lam_t"])
    kb.op("dve", I("tensor_tensor", out=lam_t[:, 0:128], in0=lam_t[:, 0:128], in1=lam_t[:, 128:256], op=ALU.mult),
          r=["lam_t"], w=["lam_t"])
    kb.op("dve", I("tensor_tensor", out=lam_t[:, 256:384], in0=lam_t[:, 256:384], in1=lam_t[:, 384:512], op=ALU.mult),
          r=["lam_t"], w=["lam_t"])
    kb.op("dve", I("reduce_sum", out=lam_t[:, 512:513], in_=lam_t[:, 0:128], axis=mybir.AxisListType.X),
          r=["lam_t"], w=["lam_t"])
    kb.op("dve", I("reduce_sum", out=lam_t[:, 513:514], in_=lam_t[:, 256:384], axis=mybir.AxisListType.X),
          r=["lam_t"], w=["lam_t"])
    kb.op("act", I("activation", out=lam_t[:, 514:516], in_=lam_t[:, 512:514], func=AF.Exp), r=["lam_t"], w=["lam_t"])
    kb.op("dve", I("tensor_tensor", out=lam_t[:, 516:517], in0=lam_t[:, 515:516], in1=lam_t[:, 514:515], op=ALU.subtract),
          r=["lam_t"], w=["lam_t"])
    kb.op("dve", I("tensor_scalar", out=lam_t[:, 518:519], in0=lam_t[:, 516:517], scalar1=-LAM_INIT,
                                           scalar2=None, op0=ALU.add), r=["lam_t"], w=["lam_t"])
    kb.op("pe", I("matmul", P[1][:, 0:1], lhsT=ones_f[0:1, :], rhs=lam_t[:, 518:519], start=True, stop=True),
          r=["lam_t", "ones_f"], w=["P1"])
    kb.op("dve", I("tensor_copy", neglam[:], P[1][:, 0:1]), r=["P1"], w=["neglam"])
    kb.op("dve", I("scalar_tensor_tensor", out=sgn8[:, :], in0=cf[:, CF_M1:CF_M1 + 1], scalar=neglam[:, 0:1],
                                                  in1=cf[:, CF_M0:CF_M0 + 1], op0=ALU.mult, op1=ALU.add),
          r=["neglam", "cf"], w=["sgn8"])
    kb.dma("sp", I("dma_start", out=wsub[:], in_=subln_d[:].partition_broadcast(128)), "ld_bc", w=["wsub"])
    kb.op("dve", I("tensor_scalar", out=wsub[:], in0=wsub[:], scalar1=1.0 - LAM_INIT, scalar2=None, op0=ALU.mult),
          r=["wsub"], w=["wsub"])

    for l in range(2):
        for g in range(12):
            sl = (l * 12 + g) % 2
            kb.dma("pool", I("dma_start",
                out=wada_sl[sl][:], in_=wada_d[l, :, g * 512:(g + 1) * 512].rearrange("(c p) n -> p c n", p=128)),
                f"ld_wada{sl}", w=[f"wada{sl}"])
            pb = P[2 + sl]
            for blk in range(4):
                for c in range(NCH):
                    kb.op("pe", I("matmul",
                        pb[:, blk * 5:(blk + 1) * 5], lhsT=wada_sl[sl][:, c, blk * 128:(blk + 1) * 128], rhs=scT[:, c, :],
                        start=(c == 0), stop=(c == NCH - 1)), r=[f"wada{sl}", "scT"], w=[f"P{2 + sl}"])
            for blk in range(4):
                bi = l * 48 + g * 4 + blk
                kb.op("dve", I("tensor_scalar",
                    out=modT[:, bi * 5:(bi + 1) * 5], in0=pb[:, blk * 5:(blk + 1) * 5], scalar1=bT[:, bi:bi + 1], scalar2=None,
                    op0=ALU.add), r=[f"P{2 + sl}", "bT"], w=["modT"])
    for l in range(2):
        for i in range(2):
            for c in range(NCH):
                o = ((l * 2 + i) * 8 + c)
                kb.op("dve", I("tensor_scalar",
                    out=wmT[:, o * 5:(o + 1) * 5], in0=mod_ap(l, 1 + 3 * i, c, 0, 5), scalar1=1.0, scalar2=nwT[:, o:o + 1],
                    op0=ALU.add, op1=ALU.mult), r=["modT", "nwT"], w=["wmT"])

    xld = [A(f"xld{i}", [128, D], F32) for i in range(2)]
    n_xt = S // 128
    for it in range(n_xt + 1):
        sl = it % 2
        rows = 128 if it < n_xt else 16
        src = xp_d[it * 128:(it + 1) * 128, :] if it < n_xt else xs_d[:]
        kb.dma("sp", I("dma_start", out=xld[sl][0:rows, :], in_=src),
               f"ld_x{sl}", w=[f"xld{sl}"])
        for half in range(2):
            pb = P[4 + 2 * sl + half]
            pk = f"P{4 + 2 * sl + half}"
            for cc in range(4):
                c = half * 4 + cc
                kb.op("pe", I("transpose",
                    pb[:, cc * 128:cc * 128 + rows], xld[sl][0:rows, c * 128:(c + 1) * 128], cf[0:rows, 0:rows]),
                    r=[f"xld{sl}", "cf"], w=[pk])
            t0 = it * 128
            eng = "act" if half == 0 else "dve"
            if eng == "act":
                kb.op("act", I("copy",
                    xT[:, half * 4:(half + 1) * 4, t0:t0 + rows],
                    pb[:].rearrange("p (c t) -> p c t", c=4)[:, :, 0:rows]), r=[pk], w=["xT"])
            else:
                kb.op("dve", I("tensor_copy",
                    xT[:, half * 4:(half + 1) * 4, t0:t0 + rows],
                    pb[:].rearrange("p (c t) -> p c t", c=4)[:, :, 0:rows]), r=[pk], w=["xT"])
    kb.barrier()
    ar.reset(setup_m)

    def prenorm(l, i):
        jsh = 3 * i
        for ti, (t0, n) in enumerate(cfg.tiles):
            pk = f"P{ti % 2}"
            pb = P[ti % 2]
            for c in range(NCH):
                sq = sqb[c % 2]
                kb.op("act", I("activation", out=sq[:, 0:n], in_=xT[:, c, t0:t0 + n], func=AF.Square),
                      r=["xT"], w=[f"sqb{c % 2}"])
                kb.op("pe", I("matmul", pb[:, 0:n], lhsT=ones_f[:], rhs=sq[:, 0:n],
                                                                      start=(c == 0), stop=(c == NCH - 1)),
                      r=[f"sqb{c % 2}", "ones_f"], w=[pk])
            kb.op("act", I("activation", out=rstd[:, 0:n], in_=pb[:, 0:n], func=AF.Sqrt, scale=1.0 / D, bias=EPS),
                  r=[pk], w=["rstd"])
            kb.op("dve", I("reciprocal", rstd[:, 0:n], rstd[:, 0:n]), r=["rstd"], w=["rstd"])
            for c in range(NCH):
                tn = tnrm[c % 2]
                kb.op("dve", I("tensor_tensor", out=tn[:, 0:n], in0=xT[:, c, t0:t0 + n], in1=rstd[:, 0:n],
                                                                              op=ALU.mult), r=["xT", "rstd"], w=[f"tnrm{c % 2}"])
                for (s0, sn, b) in bsegs(t0, n):
                    kb.op("act", I("activation",
                        out=hT[:, c, s0:s0 + sn], in_=tn[:, s0 - t0:s0 - t0 + sn], func=AF.Identity,
                        scale=wm_ap(l, i, c, b), bias=mod_ap(l, jsh, c, b)),
                        r=[f"tnrm{c % 2}", "wmT", "modT"], w=["hT"])

    def resid_add(pb, pk, cc, t0, n, l, jg):
        for (s0, sn, b) in bsegs(t0, n):
            kb.op("dve", I("scalar_tensor_tensor",
                out=xT[:, cc, s0:s0 + sn], in0=pb[:, s0 - t0:s0 - t0 + sn], scalar=mod_ap(l, jg, cc, b),
                in1=xT[:, cc, s0:s0 + sn], op0=ALU.mult, op1=ALU.add), r=[pk, "modT", "xT"], w=["xT"])

    def mlp(l):
        m = ar.mark()
        wup = [A(f"wup{i}", [128, NCH, 1024], BF16) for i in range(2)]
        wdn = [A(f"wdn{i}", [128, NCH, 1024], BF16) for i in range(2)]
        uT = [A(f"uT{i}", [128, NCH, 512], BF16) for i in range(2)]
        rl = [A(f"rl{i}", [128, 512], BF16) for i in range(2)]
        prenorm(l, 1)
        it = 0
        nup = 0
        ndn = 0
        for g in range(4):
            sl = g % 2
            kb.dma("pool", I("dma_start",
                out=wup[sl][:], in_=wup_d[l, :, g * 1024:(g + 1) * 1024].rearrange("(c p) n -> p c n", p=128)),
                f"ld_wup{sl}", w=[f"wup{sl}"])
            kb.dma("pool", I("dma_start",
                out=wdn[sl][:], in_=wdn_d[l, g * 1024:(g + 1) * 1024, :].rearrange("(f p) n -> p f n", p=128)),
                f"ld_wdn{sl}", w=[f"wdn{sl}"])
            for (t0, n) in cfg.tiles:
                us = it % 2
                it += 1
                for f in range(NCH):
                    pi = nup % 4
                    nup += 1
                    pb, pk = P[pi], f"P{pi}"
                    for c in range(NCH):
                        kb.op("pe", I("matmul",
                            pb[:, 0:n], lhsT=wup[sl][:, c, f * 128:(f + 1) * 128], rhs=hT[:, c, t0:t0 + n],
                            start=(c == 0), stop=(c == NCH - 1)), r=[f"wup{sl}", "hT"], w=[pk])
                    rs = f % 2
                    kb.op("act", I("activation", out=rl[rs][:, 0:n], in_=pb[:, 0:n], func=AF.Relu),
                          r=[pk], w=[f"rl{rs}"])
                    kb.op("dve", I("tensor_tensor",
                        out=uT[us][:, f, 0:n], in0=pb[:, 0:n], in1=rl[rs][:, 0:n], op=ALU.mult),
                        r=[pk, f"rl{rs}"], w=[f"uT{us}"])
                for cc in range(NCH):
                    pi = 4 + ndn % 4
                    ndn += 1
                    pb, pk = P[pi], f"P{pi}"
                    for f in range(NCH):
                        kb.op("pe", I("matmul",
                            pb[:, 0:n], lhsT=wdn[sl][:, f, cc * 128:(cc + 1) * 128], rhs=uT[us][:, f, 0:n],
                            start=(f == 0), stop=(f == NCH - 1)), r=[f"wdn{sl}", f"uT{us}"], w=[pk])
                    resid_add(pb, pk, cc, t0, n, l, 5)
        kb.barrier()
        ar.reset(m)

    def final_out():
        m = ar.mark()
        ybuf = [A(f"ybuf{i}", [128, D], F32) for i in range(2)]
        finw = A("finw", [128, D], F32)
        kb.dma("sp", I("dma_start", out=finw[:], in_=fnw_d[:].partition_broadcast(128)), "ld_bc2", w=["finw"])
        ssq = A("ssq", [128, 8], F32)
        junk = A("junk", [128, 512], F32)
        n_t = S // 128
        for it in range(n_t + 1):
            sl = it % 2
            rows = 128 if it < n_t else 16
            t0 = it * 128
            for half in range(2):
                pi = 2 * sl + half
                pb, pk = P[pi], f"P{pi}"
                for cc in range(4):
                    c = half * 4 + cc
                    kb.op("pe", I("transpose",
                        pb[0:rows, cc * 128:(cc + 1) * 128], xT[:, c, t0:t0 + rows], ident), r=["xT", "cf"], w=[pk])
                kb.op("act", I("activation",
                    out=junk[0:rows, :], in_=pb[0:rows, :], func=AF.Square, accum_out=ssq[0:rows, 2 * sl + half:2 * sl + half + 1]),
                    r=[pk], w=["junk", f"ssq{sl}{half}"])
            kb.op("dve", I("tensor_tensor",
                out=ssq[0:rows, 4 + sl:5 + sl], in0=ssq[0:rows, 2 * sl:2 * sl + 1], in1=ssq[0:rows, 2 * sl + 1:2 * sl + 2], op=ALU.add),
                r=[f"ssq{sl}0", f"ssq{sl}1"], w=[f"ssqs{sl}"])
            kb.op("act", I("activation",
                out=ssq[0:rows, 6 + sl:7 + sl], in_=ssq[0:rows, 4 + sl:5 + sl], func=AF.Sqrt, scale=1.0 / D, bias=EPS),
                r=[f"ssqs{sl}"], w=[f"ssqr{sl}"])
            kb.op("dve", I("reciprocal", ssq[0:rows, 6 + sl:7 + sl], ssq[0:rows, 6 + sl:7 + sl]),
                  r=[f"ssqr{sl}"], w=[f"ssqr{sl}"])
            for half in range(2):
                pi = 2 * sl + half
                pb, pk = P[pi], f"P{pi}"
                kb.op("dve", I("scalar_tensor_tensor",
                    out=ybuf[sl][0:rows, half * 512:(half + 1) * 512], in0=pb[0:rows, :], scalar=ssq[0:rows, 6 + sl:7 + sl],
                    in1=finw[0:rows, half * 512:(half + 1) * 512], op0=ALU.mult, op1=ALU.mult),
                    r=[pk, f"ssqr{sl}", "finw"], w=[f"ybuf{sl}"])
            dst = yp_d[t0:t0 + 128, :] if it < n_t else ys_d[:]
            key = f"y{it}"
            kb.dma("sp", I("dma_start", out=dst, in_=ybuf[sl][0:rows, :]),
                   f"st_y{sl}", r=[f"ybuf{sl}"], w=[key])
            out_keys.append(key)
        ar.reset(m)

    from_layers(cfg, nc, kb, ar, A, P, Pb, locals())
    return nc


def from_layers(cfg, nc, kb, ar, A, P, Pb, env):
    stages = getattr(cfg, "stages", ("mix0", "mlp0", "mix1", "mlp1"))
    if "mix0" in stages:
        layer0_mixer(cfg, nc, kb, ar, A, P, Pb, env)
    if "mlp0" in stages:
        env["mlp"](0)
    if "mix1" in stages:
        layer1_mixer(cfg, nc, kb, ar, A, P, Pb, env)
    if "mlp1" in stages:
        env["mlp"](1)
    env["final_out"]()
    waits = kb._deps("sp", tuple(env["out_keys"]), ())
    kb.q["sp"].append((waits, None, None, 0))
    kb.emit()


def prep_shared(cfg, inp):
    f = lambda a: np.ascontiguousarray(np.asarray(a))
    w4 = np.asarray(inp["w_in_ab"])[0].reshape(D, 8, 4, 128)
    units = []
    for u in range(4):
        units.append(np.concatenate([w4[:, 0, u], w4[:, 1, u], w4[:, 3, u], w4[:, 2, u]], axis=1))
    for u in range(4):
        units.append(np.concatenate([w4[:, 4, u], w4[:, 5, u], w4[:, 7, u], w4[:, 6, u]], axis=1))
    wc = np.asarray(inp["w_in_c"])[0]
    heads = []
    for h in range(4):
        heads.append(np.concatenate([wc[:, h * 256:(h + 1) * 256], wc[:, 1024 + h * 256:1024 + (h + 1) * 256],
                                     wc[:, 2048 + h * 256:2048 + (h + 1) * 256]], axis=1))
    cf, cfb, rope = make_consts(cfg)
    return {
        "ck": f(np.asarray(inp["cache_k"])[0].reshape(cfg.NPHYS, 128, 1024)),
        "cv": f(np.asarray(inp["cache_v"])[0].reshape(cfg.NPHYS, 128, 1024)),
        "w_ada": f(inp["w_ada"]), "b_ada": f(np.asarray(inp["b_ada"]).reshape(96, 128)),
        "norm_w": f(np.asarray(inp["norm_w"]).reshape(32, 128)),
        "w_in_ab": f(np.stack(units)), "w_out_ab": f(np.asarray(inp["w_out_ab"])[0]),
        "lb_logits": f(np.asarray(inp["hgrn_lb_logits"]).reshape(8, 128)),
        "hgrn_norm_w": f(np.asarray(inp["hgrn_norm_w"]).reshape(1, 128)),
        "w_in_c": f(np.stack(heads)), "w_out_c": f(np.asarray(inp["w_out_c"])[0]),
        "diff_lambda": f(np.asarray(inp["diff_lambda"]).reshape(1, 512)),
        "subln_w": f(np.asarray(inp["diff_subln_w"]).reshape(1, 256)),
        "w_up": f(inp["w_mlp_up"]), "w_down": f(inp["w_mlp_down"]),
        "final_norm_w": f(np.asarray(inp["final_norm_w"]).reshape(1, D)),
        "cf": cf, "cfb": cfb, "rope": rope,
    }


def prep_core(cfg, inp, shared, c):
    f = lambda a: np.ascontiguousarray(np.asarray(a))
    m = dict(shared)
    m["xp"] = f(np.asarray(inp["x_prompt"])[c])
    m["xs"] = f(np.asarray(inp["x_sample"])[4 * c:4 * c + 4].reshape(16, D))
    m["st_h"] = f(np.asarray(inp["state_hgrn"])[0, 4 * c:4 * c + 4])
    m["st_r"] = f(np.asarray(inp["state_ret"])[0, 4 * c:4 * c + 4])
    m["pt"] = f(np.asarray(inp["page_table"])[4 * c:4 * c + 4].reshape(1, 4 * cfg.NPG).astype(np.int32))
    m["cvec"] = f(np.concatenate([np.asarray(inp["c_prompt"])[c:c + 1], np.asarray(inp["c_sample"])[4 * c:4 * c + 4]], axis=0))
    return m


def assemble(cfg, res):
    n = cfg.n_cores
    S = cfg.S
    g = lambda k: [np.asarray(r[k]) for r in res]
    y_p = np.stack(g("y_p"))
    y_s = np.concatenate([a.reshape(4, 4, D) for a in g("y_s")], axis=0)
    hg_p = np.stack(g("hg_p"))[None]
    rt_p = np.stack(g("rt_p"))[None]
    k_p = np.stack([a.reshape(S // 128, 128, 4, 2, 128) for a in g("k_p")])[None]
    v_p = np.stack([a.reshape(S // 128, 128, 4, 256) for a in g("v_p")])[None]
    hg_s = np.concatenate(g("hg_s"), axis=0)[None]
    rt_s = np.concatenate(g("rt_s"), axis=0)[None]
    k_s = np.concatenate([a.reshape(4, 4, 4, 2, 128) for a in g("k_s")], axis=0)[None]
    v_s = np.concatenate([a.reshape(4, 4, 4, 256) for a in g("v_s")], axis=0)[None]
    return tuple(np.ascontiguousarray(a.astype(np.float32)) for a in (y_p, y_s, hg_p, rt_p, k_p, v_p, hg_s, rt_s, k_s, v_s))


_NC_CACHE = {}


def kernel(**inputs):
    cfg = Cfg()
    if "nc" not in _NC_CACHE:
        _NC_CACHE["nc"] = build_program(cfg)
    nc = _NC_CACHE["nc"]
    shared = prep_shared(cfg, inputs)
    in_maps = [prep_core(cfg, inputs, shared, c) for c in range(cfg.n_cores)]
    res = run_bass_kernel_spmd(nc, in_maps, core_ids=list(range(cfg.n_cores)))
    return assemble(cfg, res.results)


def layer0_mixer(cfg, nc, kb, ar, A, P, Pb, env):
    S, NT = cfg.S, cfg.NT
    xT, hT, cf, identb, ones_f, zeros_f, lbv, na = (env[k] for k in
                                                    ("xT", "hT", "cf", "identb", "ones_f", "zeros_f", "lbv", "na"))
    sqb, rstd, tnrm = env["sqb"], env["rstd"], env["tnrm"]
    maskf = env["maskf"]
    d = env
    winab_d, woutab_d, rope_d = d["winab_d"], d["woutab_d"], d["rope_d"]
    sth_d, str_d, hgp_d, rtp_d, hgs_d, rts_d = d["sth_d"], d["str_d"], d["hgp_d"], d["rtp_d"], d["hgs_d"], d["rts_d"]
    out_keys = d["out_keys"]
    m0 = ar.mark()
    oT = A("oT", [128, NCH, NT], BF16)
    cfb = A("cfb", [128, NCFB], F32)
    kb.dma("sp", I("dma_start", out=cfb[:], in_=d["cfb_d"][:]), "ld_cfb", w=["cfb"])
    m_w = ar.mark()
    wu = [A(f"wu{i}", [128, NCH, 512], BF16) for i in range(2)]
    wR = A("wR", [128, NCH, 256], BF16)
    f32t = {k: A(k, [128, 512], F32) for k in ("qs", "sg", "om", "bb", "rb")}
    qTt = A("qTt", [128, 512], BF16); kTt = A("kTt", [128, 512], BF16)
    gs = A("gs", [128, 512], BF16); vTt = A("vTt", [128, 512], BF16)
    cosT = A("cosT", [128, 512], F32); sinT = A("sinT", [128, 512], F32)
    Am2 = [A(f"Am{i}", [128, 128], BF16) for i in range(2)]
    kvtok2 = [A(f"kvtok{i}", [128, 256], BF16) for i in range(2)]
    qd2 = [A(f"qd{i}", [128, 128], BF16) for i in range(2)]
    xcnt = [0]
    U = A("U", [128, 128], F32); Sbf = A("Sbf", [128, 128], BF16); Sfin = A("Sfin", [128, 128], F32)
    belast = A("belast", [128, 1], F32)
    qs, sg, om, bb, rb = (f32t[k] for k in ("qs", "sg", "om", "bb", "rb"))

    env["prenorm"](0, 0)

    pjc = [0]

    def proj(ws_ap_fn, t0, n, pi_unused, rkeys):
        pi = pjc[0] % 2
        pjc[0] += 1
        pb, pk = P[pi], f"P{pi}"
        for c in range(NCH):
            kb.op("pe", I("matmul", pb[:, 0:n], lhsT=ws_ap_fn(c), rhs=hT[:, c, t0:t0 + n],
                                                start=(c == 0), stop=(c == NCH - 1)), r=list(rkeys) + ["hT"], w=[pk])
        return pb, pk

    for u in range(8):
        ret = u >= 4
        h = u % 4
        sl = u % 2
        wk = f"wu{sl}"
        kb.dma("pool", I("dma_start", out=wu[sl][:], in_=winab_d[u].rearrange("(c p) n -> p c n", p=128)),
               f"ld_wu{sl}", w=[wk])
        if ret:
            wuv = wu[sl][:, :, 0:256].rearrange("p c (i two) -> p c i two", two=2)
            wRv = wR[:].rearrange("p c (i two) -> p c i two", two=2)
            kb.op("pool", I("tensor_scalar", out=wRv[:, :, :, 0], in0=wuv[:, :, :, 1], scalar1=-1.0, scalar2=None,
                                                                     op0=ALU.mult), r=[wk], w=["wR"])
            kb.op("pool", I("tensor_copy", wRv[:, :, :, 1], wuv[:, :, :, 0]), r=[wk], w=["wR"])
            g_h = 1.0 - 2.0 ** (-5.0 - h)
        st_in = str_d if ret else sth_d
        st_out_p = rtp_d if ret else hgp_d
        st_out_s = rts_d if ret else hgs_d
        for (t0, n) in cfg.tiles:
            is_s = t0 >= S
            W = wu[sl]
            if not ret:
                pb, pk = proj(lambda c: W[:, c, 0:128], t0, n, 0, [wk])
                kb.op("act", I("activation", out=qs[:, 0:n], in_=pb[:, 0:n], func=AF.Silu), r=[pk], w=["qs"])
                pb, pk = proj(lambda c: W[:, c, 128:256], t0, n, 1, [wk])
                kb.op("act", I("activation", out=sg[:, 0:n], in_=pb[:, 0:n], func=AF.Sigmoid), r=[pk], w=["sg"])
                pb, pk = proj(lambda c: W[:, c, 256:384], t0, n, 2, [wk])
                kb.op("act", I("activation", out=gs[:, 0:n], in_=pb[:, 0:n], func=AF.Silu), r=[pk], w=["gs"])
                pb, pk = proj(lambda c: W[:, c, 384:512], t0, n, 0, [wk])
                kb.op("act", I("copy", vTt[:, 0:n], pb[:, 0:n]), r=[pk], w=["vTt"])
                kb.op("dve", I("tensor_scalar", out=om[:, 0:n], in0=sg[:, 0:n], scalar1=lbv[:, 8 + h:9 + h], scalar2=lbv[:, 4 + h:5 + h],
                                                       op0=ALU.mult, op1=ALU.add), r=["sg", "lbv"], w=["om"])
                kb.op("dve", I("tensor_scalar", out=sg[:, 0:n], in0=sg[:, 0:n], scalar1=lbv[:, 4 + h:5 + h], scalar2=lbv[:, h:h + 1],
                                                       op0=ALU.mult, op1=ALU.add), r=["sg", "lbv"], w=["sg"])
                C = 4 if is_s else 64
                for c0 in range(0, n, C):
                    kb.op("dve", I("tensor_tensor_scan",
                        out=bb[:, c0:c0 + C], data0=sg[:, c0:c0 + C], data1=zeros_f[:, 0:C], initial=1.0, op0=ALU.mult, op1=ALU.add),
                        r=["sg", "zeros_f"], w=["bb"])
                kb.op("dve", I("reciprocal", rb[:, 0:n], bb[:, 0:n]), r=["bb"], w=["rb"])
                kb.op("dve", I("scalar_tensor_tensor", out=qTt[:, 0:n], in0=qs[:, 0:n], scalar=DK_SCALE, in1=bb[:, 0:n],
                                                              op0=ALU.mult, op1=ALU.mult), r=["qs", "bb"], w=["qTt"])
                kb.op("dve", I("tensor_tensor", out=kTt[:, 0:n], in0=om[:, 0:n], in1=rb[:, 0:n], op=ALU.mult),
                      r=["om", "rb"], w=["kTt"])
            else:
                kb.dma("sp", I("dma_start", out=cosT[:, 0:n], in_=rope_d[0, :, t0:t0 + n]), "ld_cos", w=["cosT"])
                kb.dma("sp", I("dma_start", out=sinT[:, 0:n], in_=rope_d[1, :, t0:t0 + n]), "ld_sin", w=["sinT"])
                pa, pka = proj(lambda c: W[:, c, 0:128], t0, n, 0, [wk])
                pr, pkr = proj(lambda c: wR[:, c, 0:128], t0, n, 1, ["wR"])
                kb.op("dve", I("tensor_tensor", out=qs[:, 0:n], in0=pa[:, 0:n], in1=cosT[:, 0:n], op=ALU.mult),
                      r=[pka, "cosT"], w=["qs"])
                kb.op("dve", I("tensor_tensor", out=sg[:, 0:n], in0=pr[:, 0:n], in1=sinT[:, 0:n], op=ALU.mult),
                      r=[pkr, "sinT"], w=["sg"])
                kb.op("pool", I("tensor_tensor", out=qTt[:, 0:n], in0=qs[:, 0:n], in1=sg[:, 0:n], op=ALU.add),
                      r=["qs", "sg"], w=["qTt"])
                pa, pka = proj(lambda c: W[:, c, 128:256], t0, n, 2, [wk])
                pr, pkr = proj(lambda c: wR[:, c, 128:256], t0, n, 0, ["wR"])
                kb.op("dve", I("scalar_tensor_tensor", out=om[:, 0:n], in0=pa[:, 0:n], scalar=DK_SCALE, in1=cosT[:, 0:n],
                                                                     op0=ALU.mult, op1=ALU.mult), r=[pka, "cosT"], w=["om"])
                kb.op("dve", I("scalar_tensor_tensor", out=rb[:, 0:n], in0=pr[:, 0:n], scalar=DK_SCALE, in1=sinT[:, 0:n],
                                                                     op0=ALU.mult, op1=ALU.mult), r=[pkr, "sinT"], w=["rb"])
                kb.op("pool", I("tensor_tensor", out=kTt[:, 0:n], in0=om[:, 0:n], in1=rb[:, 0:n], op=ALU.add),
                      r=["om", "rb"], w=["kTt"])
                pb, pk = proj(lambda c: W[:, c, 256:384], t0, n, 1, [wk])
                kb.op("act", I("activation", out=gs[:, 0:n], in_=pb[:, 0:n], func=AF.Silu), r=[pk], w=["gs"])
                pb, pk = proj(lambda c: W[:, c, 384:512], t0, n, 2, [wk])
                kb.op("act", I("copy", vTt[:, 0:n], pb[:, 0:n]), r=[pk], w=["vTt"])
                C = 4 if is_s else 128

            nchunks = n // C

            def stage_x(ci):
                c0 = ci * C
                b2 = xcnt[0] % 2
                xcnt[0] += 1
                xb[ci] = b2
                state_zero = (not is_s) and t0 == 0 and ci == 0
                pa, pka = P[2 + b2], f"P{2 + b2}"
                pt, pkt = Pb[4 + b2], f"P{4 + b2}"
                kb.op("pe", I("matmul", pa[0:C, 0:C], lhsT=kTt[:, c0:c0 + C], rhs=qTt[:, c0:c0 + C], start=True, stop=True),
                      r=["kTt", "qTt"], w=[pka])
                kb.op("pe", I("transpose", pt[0:C, 0:128], kTt[:, c0:c0 + C], identb[:]), r=["kTt", "identb"], w=[pkt])
                kb.op("pe", I("transpose", pt[0:C, 128:256], vTt[:, c0:c0 + C], identb[:]), r=["vTt", "identb"], w=[pkt])
                if not ret:
                    mk_ap = maskf[0:C, 0:C]
                else:
                    mk_ap = cfb[0:C, CF_DT + h * 128:CF_DT + h * 128 + C]
                kb.op("dve", I("tensor_tensor", out=Am2[b2][0:C, 0:C], in0=pa[0:C, 0:C], in1=mk_ap, op=ALU.mult),
                      r=[pka, "cf", "cfb"], w=[f"Am{b2}"])
                if not ret:
                    kb.op("act", I("copy", kvtok2[b2][0:C, :], pt[0:C, 0:256]), r=[pkt], w=[f"kvtok{b2}"])
                else:
                    kd = cfb[0:C, (CF_KD4 if is_s else CF_KD128) + h:(CF_KD4 if is_s else CF_KD128) + h + 1]
                    kb.op("act", I("activation", out=kvtok2[b2][0:C, 0:128], in_=pt[0:C, 0:128], func=AF.Identity, scale=kd),
                          r=[pkt, "cfb"], w=[f"kvtok{b2}"])
                    kb.op("act", I("copy", kvtok2[b2][0:C, 128:256], pt[0:C, 128:256]), r=[pkt], w=[f"kvtok{b2}"])
                    if not state_zero:
                        kb.op("pool", I("tensor_tensor", out=qd2[b2][:, 0:C], in0=qTt[:, c0:c0 + C], in1=cfb[:, CF_G1 + h * 128:CF_G1 + h * 128 + C],
                                        op=ALU.mult), r=["qTt", "cfb"], w=[f"qd{b2}"])

            def stage_y(ci):
                c0 = ci * C
                b2 = xb[ci]
                Amb, kvb, qdb = Am2[b2], kvtok2[b2], qd2[b2]
                seq = ci if is_s else None
                state_zero = (not is_s) and t0 == 0 and ci == 0
                if is_s:
                    kb.dma("sp", I("dma_start", out=U[:], in_=st_in[seq, h]), "ld_U", w=["U"])
                    kb.op("dve", I("tensor_copy", Sbf[:], U[:]), r=["U"], w=["Sbf"])
                kb.op("pe", I("matmul", P[7][:, c0:c0 + C], lhsT=kvb[0:C, 128:256], rhs=Amb[0:C, 0:C],
                              start=True, stop=state_zero), r=[f"kvtok{b2}", f"Am{b2}"], w=["P7"])
                if not state_zero:
                    q_in = qdb[:, 0:C] if ret else qTt[:, c0:c0 + C]
                    kb.op("pe", I("matmul", P[7][:, c0:c0 + C], lhsT=Sbf[:], rhs=q_in, start=False, stop=True),
                          r=["Sbf", f"qd{b2}", "qTt"], w=["P7"])
                kb.op("pe", I("matmul", P[6][:, 0:128], lhsT=kvb[0:C, 0:128], rhs=kvb[0:C, 128:256], start=True, stop=True),
                      r=[f"kvtok{b2}"], w=["P6"])
                if state_zero:
                    kb.op("dve", I("tensor_copy", U[:], P[6][:, 0:128]), r=["P6"], w=["U"])
                else:
                    if ret:
                        sc_prev = g_h ** C
                    elif is_s:
                        sc_prev = 1.0
                    elif ci == 0:
                        sc_prev = belast[:, 0:1]
                    else:
                        sc_prev = bb[:, c0 - 1:c0]
                    kb.op("dve", I("scalar_tensor_tensor", out=U[:], in0=U[:], scalar=sc_prev, in1=P[6][:, 0:128],
                                                                                 op0=ALU.mult, op1=ALU.add),
                          r=["U", "P6", "bb", "belast"], w=["U"])
                last_of_seq = is_s or (t0 + n == S and ci == nchunks - 1)
                be_cur = bb[:, c0 + C - 1:c0 + C]
                if not last_of_seq:
                    if ret:
                        kb.op("act", I("copy", Sbf[:], U[:]), r=["U"], w=["Sbf"])
                    else:
                        kb.op("dve", I("tensor_scalar", out=Sbf[:], in0=U[:], scalar1=be_cur, scalar2=None, op0=ALU.mult),
                              r=["U", "bb"], w=["Sbf"])
                        if ci == nchunks - 1:
                            kb.op("dve", I("tensor_copy", belast[:], be_cur), r=["bb"], w=["belast"])
                else:
                    dst = st_out_s[seq, h] if is_s else st_out_p[h]
                    key = f"st_{u}_{seq}"
                    if ret:
                        kb.op("act", I("copy", Sfin[:], U[:]), r=["U"], w=["Sfin"])
                    else:
                        kb.op("dve", I("tensor_scalar", out=Sfin[:], in0=U[:], scalar1=be_cur, scalar2=None, op0=ALU.mult),
                              r=["U", "bb"], w=["Sfin"])
                    kb.dma("sp", I("dma_start", out=dst, in_=Sfin[:]), "st_S", r=["Sfin"], w=[key])
                    out_keys.append(key)

            xb = {}
            stage_x(0)
            for ci in range(nchunks):
                if ci + 1 < nchunks:
                    stage_x(ci + 1)
                stage_y(ci)

            kb.op("act", I("activation", out=sqb[0][:, 0:n], in_=P[7][:, 0:n], func=AF.Square), r=["P7"], w=["sqb0"])
            pns = pjc[0] % 2
            pjc[0] += 1
            kb.op("pe", I("matmul", P[pns][:, 0:n], lhsT=ones_f[:], rhs=sqb[0][:, 0:n], start=True, stop=True),
                  r=["sqb0", "ones_f"], w=[f"P{pns}"])
            kb.op("act", I("activation", out=rstd[:, 0:n], in_=P[pns][:, 0:n], func=AF.Sqrt, scale=1.0 / 128, bias=EPS),
                  r=[f"P{pns}"], w=["rstd"])
            kb.op("dve", I("reciprocal", rstd[:, 0:n], rstd[:, 0:n]), r=["rstd"], w=["rstd"])
            kb.op("dve", I("tensor_tensor", out=tnrm[0][:, 0:n], in0=P[7][:, 0:n], in1=rstd[:, 0:n], op=ALU.mult),
                  r=["P7", "rstd"], w=["tnrm0"])
            nsc = 1.0 if ret else na[:, 0:1]
            kb.op("dve", I("scalar_tensor_tensor", out=oT[:, u, t0:t0 + n], in0=tnrm[0][:, 0:n], scalar=nsc, in1=gs[:, 0:n],
                                                                  op0=ALU.mult, op1=ALU.mult), r=["tnrm0", "gs", "na"], w=["oT"])

    kb.barrier()
    m_end = ar.mark()
    ar.reset(m_w)
    wo = A("wo", [128, NCH, 1024], BF16)
    kb.dma("pool", I("dma_start", out=wo[:], in_=woutab_d[:].rearrange("(c p) n -> p c n", p=128)), "ld_wo", w=["wo"])
    npj = 0
    for (t0, n) in cfg.tiles:
        for cc in range(NCH):
            pi = npj % 4
            npj += 1
            pb, pk = P[pi], f"P{pi}"
            for c in range(NCH):
                kb.op("pe", I("matmul", pb[:, 0:n], lhsT=wo[:, c, cc * 128:(cc + 1) * 128], rhs=oT[:, c, t0:t0 + n],
                                                                  start=(c == 0), stop=(c == NCH - 1)), r=["wo", "oT"], w=[pk])
            env["resid_add"](pb, pk, cc, t0, n, 0, 2)
    kb.barrier()
    ar.reset(m0)


def layer1_mixer(cfg, nc, kb, ar, A, P, Pb, env):
    S, NT, NPG = cfg.S, cfg.NT, cfg.NPG
    xT, hT, cf, identb, maskb, neglam, sgn8, wsub = (env[k] for k in ("xT", "hT", "cf", "identb", "maskb", "neglam", "sgn8", "wsub"))
    sqb, rstd, tnrm = env["sqb"], env["rstd"], env["tnrm"]
    d = env
    winc_d, woutc_d, ck_d, cv_d, pt_d = d["winc_d"], d["woutc_d"], d["ck_d"], d["cv_d"], d["pt_d"]
    kp_d, vp_d, ks_d, vs_d = d["kp_d"], d["vp_d"], d["ks_d"], d["vs_d"]
    out_keys = d["out_keys"]
    NVT = S // 128
    NQT = S // 512
    VW = 264

    env["prenorm"](1, 0)
    kb.barrier()
    m0 = ar.mark()
    oTs = A("oTs", [128, NCH, 16], BF16)
    QTs = A("QTs", [128, 8, 16], BF16)
    KTs = A("KTs", [128, 8, 16], BF16)
    Vs = A("Vs", [4, 16, VW], BF16)
    small = A("small", [128, 8], F32)
    ssm = A("ssm", [128, 8], F32)
    oTh = A("oTh", [128, 2, S], BF16)
    wqkv = A("wqkv", [128, NCH, 768], BF16)
    woh = A("woh", [128, 2, 1024], BF16)
    QT = A("QT", [128, 2, S], BF16)
    KT = A("KT", [128, 2, S], BF16)
    Vtok = A("Vtok", [128, NVT, VW], BF16)
    stg = [A(f"stg{i}", [128, 512], F32) for i in range(2)]
    PT = [A(f"PT{i}", [128, 512], BF16) for i in range(2)]
    n1b = A("n1b", [128, 2, 256], F32)
    og = A("og", [128, 256], BF16)
    n1a = rstd[:].rearrange("p (a b) -> p a b", a=2)
    n1 = lambda qs: (n1a if qs < 2 else n1b)[:, qs % 2, :]
    dd = tnrm[0][:, 0:256]
    junk = tnrm[0][:, 256:512]
    NSL = 3
    Kpg = [A(f"Kpg{i}", [128, 1024], BF16) for i in range(NSL)]
    Vpg = [A(f"Vpg{i}", [128, 1024], BF16) for i in range(NSL)]
    KTpg = [A(f"KTpg{i}", [128, 8, 128], BF16) for i in range(2)]
    PTs = [A(f"PTs{i}", [128, 32], BF16) for i in range(2)]
    PTn = A("PTn", [4, 32], BF16)
    ogs = A("ogs", [16, 256], BF16)
    ones_b = A("ones_b", [128, 8], BF16)
    zeros_b = A("zeros_b", [128, 128], BF16)
    ptb = A("ptb", [128, 4 * NPG], I32)
    pid = A("pid", [128, 1], I32)
    pidx = A("pidx", [128, 4 * NPG], I32)
    acc = sqb[0][:, 0:260]
    nsg = sqb[1][:, 0:256]
    junk2 = tnrm[1][0:16, 0:256]
    dsm = tnrm[1][0:16, 256:512]

    kb.op("pool", I("memset", Vtok[:, :, 256:257], 1.0), w=["Vtok"])
    kb.op("pool", I("memset", Vs[:, :, 256:257], 1.0), w=["Vs"])
    kb.op("dve", I("memset", ones_b[:], 1.0), w=["ones_b"])
    kb.op("dve", I("memset", zeros_b[:], 0.0), w=["zeros_b"])
    kb.dma("sp", I("dma_start", out=ptb[:], in_=pt_d[:].partition_broadcast(128)), "ld_pts", w=["ptb"])
    kb.op("pool", I("iota", pid[:], pattern=[[0, 1]], base=0, channel_multiplier=1), w=["pid"])
    kb.op("dve", I("tensor_scalar", out=pidx[:], in0=ptb[:], scalar1=128, scalar2=pid[:, 0:1], op0=ALU.mult, op1=ALU.add),
          r=["ptb", "pid"], w=["pidx"])
    ck_rows = ck_d.rearrange("a p f -> (a p) f")
    cv_rows = cv_d.rearrange("a p f -> (a p) f")

    nproj = [0]

    def pbank():
        pi = 2 + nproj[0] % 2
        nproj[0] += 1
        return P[pi], f"P{pi}"

    nstg = [0]
    tick_hook = [lambda: None]

    def load_head_weights(h, with_out):
        kb.dma("pool", I("dma_start", out=wqkv[:], in_=winc_d[h].rearrange("(c p) n -> p c n", p=128)), "ld_wqkv", w=["wqkv"])
        if with_out:
            kb.dma("pool", I("dma_start", out=woh[:], in_=woutc_d[h * 256:(h + 1) * 256, :].rearrange("(c p) n -> p c n", p=128)),
                   "ld_woh", w=["woh"])

    def project(h, t0, n):
        is_s = t0 >= S
        for j in range(2):
            pb, pk = pbank()
            for c in range(NCH):
                kb.op("pe", I("matmul", pb[:, 0:n], lhsT=wqkv[:, c, j * 128:(j + 1) * 128], rhs=hT[:, c, t0:t0 + n],
                              start=(c == 0), stop=(c == NCH - 1)), r=["wqkv", "hT"], w=[pk])
            if is_s:
                kb.op("act", I("activation", out=QTs[:, h * 2 + j, :], in_=pb[:, 0:16], func=AF.Identity, scale=DK_SCALE), r=[pk], w=["QTs"])
            else:
                kb.op("act", I("activation", out=QT[:, j, t0:t0 + n], in_=pb[:, 0:n], func=AF.Identity, scale=DK_SCALE), r=[pk], w=["QT"])
                tick_hook[0]()
        for j in range(2):
            pb, pk = pbank()
            for c in range(NCH):
                kb.op("pe", I("matmul", pb[:, 0:n], lhsT=wqkv[:, c, 256 + j * 128:256 + (j + 1) * 128], rhs=hT[:, c, t0:t0 + n],
                              start=(c == 0), stop=(c == NCH - 1)), r=["wqkv", "hT"], w=[pk])
            if is_s:
                kb.op("dve", I("tensor_copy", KTs[:, h * 2 + j, :], pb[:, 0:16]), r=[pk], w=["KTs"])
            else:
                kb.op("dve", I("tensor_copy", KT[:, j, t0:t0 + n], pb[:, 0:n]), r=[pk], w=["KT"])
                tick_hook[0]()
        subs = [(S + 4 * i, 4, i) for i in range(4)] if is_s else [(t0 + s * 128, 128, None) for s in range(n // 128)]
        for (c0, rows, seq) in subs:
            pb, pk = pbank()
            for c in range(NCH):
                kb.op("pe", I("matmul", pb[0:rows, 0:512], lhsT=hT[:, c, c0:c0 + rows], rhs=wqkv[:, c, 256:768],
                              start=(c == 0), stop=(c == NCH - 1)), r=["wqkv", "hT"], w=[pk])
            si = nstg[0] % 2
            nstg[0] += 1
            kb.op("act", I("copy", stg[si][0:rows, :], pb[0:rows, 0:512]), r=[pk], w=[f"stg{si}"])
            if seq is None:
                kdst = kp_d[c0:c0 + rows, h * 256:(h + 1) * 256]
                vdst = vp_d[c0:c0 + rows, h * 256:(h + 1) * 256]
                kb.op("dve", I("tensor_copy", Vtok[:, c0 // 128, 0:256], stg[si][:, 256:512]), r=[f"stg{si}"], w=["Vtok"])
            else:
                r0 = c0 - S
                kdst = ks_d[r0:r0 + rows, h * 256:(h + 1) * 256]
                vdst = vs_d[r0:r0 + rows, h * 256:(h + 1) * 256]
                kb.op("dve", I("tensor_copy", Vs[0:4, seq * 4 + h, 0:256], stg[si][0:4, 256:512]), r=[f"stg{si}"], w=["Vs"])
            key = f"kv_{h}_{c0}"
            kb.dma("sp", I("dma_start", out=kdst, in_=stg[si][0:rows, 0:256]), f"st_k{si}", r=[f"stg{si}"], w=[key + "k"])
            kb.dma("sp", I("dma_start", out=vdst, in_=stg[si][0:rows, 256:512]), f"st_v{si}", r=[f"stg{si}"], w=[key + "v"])
            out_keys.extend([key + "k", key + "v"])
            if seq is None:
                tick_hook[0]()

    def sample_gen():
        pages = [(i, p) for i in range(4) for p in range(NPG)]
        NP = len(pages)

        def stageA(n):
            sl, s2 = n % NSL, n % 2
            for hj in range(8):
                kb.op("pe", I("transpose", Pb[0][:, hj * 128:(hj + 1) * 128], Kpg[sl][:, hj * 128:(hj + 1) * 128], identb[:]),
                      r=[f"Kpg{sl}", "identb"], w=["P0"])
            if s2 == 0:
                kb.op("dve", I("tensor_copy", KTpg[s2][:].rearrange("p a b -> p (a b)"), Pb[0][:, 0:1024]), r=["P0"], w=[f"KTpg{s2}"])
            else:
                kb.op("act", I("copy", KTpg[s2][:].rearrange("p a b -> p (a b)"), Pb[0][:, 0:1024]), r=["P0"], w=[f"KTpg{s2}"])

        def stageB(n):
            i, p = pages[n]
            s2 = n % 2
            for hj in range(8):
                kb.op("pe", I("matmul", P[1][:, hj * 4:(hj + 1) * 4], lhsT=KTpg[s2][:, hj, :], rhs=QTs[:, hj, 4 * i:4 * i + 4], start=True, stop=True),
                      r=[f"KTpg{s2}", "QTs"], w=["P1"])
            kb.op("act", I("activation", out=PTs[s2][:], in_=P[1][:, 0:32], func=AF.Exp), r=["P1"], w=[f"PTs{s2}"])

        def stageC(n):
            i, p = pages[n]
            sl, s2 = n % NSL, n % 2
            if p == 0:
                kb.op("pool", I("memset", acc, 0.0), w=["acc"])
            for h in range(4):
                kb.op("pe", I("matmul", P[1][32 * h:32 * h + 8, 64:320], lhsT=PTs[s2][:, h * 8:(h + 1) * 8], rhs=Vpg[sl][:, h * 256:(h + 1) * 256],
                              start=True, stop=False, tile_position=(0, 32 * h)), r=[f"PTs{s2}", f"Vpg{sl}"], w=["P1"])
                kb.op("pe", I("matmul", P[1][32 * h:32 * h + 8, 320:321], lhsT=PTs[s2][:, h * 8:(h + 1) * 8], rhs=ones_b[:, 0:1],
                              start=False, stop=True, tile_position=(0, 32 * h)), r=[f"PTs{s2}", "ones_b"], w=["P1"])
            kb.op("dve", I("tensor_tensor", out=acc[:, 0:257], in0=P[1][:, 64:321], in1=acc[:, 0:257], op=ALU.add), r=["P1", "acc"], w=["acc"])
            if p == NPG - 1:
                finish_seq(i)

        def finish_seq(i):
            for hj in range(8):
                kb.op("pe", I("matmul", P[1][0:4, hj * 4:(hj + 1) * 4], lhsT=KTs[:, hj, 4 * i:4 * i + 4], rhs=QTs[:, hj, 4 * i:4 * i + 4],
                              start=True, stop=True), r=["KTs", "QTs"], w=["P1"])
            kb.op("act", I("activation", out=PTn[:], in_=P[1][0:4, 0:32], func=AF.Exp), r=["P1"], w=["PTn"])
            kb.op("dve", I("tensor_tensor", out=PTn[:].rearrange("p (a b) -> p a b", a=8), in0=PTn[:].rearrange("p (a b) -> p a b", a=8),
                           in1=maskb[0:4, 0:4].unsqueeze(1).to_broadcast([4, 8, 4]), op=ALU.mult), r=["PTn", "maskb"], w=["PTn"])
            for h in range(4):
                kb.op("pe", I("matmul", P[1][32 * h:32 * h + 8, 64:321], lhsT=PTn[0:4, h * 8:(h + 1) * 8], rhs=Vs[0:4, i * 4 + h, 0:257],
                              start=True, stop=True, tile_position=(0, 32 * h)), r=["PTn", "Vs"], w=["P1"])
            kb.op("dve", I("tensor_tensor", out=acc[:, 0:257], in0=P[1][:, 64:321], in1=acc[:, 0:257], op=ALU.add), r=["P1", "acc"], w=["acc"])
            kb.op("dve", I("tensor_scalar", out=ssm[:, 0:1], in0=acc[:, 256:257], scalar1=1e-30, scalar2=None, op0=ALU.add), r=["acc"], w=["ssm0"])
            kb.op("dve", I("reciprocal", ssm[:, 1:2], ssm[:, 0:1]), r=["ssm0"], w=["ssm1"])
            kb.op("dve", I("tensor_tensor", out=ssm[:, 2:3], in0=ssm[:, 1:2], in1=sgn8[:, 0:1], op=ALU.mult), r=["ssm1", "sgn8"], w=["ssm2"])
            kb.op("dve", I("tensor_scalar", out=nsg, in0=acc[:, 0:256], scalar1=ssm[:, 2:3], scalar2=None, op0=ALU.mult), r=["acc", "ssm2"], w=["nsg"])
            kb.op("pe", I("matmul", P[0][0:16, 0:256], lhsT=cf[:, CF_SEL2:CF_SEL2 + 16], rhs=nsg, start=True, stop=True), r=["nsg", "cf"], w=["P0"])
            kb.op("act", I("copy", dsm, P[0][0:16, 0:256]), r=["P0"], w=["dsm"])
            kb.op("act", I("activation", out=junk2, in_=dsm, func=AF.Square, accum_out=ssm[0:16, 3:4]), r=["dsm"], w=["junk2", "ssm3"])
            kb.op("act", I("activation", out=ssm[0:16, 4:5], in_=ssm[0:16, 3:4], func=AF.Sqrt, scale=1.0 / 256, bias=EPS), r=["ssm3"], w=["ssm4"])
            kb.op("dve", I("reciprocal", ssm[0:16, 5:6], ssm[0:16, 4:5]), r=["ssm4"], w=["ssm5"])
            kb.op("dve", I("scalar_tensor_tensor", out=ogs[:], in0=dsm, scalar=ssm[0:16, 5:6], in1=wsub[0:16, :], op0=ALU.mult, op1=ALU.mult),
                  r=["dsm", "ssm5", "wsub"], w=["ogs"])
            for half in range(2):
                kb.op("pe", I("transpose", Pb[0][:, half * 16:(half + 1) * 16], ogs[:, half * 128:(half + 1) * 128], identb[0:16, 0:16]),
                      r=["ogs", "identb"], w=["P0"])
            for half in range(2):
                kb.op("dve", I("tensor_copy", oTs[:].rearrange("p (h two) t -> p two h t", two=2)[:, half, :, 4 * i:4 * i + 4],
                               Pb[0][:, half * 16:(half + 1) * 16].rearrange("p (h q) -> p h q", h=4)), r=["P0"], w=["oTs"])

        for (c0_, c1_) in ((64, 192), (192, 320), (320, 321)):
            kb.op("pe", I("matmul", P[1][:, c0_:c1_], lhsT=zeros_b[:], rhs=identb[:, 0:c1_ - c0_], start=True, stop=True),
                  r=["zeros_b", "identb"], w=["P1"])
        kdma = lambda n: kb.dma("pool", I("indirect_dma_start", out=Kpg[n % NSL][:], out_offset=None, in_=ck_rows,
                                          in_offset=bass.IndirectOffsetOnAxis(ap=pidx[:, pages[n][0] * NPG + pages[n][1]:pages[n][0] * NPG + pages[n][1] + 1], axis=0)),
                                f"ld_kpg{n % NSL}", r=["pidx"], w=[f"Kpg{n % NSL}"])
        vdma = lambda n: kb.dma("pool", I("indirect_dma_start", out=Vpg[n % NSL][:], out_offset=None, in_=cv_rows,
                                          in_offset=bass.IndirectOffsetOnAxis(ap=pidx[:, pages[n][0] * NPG + pages[n][1]:pages[n][0] * NPG + pages[n][1] + 1], axis=0)),
                                f"ld_vpg{n % NSL}", r=["pidx"], w=[f"Vpg{n % NSL}"])
        for n in range(min(2, NP)):
            kdma(n)
        vdma(0)
        for sstep in range(NP + 2):
            if sstep < NP:
                stageA(sstep)
            if 0 <= sstep - 1 < NP:
                stageB(sstep - 1)
            if 0 <= sstep - 2 < NP:
                stageC(sstep - 2)
            if sstep + 2 < NP:
                kdma(sstep + 2)
            if sstep + 1 < NP:
                vdma(sstep + 1)
            yield

    for h in range(4):
        load_head_weights(h, False)
        project(h, S, 16)
    gen = sample_gen()
    n_points = 4 * ((S // 512) * 8 + sum(4 * qt + 4 for qt in range(NQT)) * 2 + (S // 512) * 8)
    n_steps = 4 * NPG + 2
    pace = {"pts": 0, "done": 0}

    def tick():
        pace["pts"] += 1
        want = min(n_steps, (pace["pts"] * n_steps + n_points - 1) // n_points)
        while pace["done"] < want:
            next(gen, None)
            pace["done"] += 1

    tick_hook[0] = tick

    for h in range(4):
        load_head_weights(h, True)
        for (t0, n) in cfg.tiles:
            if t0 < S:
                project(h, t0, n)
        nst = [0]
        for qt in range(NQT):
            for j in range(2):
                nk = 4 * qt + 4
                bufs = {}

                def scores(kbi):
                    n0 = max(0, kbi - 4 * qt)
                    sb = 2 + nst[0] % 2
                    ps = nst[0] % 2
                    nst[0] += 1
                    bufs[kbi] = ps
                    q0 = qt * 512 + n0 * 128
                    kb.op("pe", I("matmul", P[sb][:, n0 * 128:512], lhsT=KT[:, j, kbi * 128:(kbi + 1) * 128], rhs=QT[:, j, q0:(qt + 1) * 512],
                                  start=True, stop=True), r=["KT", "QT"], w=[f"P{sb}"])
                    kb.op("act", I("activation", out=PT[ps][:, n0 * 128:512], in_=P[sb][:, n0 * 128:512], func=AF.Exp),
                          r=[f"P{sb}"], w=[f"PT{ps}"])
                    if kbi >= 4 * qt:
                        kb.op("dve", I("tensor_tensor", out=PT[ps][:, n0 * 128:(n0 + 1) * 128], in0=PT[ps][:, n0 * 128:(n0 + 1) * 128],
                                       in1=maskb[:], op=ALU.mult), r=[f"PT{ps}", "maskb"], w=[f"PT{ps}"])

                scores(0)
                for kbi in range(nk):
                    n0 = max(0, kbi - 4 * qt)
                    if kbi + 1 < nk:
                        scores(kbi + 1)
                    ps = bufs[kbi]
                    for qs in range(n0, 4):
                        kb.op("pe", I("matmul", P[4 + qs][:, 0:257], lhsT=PT[ps][:, qs * 128:(qs + 1) * 128], rhs=Vtok[:, kbi, 0:257],
                                      start=(kbi == 0), stop=(kbi == 4 * qt + qs)), r=[f"PT{ps}", "Vtok"], w=[f"P{4 + qs}"])
                    tick()
                for qs in range(4):
                    ob, ok = P[4 + qs], f"P{4 + qs}"
                    tok0 = qt * 512 + qs * 128
                    if j == 0:
                        kb.op("dve", I("reciprocal", small[:, 0:1], ob[:, 256:257]), r=[ok], w=["sm0"])
                        kb.op("act", I("activation", out=n1(qs), in_=ob[:, 0:256], func=AF.Identity, scale=small[:, 0:1]),
                              r=[ok, "sm0"], w=[f"n1{qs}"])
                    else:
                        kb.op("dve", I("reciprocal", small[:, 1:2], ob[:, 256:257]), r=[ok], w=["sm1"])
                        kb.op("dve", I("tensor_tensor", out=small[:, 2:3], in0=small[:, 1:2], in1=neglam[:, 0:1], op=ALU.mult),
                              r=["sm1", "neglam"], w=["sm2"])
                        kb.op("dve", I("scalar_tensor_tensor", out=dd, in0=ob[:, 0:256], scalar=small[:, 2:3], in1=n1(qs),
                                       op0=ALU.mult, op1=ALU.add), r=[ok, "sm2", f"n1{qs}"], w=["dd"])
                        kb.op("act", I("activation", out=junk, in_=dd, func=AF.Square, accum_out=small[:, 3:4]), r=["dd"], w=["junk1", "sm3"])
                        kb.op("act", I("activation", out=small[:, 4:5], in_=small[:, 3:4], func=AF.Sqrt, scale=1.0 / 256, bias=EPS),
                              r=["sm3"], w=["sm4"])
                        kb.op("dve", I("reciprocal", small[:, 5:6], small[:, 4:5]), r=["sm4"], w=["sm5"])
                        kb.op("dve", I("scalar_tensor_tensor", out=og[:], in0=dd, scalar=small[:, 5:6], in1=wsub[:], op0=ALU.mult, op1=ALU.mult),
                              r=["dd", "sm5", "wsub"], w=["og"])
                        pb, pk = pbank()
                        pbb = pb[:].bitcast(BF16)
                        for half in range(2):
                            kb.op("pe", I("transpose", pbb[:, half * 128:(half + 1) * 128], og[:, half * 128:(half + 1) * 128], identb[:]),
                                  r=["og", "identb"], w=[pk])
                        kb.op("act", I("copy", oTh[:, :, tok0:tok0 + 128], pbb[:, 0:256].rearrange("p (c t) -> p c t", c=2)), r=[pk], w=["oTh"])
        for (t0, n) in cfg.tiles:
            if t0 >= S:
                continue
            for cc in range(NCH):
                pb, pk = pbank()
                for c2 in range(2):
                    kb.op("pe", I("matmul", pb[:, 0:n], lhsT=woh[:, c2, cc * 128:(cc + 1) * 128], rhs=oTh[:, c2, t0:t0 + n],
                                  start=(c2 == 0), stop=(c2 == 1)), r=["woh", "oTh"], w=[pk])
                env["resid_add"](pb, pk, cc, t0, n, 1, 2)
                tick()
    for _ in gen:
        pass

    for h in range(4):
        kb.dma("pool", I("dma_start", out=woh[:], in_=woutc_d[h * 256:(h + 1) * 256, :].rearrange("(c p) n -> p c n", p=128)),
               "ld_woh", w=["woh"])
        for cc in range(NCH):
            pb, pk = pbank()
            for c2 in range(2):
                kb.op("pe", I("matmul", pb[:, 0:16], lhsT=woh[:, c2, cc * 128:(cc + 1) * 128], rhs=oTs[:, 2 * h + c2, :],
                              start=(c2 == 0), stop=(c2 == 1)), r=["woh", "oTs"], w=[pk])
            env["resid_add"](pb, pk, cc, S, 16, 1, 2)
    kb.barrier()
    ar.reset(m0)
```

```python
import math
import numpy as np
import concourse.bass as bass
import concourse.mybir as mybir
from concourse.bass_utils import run_bass_kernel_spmd

F32 = mybir.dt.float32
BF16 = mybir.dt.bfloat16
I32 = mybir.dt.int32
AF = mybir.ActivationFunctionType
ALU = mybir.AluOpType

ENGS = ("pe", "act", "dve", "pool", "sp")
D = 1024
NCH = 8
DFF = 4096
EPS = 1e-6
LAM_INIT = 0.8 - 0.6 * math.exp(-0.3 * 1)
DK_SCALE = 128.0 ** -0.5


class Cfg:
    def __init__(self, S=2048, NPG=64, NPHYS=2560, n_cores=8, debug=False):
        self.S = S
        self.NPG = NPG
        self.NPHYS = NPHYS
        self.n_cores = n_cores
        self.NT = S + 16
        self.POS0 = NPG * 128
        self.debug = debug
        self.tiles = [(t0, 512) for t0 in range(0, S, 512)] + [(S, 16)]


class KB:
    def __init__(self, nc):
        self.nc = nc
        self.q = {e: [] for e in ENGS}
        self.sems = {e: nc.alloc_semaphore("sem_" + e) for e in ENGS}
        self.cnt = {e: 0 for e in ENGS}
        self.seen = {e: {} for e in ENGS}
        self.lastw = {}
        self.readers = {}

    def dsem(self, name):
        if name not in self.sems:
            self.sems[name] = self.nc.alloc_semaphore("dsem_" + name)
            self.cnt[name] = 0
        return name

    def _deps(self, eng, r, w, skip_sem=None):
        deps = {}

        def need(x):
            if x is None:
                return
            sk, v = x
            if v > deps.get(sk, 0):
                deps[sk] = v
        for b in r:
            need(self.lastw.get(b))
        for b in w:
            need(self.lastw.get(b))
            for rd in self.readers.get(b, ()):
                need(rd)
        waits = []
        for sk, v in deps.items():
            if sk == skip_sem:
                continue
            if self.seen[eng].get(sk, 0) >= v:
                continue
            self.seen[eng][sk] = v
            waits.append((sk, v))
        return waits

    def _commit(self, me, r, w):
        for b in w:
            self.lastw[b] = me
            self.readers[b] = []
        for b in r:
            self.readers.setdefault(b, []).append(me)

    @staticmethod
    def _is_psum(k):
        return len(k) == 2 and k[0] == "P" and k[1].isdigit()

    def op(self, eng, fn, r=(), w=()):
        w = tuple(w) + tuple(k for k in r if self._is_psum(k))
        r = tuple(k for k in r if not self._is_psum(k))
        waits = self._deps(eng, r, w, skip_sem="pe" if eng == "pe" else None)
        self.cnt[eng] += 1
        me = (eng, self.cnt[eng])
        self.q[eng].append((waits, fn, eng, 1))
        self._commit(me, r, w)

    def dma(self, eng, fn, sem, r=(), w=(), group=False):
        r = tuple(r); w = tuple(w)
        self.dsem(sem)
        waits = self._deps(eng, r, w, skip_sem=sem if group else None)
        self.cnt[sem] += 16
        me = (sem, self.cnt[sem])
        self.q[eng].append((waits, fn, sem, 16))
        self._commit(me, r, w)

    def barrier(self):
        for e in ENGS:
            waits = []
            for sk, v in self.cnt.items():
                if v > self.seen[e].get(sk, 0):
                    self.seen[e][sk] = v
                    waits.append((sk, v))
            if waits:
                self.q[e].append((waits, None, None, 0))

    def emit(self):
        nc = self.nc
        with nc.Block() as block:
            def mk(ename):
                def body(e):
                    for waits, fn, incsem, incv in self.q[ename]:
                        for sk, v in waits:
                            e.wait_ge(self.sems[sk], v)
                        if fn is not None:
                            fn(e).then_inc(self.sems[incsem], incv)
                return body
            block.tensor(mk("pe"))
            block.scalar(mk("act"))
            block.vector(mk("dve"))
            block.gpsimd(mk("pool"))
            block.sync(mk("sp"))


def I(name, *a, **k):
    return lambda e: getattr(e, name)(*a, **k)


class Arena:
    def __init__(self, nc):
        self.nc = nc
        self.off = (int(nc.sbuf_base) + 63) // 64 * 64
        self.limit = int(nc.sbuf_top)
        self.n = 0

    def alloc(self, name, shape, dtype):
        esz = 2 if dtype == BF16 else 4
        nbytes = int(np.prod(shape[1:])) * esz
        self.off = (self.off + 31) // 32 * 32
        assert self.off + nbytes <= self.limit, f"SBUF overflow allocating {name}: {self.off}+{nbytes}>{self.limit}"
        self.n += 1
        t = self.nc.alloc_sbuf_tensor_at(f"{name}_{self.n}", list(shape), dtype, offset=self.off)
        self.off += nbytes
        return t

    def mark(self):
        return self.off

    def reset(self, m):
        self.off = m


CF_IDENT, CF_MASK, CF_M0, CF_M1, CF_SEL2 = 0, 128, 256, 257, 258
NCF = 274
CF_DT, CF_G1, CF_KD128, CF_KD4 = 0, 512, 1024, 1028
NCFB = 1032


def make_consts(cfg):
    cf = np.zeros((128, NCF), np.float64)
    cfb = np.zeros((128, NCFB), np.float64)
    cf[:, CF_IDENT:CF_IDENT + 128] = np.eye(128)
    s = np.arange(128)[:, None]
    t = np.arange(128)[None, :]
    cf[:, CF_MASK:CF_MASK + 128] = (s <= t)
    for h in range(4):
        g = 1.0 - 2.0 ** (-5.0 - h)
        cfb[:, CF_DT + h * 128:CF_DT + (h + 1) * 128] = np.where(t >= s, g ** np.maximum(t - s, 0), 0.0)
        cfb[:, CF_G1 + h * 128:CF_G1 + (h + 1) * 128] = g ** (t + 1.0)
        cfb[:, CF_KD128 + h] = g ** (127.0 - s[:, 0])
        cfb[:4, CF_KD4 + h] = g ** (3.0 - s[:4, 0])
        cf[32 * h:32 * h + 4, CF_M0] = 1.0
        cf[32 * h + 4:32 * h + 8, CF_M1] = 1.0
        for j in range(2):
            for q in range(4):
                cf[32 * h + j * 4 + q, CF_SEL2 + h * 4 + q] = 1.0
    d = 128
    inv = 1.0 / (10000.0 ** np.linspace(0.0, 1.0, d // 2, dtype=np.float32).astype(np.float64))
    pos = np.concatenate([np.arange(cfg.S), np.tile(cfg.POS0 + np.arange(4), 4)]).astype(np.float64)
    ang = np.repeat(inv, 2)[:, None] * pos[None, :]
    rope = np.stack([np.cos(ang), np.sin(ang)]).astype(np.float32)
    return cf.astype(np.float32), cfb.astype(np.float32), rope


def build_program(cfg):
    nc = bass.Bass("TRN2", target_bir_lowering=False)
    S, NT, NPG = cfg.S, cfg.NT, cfg.NPG
    kb = KB(nc)

    def din(name, shape, dt=F32):
        return nc.dram_tensor(name, list(shape), dt, kind="ExternalInput").ap()

    def dout(name, shape, dt=F32):
        return nc.dram_tensor(name, list(shape), dt, kind="ExternalOutput").ap()

    xp_d = din("xp", [S, D]); xs_d = din("xs", [16, D])
    sth_d = din("st_h", [4, 4, 128, 128]); str_d = din("st_r", [4, 4, 128, 128])
    ck_d = din("ck", [cfg.NPHYS, 128, 1024]); cv_d = din("cv", [cfg.NPHYS, 128, 1024])
    pt_d = din("pt", [1, 4 * NPG], I32)
    cvec_d = din("cvec", [5, D])
    wada_d = din("w_ada", [2, D, 6 * D]); bada_d = din("b_ada", [96, 128])
    normw_d = din("norm_w", [32, 128])
    winab_d = din("w_in_ab", [8, D, 512]); woutab_d = din("w_out_ab", [D, D])
    lbl_d = din("lb_logits", [8, 128]); hnw_d = din("hgrn_norm_w", [1, 128])
    winc_d = din("w_in_c", [4, D, 768]); woutc_d = din("w_out_c", [D, D])
    dlam_d = din("diff_lambda", [1, 512]); subln_d = din("subln_w", [1, 256])
    wup_d = din("w_up", [2, D, DFF]); wdn_d = din("w_down", [2, DFF, D])
    fnw_d = din("final_norm_w", [1, D])
    cf_d = din("cf", [128, NCF]); cfb_d = din("cfb", [128, NCFB]); rope_d = din("rope", [2, 128, NT])

    yp_d = dout("y_p", [S, D]); ys_d = dout("y_s", [16, D])
    hgp_d = dout("hg_p", [4, 128, 128]); rtp_d = dout("rt_p", [4, 128, 128])
    kp_d = dout("k_p", [S, 1024]); vp_d = dout("v_p", [S, 1024])
    hgs_d = dout("hg_s", [4, 4, 128, 128]); rts_d = dout("rt_s", [4, 4, 128, 128])
    ks_d = dout("k_s", [16, 1024]); vs_d = dout("v_s", [16, 1024])
    out_keys = []

    ar = Arena(nc)
    A = ar.alloc
    xT = A("xT", [128, NCH, NT], F32)
    hT = A("hT", [128, NCH, NT], BF16)
    cf = A("cf", [128, NCF], F32)
    identb = A("identb", [128, 128], BF16)
    maskb = A("maskb", [128, 128], BF16)
    ones_f = A("ones_f", [128, 128], F32)
    zeros_f = A("zeros_f", [128, 64], F32)
    modT = A("modT", [128, 2 * 48 * 5], F32)
    wmT = A("wmT", [128, 4 * 8 * 5], F32)
    nwT = A("nwT", [128, 32], F32)
    bT = A("bT", [128, 96], F32)
    scT = A("scT", [128, NCH, 5], BF16)
    lbv = A("lbv", [128, 12], F32)
    na = A("na", [128, 1], F32)
    neglam = A("neglam", [128, 1], F32)
    sgn8 = A("sgn8", [128, 1], F32)
    wsub = A("wsub", [128, 256], F32)
    ident = cf[:, CF_IDENT:CF_IDENT + 128]
    maskf = cf[:, CF_MASK:CF_MASK + 128]
    sqb = [A(f"sqb{i}", [128, 512], F32) for i in range(2)]
    rstd = A("rstd", [128, 512], F32)
    tnrm = [A(f"tnrm{i}", [128, 512], F32) for i in range(2)]
    phase_base = ar.mark()

    P = [nc.alloc_psum_tensor(f"P{i}", [128, 512], F32) for i in range(8)]
    Pb = [p[:].bitcast(BF16) for p in P]

    def mod_ap(l, j, c, b0, nb=1):
        o = ((l * 6 + j) * 8 + c) * 5 + b0
        return modT[:, o:o + nb]

    def wm_ap(l, i, c, b0):
        o = ((l * 2 + i) * 8 + c) * 5 + b0
        return wmT[:, o:o + 1]

    def bsegs(t0, n):
        if t0 < S:
            return [(t0, n, 0)]
        return [(S + 4 * i, 4, 1 + i) for i in range(4)]

    setup_m = ar.mark()
    c5 = A("c5", [5, D], F32)
    s5 = A("s5", [5, D], F32)
    ldrow = A("ldrow", [128, 128], F32)
    lam_t = A("lam_t", [1, 520], F32)
    wada_sl = [A(f"wada{i}", [128, NCH, 512], BF16) for i in range(2)]

    kb.dma("sp", I("dma_start", out=cf[:], in_=cf_d[:]), "ld_cf", w=["cf"])
    kb.dma("sp", I("dma_start", out=c5[:], in_=cvec_d[:]), "ld_c5", w=["c5"])
    kb.op("dve", I("memset", ones_f[:], 1.0), w=["ones_f"])
    kb.op("dve", I("memset", zeros_f[:], 0.0), w=["zeros_f"])
    kb.op("dve", I("tensor_copy", identb[:], ident), r=["cf"], w=["identb"])
    kb.op("dve", I("tensor_copy", maskb[:], maskf), r=["cf"], w=["maskb"])

    kb.op("act", I("activation", out=s5[:], in_=c5[:], func=AF.Silu), r=["c5"], w=["s5"])
    for c in range(NCH):
        kb.op("pe", I("transpose", P[0][:, c * 5:(c + 1) * 5], s5[:, c * 128:(c + 1) * 128], cf[0:5, 0:5]),
              r=["s5", "cf"], w=["P0"])
    kb.op("dve", I("tensor_copy", scT[:].rearrange("p c b -> p (c b)"), P[0][:, 0:40]), r=["P0"], w=["scT"])

    def load_cols(src_ap, nrows, dst_ap, key, tag):
        kb.dma("sp", I("dma_start", out=ldrow[0:nrows, :], in_=src_ap), "ld_row", w=["ldrow"])
        kb.op("pe", I("transpose", P[1][:, 0:nrows], ldrow[0:nrows, :], cf[0:nrows, 0:nrows]),
              r=["ldrow", "cf"], w=["P1"])
        kb.op("dve", I("tensor_copy", dst_ap, P[1][:, 0:nrows]), r=["P1"], w=[key])

    load_cols(bada_d[:], 96, bT[:], "bT", "b")
    load_cols(normw_d[:], 32, nwT[:], "nwT", "n")
    lbt = A("lbt", [128, 8], F32)
    load_cols(lbl_d[:], 8, lbt[:], "lbt", "l")
    load_cols(hnw_d[:], 1, na[:], "na", "h")
    kb.op("dve", I("tensor_tensor", out=lbt[:, 0:4], in0=lbt[:, 0:4], in1=lbt[:, 4:8], op=ALU.subtract),
          r=["lbt"], w=["lbt"])
    kb.op("act", I("activation", out=lbv[:, 0:4], in_=lbt[:, 0:4], func=AF.Sigmoid), r=["lbt"], w=["lbv"])
    kb.op("dve", I("tensor_scalar", out=lbv[:, 4:8], in0=lbv[:, 0:4], scalar1=-1.0, scalar2=1.0,
                                           op0=ALU.mult, op1=ALU.add), r=["lbv"], w=["lbv"])
    kb.op("dve", I("tensor_scalar", out=lbv[:, 8:12], in0=lbv[:, 0:4], scalar1=-1.0, scalar2=None,
                                           op0=ALU.add), r=["lbv"], w=["lbv"])

    kb.dma("sp", I("dma_start", out=lam_t[:, 0:512], in_=dlam_d[:]), "ld_lam", w=["lam_t"])
    kb.op("dve", I("tensor_tensor", out=lam_t[:, 0:128], in0=lam_t[:, 0:128], in1=lam_t[:, 128:256], op=ALU.mult),
          r=["lam_t"], w=["lam_t"])
    kb.op("dve", I("tensor_tensor", out=lam_t[:, 256:384], in0=lam_t[:, 256:384], in1=lam_t[:, 384:512], op=ALU.mult),
          r=["lam_t"], w=["lam_t"])
    kb.op("dve", I("reduce_sum", out=lam_t[:, 512:513], in_=lam_t[:, 0:128], axis=mybir.AxisListType.X),
          r=["lam_t"], w=["lam_t"])
    kb.op("dve", I("reduce_sum", out=lam_t[:, 513:514], in_=lam_t[:, 256:384], axis=mybir.AxisListType.X),
          r=["lam_t"], w=["lam_t"])
    kb.op("act", I("activation", out=lam_t[:, 514:516], in_=lam_t[:, 512:514], func=AF.Exp), r=["lam_t"], w=["lam_t"])
    kb.op("dve", I("tensor_tensor", out=lam_t[:, 516:517], in0=lam_t[:, 515:516], in1=lam_t[:, 514:515], op=ALU.subtract),
          r=["lam_t"], w=["lam_t"])
    kb.op("dve", I("tensor_scalar", out=lam_t[:, 518:519], in0=lam_t[:, 516:517], scalar1=-LAM_INIT,
                                           scalar2=None, op0=ALU.add), r=["lam_t"], w=["lam_t"])
    kb.op("pe", I("matmul", P[1][:, 0:1], lhsT=ones_f[0:1, :], rhs=lam_t[:, 518:519], start=True, stop=True),
          r=["lam_t", "ones_f"], w=["P1"])
    kb.op("dve", I("tensor_copy", neglam[:], P[1][:, 0:1]), r=["P1"], w=["neglam"])
    kb.op("dve", I("scalar_tensor_tensor", out=sgn8[:, :], in0=cf[:, CF_M1:CF_M1 + 1], scalar=neglam[:, 0:1],
                                                  in1=cf[:, CF_M0:CF_M0 + 1], op0=ALU.mult, op1=ALU.add),
          r=["neglam", "cf"], w=["sgn8"])
    kb.dma("sp", I("dma_start", out=wsub[:], in_=subln_d[:].partition_broadcast(128)), "ld_bc", w=["wsub"])
    kb.op("dve", I("tensor_scalar", out=wsub[:], in0=wsub[:], scalar1=1.0 - LAM_INIT, scalar2=None, op0=ALU.mult),
          r=["wsub"], w=["wsub"])

    for l in range(2):
        for g in range(12):
            sl = (l * 12 + g) % 2
            kb.dma("pool", I("dma_start",
                out=wada_sl[sl][:], in_=wada_d[l, :, g * 512:(g + 1) * 512].rearrange("(c p) n -> p c n", p=128)),
                f"ld_wada{sl}", w=[f"wada{sl}"])
            pb = P[2 + sl]
            for blk in range(4):
                for c in range(NCH):
                    kb.op("pe", I("matmul",
                        pb[:, blk * 5:(blk + 1) * 5], lhsT=wada_sl[sl][:, c, blk * 128:(blk + 1) * 128], rhs=scT[:, c, :],
                        start=(c == 0), stop=(c == NCH - 1)), r=[f"wada{sl}", "scT"], w=[f"P{2 + sl}"])
            for blk in range(4):
                bi = l * 48 + g * 4 + blk
                kb.op("dve", I("tensor_scalar",
                    out=modT[:, bi * 5:(bi + 1) * 5], in0=pb[:, blk * 5:(blk + 1) * 5], scalar1=bT[:, bi:bi + 1], scalar2=None,
                    op0=ALU.add), r=[f"P{2 + sl}", "bT"], w=["modT"])
    for l in range(2):
        for i in range(2):
            for c in range(NCH):
                o = ((l * 2 + i) * 8 + c)
                kb.op("dve", I("tensor_scalar",
                    out=wmT[:, o * 5:(o + 1) * 5], in0=mod_ap(l, 1 + 3 * i, c, 0, 5), scalar1=1.0, scalar2=nwT[:, o:o + 1],
                    op0=ALU.add, op1=ALU.mult), r=["modT", "nwT"], w=["wmT"])

    xld = [A(f"xld{i}", [128, D], F32) for i in range(2)]
    n_xt = S // 128
    for it in range(n_xt + 1):
        sl = it % 2
        rows = 128 if it < n_xt else 16
        src = xp_d[it * 128:(it + 1) * 128, :] if it < n_xt else xs_d[:]
        kb.dma("sp", I("dma_start", out=xld[sl][0:rows, :], in_=src),
               f"ld_x{sl}", w=[f"xld{sl}"])
        for half in range(2):
            pb = P[4 + 2 * sl + half]
            pk = f"P{4 + 2 * sl + half}"
            for cc in range(4):
                c = half * 4 + cc
                kb.op("pe", I("transpose",
                    pb[:, cc * 128:cc * 128 + rows], xld[sl][0:rows, c * 128:(c + 1) * 128], cf[0:rows, 0:rows]),
                    r=[f"xld{sl}", "cf"], w=[pk])
            t0 = it * 128
            eng = "act" if half == 0 else "dve"
            if eng == "act":
                kb.op("act", I("copy",
                    xT[:, half * 4:(half + 1) * 4, t0:t0 + rows],
                    pb[:].rearrange("p (c t) -> p c t", c=4)[:, :, 0:rows]), r=[pk], w=["xT"])
            else:
                kb.op("dve", I("tensor_copy",
                    xT[:, half * 4:(half + 1) * 4, t0:t0 + rows],
                    pb[:].rearrange("p (c t) -> p c t", c=4)[:, :, 0:rows]), r=[pk], w=["xT"])
    kb.barrier()
    ar.reset(setup_m)

    def prenorm(l, i):
        jsh = 3 * i
        for ti, (t0, n) in enumerate(cfg.tiles):
            pk = f"P{ti % 2}"
            pb = P[ti % 2]
            for c in range(NCH):
                sq = sqb[c % 2]
                kb.op("act", I("activation", out=sq[:, 0:n], in_=xT[:, c, t0:t0 + n], func=AF.Square),
                      r=["xT"], w=[f"sqb{c % 2}"])
                kb.op("pe", I("matmul", pb[:, 0:n], lhsT=ones_f[:], rhs=sq[:, 0:n],
                                                                      start=(c == 0), stop=(c == NCH - 1)),
                      r=[f"sqb{c % 2}", "ones_f"], w=[pk])
            kb.op("act", I("activation", out=rstd[:, 0:n], in_=pb[:, 0:n], func=AF.Sqrt, scale=1.0 / D, bias=EPS),
                  r=[pk], w=["rstd"])
            kb.op("dve", I("reciprocal", rstd[:, 0:n], rstd[:, 0:n]), r=["rstd"], w=["rstd"])
            for c in range(NCH):
                tn = tnrm[c % 2]
                kb.op("dve", I("tensor_tensor", out=tn[:, 0:n], in0=xT[:, c, t0:t0 + n], in1=rstd[:, 0:n],
                                                                              op=ALU.mult), r=["xT", "rstd"], w=[f"tnrm{c % 2}"])
                for (s0, sn, b) in bsegs(t0, n):
                    kb.op("act", I("activation",
                        out=hT[:, c, s0:s0 + sn], in_=tn[:, s0 - t0:s0 - t0 + sn], func=AF.Identity,
                        scale=wm_ap(l, i, c, b), bias=mod_ap(l, jsh, c, b)),
                        r=[f"tnrm{c % 2}", "wmT", "modT"], w=["hT"])

    def resid_add(pb, pk, cc, t0, n, l, jg):
        for (s0, sn, b) in bsegs(t0, n):
            kb.op("dve", I("scalar_tensor_tensor",
                out=xT[:, cc, s0:s0 + sn], in0=pb[:, s0 - t0:s0 - t0 + sn], scalar=mod_ap(l, jg, cc, b),
                in1=xT[:, cc, s0:s0 + sn], op0=ALU.mult, op1=ALU.add), r=[pk, "modT", "xT"], w=["xT"])

    def mlp(l):
        m = ar.mark()
        wup = [A(f"wup{i}", [128, NCH, 1024], BF16) for i in range(2)]
        wdn = [A(f"wdn{i}", [128, NCH, 1024], BF16) for i in range(2)]
        uT = [A(f"uT{i}", [128, NCH, 512], BF16) for i in range(2)]
        rl = [A(f"rl{i}", [128, 512], BF16) for i in range(2)]
        prenorm(l, 1)
        it = 0
        nup = 0
        ndn = 0
        for g in range(4):
            sl = g % 2
            kb.dma("pool", I("dma_start",
                out=wup[sl][:], in_=wup_d[l, :, g * 1024:(g + 1) * 1024].rearrange("(c p) n -> p c n", p=128)),
                f"ld_wup{sl}", w=[f"wup{sl}"])
            kb.dma("pool", I("dma_start",
                out=wdn[sl][:], in_=wdn_d[l, g * 1024:(g + 1) * 1024, :].rearrange("(f p) n -> p f n", p=128)),
                f"ld_wdn{sl}", w=[f"wdn{sl}"])
            for (t0, n) in cfg.tiles:
                us = it % 2
                it += 1
                for f in range(NCH):
                    pi = nup % 4
                    nup += 1
                    pb, pk = P[pi], f"P{pi}"
                    for c in range(NCH):
                        kb.op("pe", I("matmul",
                            pb[:, 0:n], lhsT=wup[sl][:, c, f * 128:(f + 1) * 128], rhs=hT[:, c, t0:t0 + n],
                            start=(c == 0), stop=(c == NCH - 1)), r=[f"wup{sl}", "hT"], w=[pk])
                    rs = f % 2
                    kb.op("act", I("activation", out=rl[rs][:, 0:n], in_=pb[:, 0:n], func=AF.Relu),
                          r=[pk], w=[f"rl{rs}"])
                    kb.op("dve", I("tensor_tensor",
                        out=uT[us][:, f, 0:n], in0=pb[:, 0:n], in1=rl[rs][:, 0:n], op=ALU.mult),
                        r=[pk, f"rl{rs}"], w=[f"uT{us}"])
                for cc in range(NCH):
                    pi = 4 + ndn % 4
                    ndn += 1
                    pb, pk = P[pi], f"P{pi}"
                    for f in range(NCH):
                        kb.op("pe", I("matmul",
                            pb[:, 0:n], lhsT=wdn[sl][:, f, cc * 128:(cc + 1) * 128], rhs=uT[us][:, f, 0:n],
                            start=(f == 0), stop=(f == NCH - 1)), r=[f"wdn{sl}", f"uT{us}"], w=[pk])
                    resid_add(pb, pk, cc, t0, n, l, 5)
        kb.barrier()
        ar.reset(m)

    def final_out():
        m = ar.mark()
        ybuf = [A(f"ybuf{i}", [128, D], F32) for i in range(2)]
        finw = A("finw", [128, D], F32)
        kb.dma("sp", I("dma_start", out=finw[:], in_=fnw_d[:].partition_broadcast(128)), "ld_bc2", w=["finw"])
        ssq = A("ssq", [128, 8], F32)
        junk = A("junk", [128, 512], F32)
        n_t = S // 128
        for it in range(n_t + 1):
            sl = it % 2
            rows = 128 if it < n_t else 16
            t0 = it * 128
            for half in range(2):
                pi = 2 * sl + half
                pb, pk = P[pi], f"P{pi}"
                for cc in range(4):
                    c = half * 4 + cc
                    kb.op("pe", I("transpose",
                        pb[0:rows, cc * 128:(cc + 1) * 128], xT[:, c, t0:t0 + rows], ident), r=["xT", "cf"], w=[pk])
                kb.op("act", I("activation",
                    out=junk[0:rows, :], in_=pb[0:rows, :], func=AF.Square, accum_out=ssq[0:rows, 2 * sl + half:2 * sl + half + 1]),
                    r=[pk], w=["junk", f"ssq{sl}{half}"])
            kb.op("dve", I("tensor_tensor",
                out=ssq[0:rows, 4 + sl:5 + sl], in0=ssq[0:rows, 2 * sl:2 * sl + 1], in1=ssq[0:rows, 2 * sl + 1:2 * sl + 2], op=ALU.add),
                r=[f"ssq{sl}0", f"ssq{sl}1"], w=[f"ssqs{sl}"])
            kb.op("act", I("activation",
                out=ssq[0:rows, 6 + sl:7 + sl], in_=ssq[0:rows, 4 + sl:5 + sl], func=AF.Sqrt, scale=1.0 / D, bias=EPS),
                r=[f"ssqs{sl}"], w=[f"ssqr{sl}"])
            kb.op("dve", I("reciprocal", ssq[0:rows, 6 + sl:7 + sl], ssq[0:rows, 6 + sl:7 + sl]),
                  r=[f"ssqr{sl}"], w=[f"ssqr{sl}"])
            for half in range(2):
                pi = 2 * sl + half
                pb, pk = P[pi], f"P{pi}"
                kb.op("dve", I("scalar_tensor_tensor",
                    out=ybuf[sl][0:rows, half * 512:(half + 1) * 512], in0=pb[0:rows, :], scalar=ssq[0:rows, 6 + sl:7 + sl],
                    in1=finw[0:rows, half * 512:(half + 1) * 512], op0=ALU.mult, op1=ALU.mult),
                    r=[pk, f"ssqr{sl}", "finw"], w=[f"ybuf{sl}"])
            dst = yp_d[t0:t0 + 128, :] if it < n_t else ys_d[:]
            key = f"y{it}"
            kb.dma("sp", I("dma_start", out=dst, in_=ybuf[sl][0:rows, :]),
                   f"st_y{sl}", r=[f"ybuf{sl}"], w=[key])
            out_keys.append(key)
        ar.reset(m)

    from_layers(cfg, nc, kb, ar, A, P, Pb, locals())
    return nc


def from_layers(cfg, nc, kb, ar, A, P, Pb, env):
    stages = getattr(cfg, "stages", ("mix0", "mlp0", "mix1", "mlp1"))
    if "mix0" in stages:
        layer0_mixer(cfg, nc, kb, ar, A, P, Pb, env)
    if "mlp0" in stages:
        env["mlp"](0)
    if "mix1" in stages:
        layer1_mixer(cfg, nc, kb, ar, A, P, Pb, env)
    if "mlp1" in stages:
        env["mlp"](1)
    env["final_out"]()
    waits = kb._deps("sp", tuple(env["out_keys"]), ())
    kb.q["sp"].append((waits, None, None, 0))
    kb.emit()


def prep_shared(cfg, inp):
    f = lambda a: np.ascontiguousarray(np.asarray(a))
    w4 = np.asarray(inp["w_in_ab"])[0].reshape(D, 8, 4, 128)
    units = []
    for u in range(4):
        units.append(np.concatenate([w4[:, 0, u], w4[:, 1, u], w4[:, 3, u], w4[:, 2, u]], axis=1))
    for u in range(4):
        units.append(np.concatenate([w4[:, 4, u], w4[:, 5, u], w4[:, 7, u], w4[:, 6, u]], axis=1))
    wc = np.asarray(inp["w_in_c"])[0]
    heads = []
    for h in range(4):
        heads.append(np.concatenate([wc[:, h * 256:(h + 1) * 256], wc[:, 1024 + h * 256:1024 + (h + 1) * 256],
                                     wc[:, 2048 + h * 256:2048 + (h + 1) * 256]], axis=1))
    cf, cfb, rope = make_consts(cfg)
    return {
        "ck": f(np.asarray(inp["cache_k"])[0].reshape(cfg.NPHYS, 128, 1024)),
        "cv": f(np.asarray(inp["cache_v"])[0].reshape(cfg.NPHYS, 128, 1024)),
        "w_ada": f(inp["w_ada"]), "b_ada": f(np.asarray(inp["b_ada"]).reshape(96, 128)),
        "norm_w": f(np.asarray(inp["norm_w"]).reshape(32, 128)),
        "w_in_ab": f(np.stack(units)), "w_out_ab": f(np.asarray(inp["w_out_ab"])[0]),
        "lb_logits": f(np.asarray(inp["hgrn_lb_logits"]).reshape(8, 128)),
        "hgrn_norm_w": f(np.asarray(inp["hgrn_norm_w"]).reshape(1, 128)),
        "w_in_c": f(np.stack(heads)), "w_out_c": f(np.asarray(inp["w_out_c"])[0]),
        "diff_lambda": f(np.asarray(inp["diff_lambda"]).reshape(1, 512)),
        "subln_w": f(np.asarray(inp["diff_subln_w"]).reshape(1, 256)),
        "w_up": f(inp["w_mlp_up"]), "w_down": f(inp["w_mlp_down"]),
        "final_norm_w": f(np.asarray(inp["final_norm_w"]).reshape(1, D)),
        "cf": cf, "cfb": cfb, "rope": rope,
    }


def prep_core(cfg, inp, shared, c):
    f = lambda a: np.ascontiguousarray(np.asarray(a))
    m = dict(shared)
    m["xp"] = f(np.asarray(inp["x_prompt"])[c])
    m["xs"] = f(np.asarray(inp["x_sample"])[4 * c:4 * c + 4].reshape(16, D))
    m["st_h"] = f(np.asarray(inp["state_hgrn"])[0, 4 * c:4 * c + 4])
    m["st_r"] = f(np.asarray(inp["state_ret"])[0, 4 * c:4 * c + 4])
    m["pt"] = f(np.asarray(inp["page_table"])[4 * c:4 * c + 4].reshape(1, 4 * cfg.NPG).astype(np.int32))
    m["cvec"] = f(np.concatenate([np.asarray(inp["c_prompt"])[c:c + 1], np.asarray(inp["c_sample"])[4 * c:4 * c + 4]], axis=0))
    return m


def assemble(cfg, res):
    n = cfg.n_cores
    S = cfg.S
    g = lambda k: [np.asarray(r[k]) for r in res]
    y_p = np.stack(g("y_p"))
    y_s = np.concatenate([a.reshape(4, 4, D) for a in g("y_s")], axis=0)
    hg_p = np.stack(g("hg_p"))[None]
    rt_p = np.stack(g("rt_p"))[None]
    k_p = np.stack([a.reshape(S // 128, 128, 4, 2, 128) for a in g("k_p")])[None]
    v_p = np.stack([a.reshape(S // 128, 128, 4, 256) for a in g("v_p")])[None]
    hg_s = np.concatenate(g("hg_s"), axis=0)[None]
    rt_s = np.concatenate(g("rt_s"), axis=0)[None]
    k_s = np.concatenate([a.reshape(4, 4, 4, 2, 128) for a in g("k_s")], axis=0)[None]
    v_s = np.concatenate([a.reshape(4, 4, 4, 256) for a in g("v_s")], axis=0)[None]
    return tuple(np.ascontiguousarray(a.astype(np.float32)) for a in (y_p, y_s, hg_p, rt_p, k_p, v_p, hg_s, rt_s, k_s, v_s))


_NC_CACHE = {}


def kernel(**inputs):
    cfg = Cfg()
    if "nc" not in _NC_CACHE:
        _NC_CACHE["nc"] = build_program(cfg)
    nc = _NC_CACHE["nc"]
    shared = prep_shared(cfg, inputs)
    in_maps = [prep_core(cfg, inputs, shared, c) for c in range(cfg.n_cores)]
    res = run_bass_kernel_spmd(nc, in_maps, core_ids=list(range(cfg.n_cores)))
    return assemble(cfg, res.results)


def layer0_mixer(cfg, nc, kb, ar, A, P, Pb, env):
    S, NT = cfg.S, cfg.NT
    xT, hT, cf, identb, ones_f, zeros_f, lbv, na = (env[k] for k in
                                                    ("xT", "hT", "cf", "identb", "ones_f", "zeros_f", "lbv", "na"))
    sqb, rstd, tnrm = env["sqb"], env["rstd"], env["tnrm"]
    maskf = env["maskf"]
    d = env
    winab_d, woutab_d, rope_d = d["winab_d"], d["woutab_d"], d["rope_d"]
    sth_d, str_d, hgp_d, rtp_d, hgs_d, rts_d = d["sth_d"], d["str_d"], d["hgp_d"], d["rtp_d"], d["hgs_d"], d["rts_d"]
    out_keys = d["out_keys"]
    m0 = ar.mark()
    oT = A("oT", [128, NCH, NT], BF16)
    cfb = A("cfb", [128, NCFB], F32)
    kb.dma("sp", I("dma_start", out=cfb[:], in_=d["cfb_d"][:]), "ld_cfb", w=["cfb"])
    m_w = ar.mark()
    wu = [A(f"wu{i}", [128, NCH, 512], BF16) for i in range(2)]
    wR = A("wR", [128, NCH, 256], BF16)
    f32t = {k: A(k, [128, 512], F32) for k in ("qs", "sg", "om", "rb")}
    bb2 = [A(f"bb{i}", [128, 512], F32) for i in range(2)]
    qTt2 = [A(f"qTt{i}", [128, 512], BF16) for i in range(2)]
    kTt2 = [A(f"kTt{i}", [128, 512], BF16) for i in range(2)]
    gs2 = [A(f"gs{i}", [128, 512], BF16) for i in range(2)]
    vTt2 = [A(f"vTt{i}", [128, 512], BF16) for i in range(2)]
    tcnt = [0]
    zeros128 = A("zeros128", [128, 128], F32)
    kb.op("pool", I("memset", zeros128[:], 0.0), w=["zeros128"])
    cosT = A("cosT", [128, 512], F32); sinT = A("sinT", [128, 512], F32)
    Am2 = [A(f"Am{i}", [128, 128], BF16) for i in range(2)]
    kvtok2 = [A(f"kvtok{i}", [128, 256], BF16) for i in range(2)]
    qd2 = [A(f"qd{i}", [128, 128], BF16) for i in range(2)]
    xcnt = [0]
    U = A("U", [128, 128], F32); Sbf = A("Sbf", [128, 128], BF16); Sfin = A("Sfin", [128, 128], F32)
    belast = A("belast", [128, 1], F32)
    qs, sg, om, rb = (f32t[k] for k in ("qs", "sg", "om", "rb"))

    env["prenorm"](0, 0)

    pjc = [0]

    def proj(ws_ap_fn, t0, n, pi_unused, rkeys):
        pi = pjc[0] % 2
        pjc[0] += 1
        pb, pk = P[pi], f"P{pi}"
        for c in range(NCH):
            kb.op("pe", I("matmul", pb[:, 0:n], lhsT=ws_ap_fn(c), rhs=hT[:, c, t0:t0 + n],
                                                start=(c == 0), stop=(c == NCH - 1)), r=list(rkeys) + ["hT"], w=[pk])
        return pb, pk

    for u in range(8):
        ret = u >= 4
        h = u % 4
        sl = u % 2
        wk = f"wu{sl}"
        kb.dma("pool", I("dma_start", out=wu[sl][:], in_=winab_d[u].rearrange("(c p) n -> p c n", p=128)),
               f"ld_wu{sl}", w=[wk])
        if ret:
            wuv = wu[sl][:, :, 0:256].rearrange("p c (i two) -> p c i two", two=2)
            wRv = wR[:].rearrange("p c (i two) -> p c i two", two=2)
            kb.op("pool", I("tensor_scalar", out=wRv[:, :, :, 0], in0=wuv[:, :, :, 1], scalar1=-1.0, scalar2=None,
                                                                     op0=ALU.mult), r=[wk], w=["wR"])
            kb.op("pool", I("tensor_copy", wRv[:, :, :, 1], wuv[:, :, :, 0]), r=[wk], w=["wR"])
            g_h = 1.0 - 2.0 ** (-5.0 - h)
        st_in = str_d if ret else sth_d
        st_out_p = rtp_d if ret else hgp_d
        st_out_s = rts_d if ret else hgs_d
        for (t0, n) in cfg.tiles:
            is_s = t0 >= S
            W = wu[sl]
            tb = tcnt[0] % 2
            tcnt[0] += 1
            qTt, kTt, gs, vTt, bb = qTt2[tb], kTt2[tb], gs2[tb], vTt2[tb], bb2[tb]
            kq, kk, kg, kv, kbb = f"qTt{tb}", f"kTt{tb}", f"gs{tb}", f"vTt{tb}", f"bb{tb}"
            if not ret:
                pb, pk = proj(lambda c: W[:, c, 0:128], t0, n, 0, [wk])
                kb.op("act", I("activation", out=qs[:, 0:n], in_=pb[:, 0:n], func=AF.Silu), r=[pk], w=["qs"])
                pb, pk = proj(lambda c: W[:, c, 128:256], t0, n, 1, [wk])
                kb.op("act", I("activation", out=sg[:, 0:n], in_=pb[:, 0:n], func=AF.Sigmoid), r=[pk], w=["sg"])
                pb, pk = proj(lambda c: W[:, c, 256:384], t0, n, 2, [wk])
                kb.op("act", I("activation", out=gs[:, 0:n], in_=pb[:, 0:n], func=AF.Silu), r=[pk], w=[kg])
                pb, pk = proj(lambda c: W[:, c, 384:512], t0, n, 0, [wk])
                kb.op("act", I("copy", vTt[:, 0:n], pb[:, 0:n]), r=[pk], w=[kv])
                kb.op("dve", I("tensor_scalar", out=om[:, 0:n], in0=sg[:, 0:n], scalar1=lbv[:, 8 + h:9 + h], scalar2=lbv[:, 4 + h:5 + h],
                                                       op0=ALU.mult, op1=ALU.add), r=["sg", "lbv"], w=["om"])
                kb.op("dve", I("tensor_scalar", out=sg[:, 0:n], in0=sg[:, 0:n], scalar1=lbv[:, 4 + h:5 + h], scalar2=lbv[:, h:h + 1],
                                                       op0=ALU.mult, op1=ALU.add), r=["sg", "lbv"], w=["sg"])
                C = 4 if is_s else 128
                for c0 in range(0, n, C):
                    kb.op("dve", I("tensor_tensor_scan",
                        out=bb[:, c0:c0 + C], data0=sg[:, c0:c0 + C], data1=zeros128[:, 0:C], initial=1.0, op0=ALU.mult, op1=ALU.add),
                        r=["sg", "zeros128"], w=[kbb])
                kb.op("dve", I("reciprocal", rb[:, 0:n], bb[:, 0:n]), r=[kbb], w=["rb"])
                kb.op("dve", I("scalar_tensor_tensor", out=qTt[:, 0:n], in0=qs[:, 0:n], scalar=DK_SCALE, in1=bb[:, 0:n],
                                                              op0=ALU.mult, op1=ALU.mult), r=["qs", kbb], w=[kq])
                kb.op("dve", I("tensor_tensor", out=kTt[:, 0:n], in0=om[:, 0:n], in1=rb[:, 0:n], op=ALU.mult),
                      r=["om", "rb"], w=[kk])
            else:
                kb.dma("sp", I("dma_start", out=cosT[:, 0:n], in_=rope_d[0, :, t0:t0 + n]), "ld_cos", w=["cosT"])
                kb.dma("sp", I("dma_start", out=sinT[:, 0:n], in_=rope_d[1, :, t0:t0 + n]), "ld_sin", w=["sinT"])
                pa, pka = proj(lambda c: W[:, c, 0:128], t0, n, 0, [wk])
                pr, pkr = proj(lambda c: wR[:, c, 0:128], t0, n, 1, ["wR"])
                kb.op("dve", I("tensor_tensor", out=qs[:, 0:n], in0=pa[:, 0:n], in1=cosT[:, 0:n], op=ALU.mult),
                      r=[pka, "cosT"], w=["qs"])
                kb.op("dve", I("tensor_tensor", out=sg[:, 0:n], in0=pr[:, 0:n], in1=sinT[:, 0:n], op=ALU.mult),
                      r=[pkr, "sinT"], w=["sg"])
                kb.op("pool", I("tensor_tensor", out=qTt[:, 0:n], in0=qs[:, 0:n], in1=sg[:, 0:n], op=ALU.add),
                      r=["qs", "sg"], w=[kq])
                pa, pka = proj(lambda c: W[:, c, 128:256], t0, n, 2, [wk])
                pr, pkr = proj(lambda c: wR[:, c, 128:256], t0, n, 0, ["wR"])
                kb.op("dve", I("scalar_tensor_tensor", out=om[:, 0:n], in0=pa[:, 0:n], scalar=DK_SCALE, in1=cosT[:, 0:n],
                                                                     op0=ALU.mult, op1=ALU.mult), r=[pka, "cosT"], w=["om"])
                kb.op("dve", I("scalar_tensor_tensor", out=rb[:, 0:n], in0=pr[:, 0:n], scalar=DK_SCALE, in1=sinT[:, 0:n],
                                                                     op0=ALU.mult, op1=ALU.mult), r=[pkr, "sinT"], w=["rb"])
                kb.op("pool", I("tensor_tensor", out=kTt[:, 0:n], in0=om[:, 0:n], in1=rb[:, 0:n], op=ALU.add),
                      r=["om", "rb"], w=[kk])
                pb, pk = proj(lambda c: W[:, c, 256:384], t0, n, 1, [wk])
                kb.op("act", I("activation", out=gs[:, 0:n], in_=pb[:, 0:n], func=AF.Silu), r=[pk], w=[kg])
                pb, pk = proj(lambda c: W[:, c, 384:512], t0, n, 2, [wk])
                kb.op("act", I("copy", vTt[:, 0:n], pb[:, 0:n]), r=[pk], w=[kv])
                C = 4 if is_s else 128

            nchunks = n // C

            def stage_x(ci):
                c0 = ci * C
                b2 = xcnt[0] % 2
                xcnt[0] += 1
                xb[ci] = b2
                state_zero = (not is_s) and t0 == 0 and ci == 0
                pa, pka = P[2 + b2], f"P{2 + b2}"
                pt, pkt = Pb[4 + b2], f"P{4 + b2}"
                kb.op("pe", I("matmul", pa[0:C, 0:C], lhsT=kTt[:, c0:c0 + C], rhs=qTt[:, c0:c0 + C], start=True, stop=True),
                      r=[kk, kq], w=[pka])
                kb.op("pe", I("transpose", pt[0:C, 0:128], kTt[:, c0:c0 + C], identb[:]), r=[kk, "identb"], w=[pkt])
                kb.op("pe", I("transpose", pt[0:C, 128:256], vTt[:, c0:c0 + C], identb[:]), r=[kv, "identb"], w=[pkt])
                if not ret:
                    mk_ap = maskf[0:C, 0:C]
                else:
                    mk_ap = cfb[0:C, CF_DT + h * 128:CF_DT + h * 128 + C]
                kb.op("dve", I("tensor_tensor", out=Am2[b2][0:C, 0:C], in0=pa[0:C, 0:C], in1=mk_ap, op=ALU.mult),
                      r=[pka, "cf", "cfb"], w=[f"Am{b2}"])
                if not ret:
                    kb.op("act", I("copy", kvtok2[b2][0:C, :], pt[0:C, 0:256]), r=[pkt], w=[f"kvtok{b2}"])
                else:
                    kd = cfb[0:C, (CF_KD4 if is_s else CF_KD128) + h:(CF_KD4 if is_s else CF_KD128) + h + 1]
                    kb.op("act", I("activation", out=kvtok2[b2][0:C, 0:128], in_=pt[0:C, 0:128], func=AF.Identity, scale=kd),
                          r=[pkt, "cfb"], w=[f"kvtok{b2}"])
                    kb.op("act", I("copy", kvtok2[b2][0:C, 128:256], pt[0:C, 128:256]), r=[pkt], w=[f"kvtok{b2}"])
                    if not state_zero:
                        kb.op("pool", I("tensor_tensor", out=qd2[b2][:, 0:C], in0=qTt[:, c0:c0 + C], in1=cfb[:, CF_G1 + h * 128:CF_G1 + h * 128 + C],
                                        op=ALU.mult), r=[kq, "cfb"], w=[f"qd{b2}"])

            def stage_y(ci):
                c0 = ci * C
                b2 = xb[ci]
                Amb, kvb, qdb = Am2[b2], kvtok2[b2], qd2[b2]
                seq = ci if is_s else None
                state_zero = (not is_s) and t0 == 0 and ci == 0
                if is_s:
                    kb.dma("sp", I("dma_start", out=U[:], in_=st_in[seq, h]), "ld_U", w=["U"])
                    kb.op("dve", I("tensor_copy", Sbf[:], U[:]), r=["U"], w=["Sbf"])
                kb.op("pe", I("matmul", P[7][:, c0:c0 + C], lhsT=kvb[0:C, 128:256], rhs=Amb[0:C, 0:C],
                              start=True, stop=state_zero), r=[f"kvtok{b2}", f"Am{b2}"], w=["P7"])
                if not state_zero:
                    q_in = qdb[:, 0:C] if ret else qTt[:, c0:c0 + C]
                    kb.op("pe", I("matmul", P[7][:, c0:c0 + C], lhsT=Sbf[:], rhs=q_in, start=False, stop=True),
                          r=["Sbf", f"qd{b2}", kq], w=["P7"])
                kb.op("pe", I("matmul", P[6][:, 0:128], lhsT=kvb[0:C, 0:128], rhs=kvb[0:C, 128:256], start=True, stop=True),
                      r=[f"kvtok{b2}"], w=["P6"])
                if state_zero:
                    kb.op("dve", I("tensor_copy", U[:], P[6][:, 0:128]), r=["P6"], w=["U"])
                else:
                    if ret:
                        sc_prev = g_h ** C
                    elif is_s:
                        sc_prev = 1.0
                    elif ci == 0:
                        sc_prev = belast[:, 0:1]
                    else:
                        sc_prev = bb[:, c0 - 1:c0]
                    kb.op("dve", I("scalar_tensor_tensor", out=U[:], in0=U[:], scalar=sc_prev, in1=P[6][:, 0:128],
                                                                                 op0=ALU.mult, op1=ALU.add),
                          r=["U", "P6", kbb, "belast"], w=["U"])
                last_of_seq = is_s or (t0 + n == S and ci == nchunks - 1)
                be_cur = bb[:, c0 + C - 1:c0 + C]
                if not last_of_seq:
                    if ret:
                        kb.op("act", I("copy", Sbf[:], U[:]), r=["U"], w=["Sbf"])
                    else:
                        kb.op("dve", I("tensor_scalar", out=Sbf[:], in0=U[:], scalar1=be_cur, scalar2=None, op0=ALU.mult),
                              r=["U", kbb], w=["Sbf"])
                        if ci == nchunks - 1:
                            kb.op("dve", I("tensor_copy", belast[:], be_cur), r=[kbb], w=["belast"])
                else:
                    dst = st_out_s[seq, h] if is_s else st_out_p[h]
                    key = f"st_{u}_{seq}"
                    if ret:
                        kb.op("act", I("copy", Sfin[:], U[:]), r=["U"], w=["Sfin"])
                    else:
                        kb.op("dve", I("tensor_scalar", out=Sfin[:], in0=U[:], scalar1=be_cur, scalar2=None, op0=ALU.mult),
                              r=["U", kbb], w=["Sfin"])
                    kb.dma("sp", I("dma_start", out=dst, in_=Sfin[:]), "st_S", r=["Sfin"], w=[key])
                    out_keys.append(key)

            xb = {}
            stage_x(0)
            for ci in range(nchunks):
                if ci + 1 < nchunks:
                    stage_x(ci + 1)
                stage_y(ci)

            kb.op("act", I("activation", out=sqb[0][:, 0:n], in_=P[7][:, 0:n], func=AF.Square), r=["P7"], w=["sqb0"])
            pns = pjc[0] % 2
            pjc[0] += 1
            kb.op("pe", I("matmul", P[pns][:, 0:n], lhsT=ones_f[:], rhs=sqb[0][:, 0:n], start=True, stop=True),
                  r=["sqb0", "ones_f"], w=[f"P{pns}"])
            kb.op("act", I("activation", out=rstd[:, 0:n], in_=P[pns][:, 0:n], func=AF.Sqrt, scale=1.0 / 128, bias=EPS),
                  r=[f"P{pns}"], w=["rstd"])
            kb.op("dve", I("reciprocal", rstd[:, 0:n], rstd[:, 0:n]), r=["rstd"], w=["rstd"])
            kb.op("dve", I("tensor_tensor", out=tnrm[0][:, 0:n], in0=P[7][:, 0:n], in1=rstd[:, 0:n], op=ALU.mult),
                  r=["P7", "rstd"], w=["tnrm0"])
            nsc = 1.0 if ret else na[:, 0:1]
            kb.op("dve", I("scalar_tensor_tensor", out=oT[:, u, t0:t0 + n], in0=tnrm[0][:, 0:n], scalar=nsc, in1=gs[:, 0:n],
                                                                  op0=ALU.mult, op1=ALU.mult), r=["tnrm0", kg, "na"], w=["oT"])

    kb.barrier()
    m_end = ar.mark()
    ar.reset(m_w)
    wo = A("wo", [128, NCH, 1024], BF16)
    kb.dma("pool", I("dma_start", out=wo[:], in_=woutab_d[:].rearrange("(c p) n -> p c n", p=128)), "ld_wo", w=["wo"])
    npj = 0
    for (t0, n) in cfg.tiles:
        for cc in range(NCH):
            pi = npj % 4
            npj += 1
            pb, pk = P[pi], f"P{pi}"
            for c in range(NCH):
                kb.op("pe", I("matmul", pb[:, 0:n], lhsT=wo[:, c, cc * 128:(cc + 1) * 128], rhs=oT[:, c, t0:t0 + n],
                                                                  start=(c == 0), stop=(c == NCH - 1)), r=["wo", "oT"], w=[pk])
            env["resid_add"](pb, pk, cc, t0, n, 0, 2)
    kb.barrier()
    ar.reset(m0)


def layer1_mixer(cfg, nc, kb, ar, A, P, Pb, env):
    S, NT, NPG = cfg.S, cfg.NT, cfg.NPG
    xT, hT, cf, identb, maskb, neglam, sgn8, wsub = (env[k] for k in ("xT", "hT", "cf", "identb", "maskb", "neglam", "sgn8", "wsub"))
    sqb, rstd, tnrm = env["sqb"], env["rstd"], env["tnrm"]
    d = env
    winc_d, woutc_d, ck_d, cv_d, pt_d = d["winc_d"], d["woutc_d"], d["ck_d"], d["cv_d"], d["pt_d"]
    kp_d, vp_d, ks_d, vs_d = d["kp_d"], d["vp_d"], d["ks_d"], d["vs_d"]
    out_keys = d["out_keys"]
    NVT = S // 128
    NQT = S // 512
    VW = 264

    env["prenorm"](1, 0)
    kb.barrier()
    m0 = ar.mark()
    oTs = A("oTs", [128, NCH, 16], BF16)
    QTs = A("QTs", [128, 8, 16], BF16)
    KTs = A("KTs", [128, 8, 16], BF16)
    Vs = A("Vs", [4, 16, VW], BF16)
    small = A("small", [128, 8], F32)
    ssm = A("ssm", [128, 8], F32)
    oTh = A("oTh", [128, 2, S], BF16)
    wqkv = A("wqkv", [128, NCH, 768], BF16)
    woh = A("woh", [128, 2, 1024], BF16)
    QT = A("QT", [128, 2, S], BF16)
    KT = A("KT", [128, 2, S], BF16)
    Vtok = A("Vtok", [128, NVT, VW], BF16)
    stg = [A(f"stg{i}", [128, 512], F32) for i in range(2)]
    PT = [A(f"PT{i}", [128, 512], BF16) for i in range(2)]
    n1b = A("n1b", [128, 2, 256], F32)
    og = A("og", [128, 256], BF16)
    n1a = rstd[:].rearrange("p (a b) -> p a b", a=2)
    n1 = lambda qs: (n1a if qs < 2 else n1b)[:, qs % 2, :]
    dd = tnrm[0][:, 0:256]
    junk = tnrm[0][:, 256:512]
    NSL = 5
    Kpg = [A(f"Kpg{i}", [128, 1024], BF16) for i in range(NSL)]
    Vpg = [A(f"Vpg{i}", [128, 1024], BF16) for i in range(NSL)]
    KTpg = [A(f"KTpg{i}", [128, 8, 128], BF16) for i in range(2)]
    PTs = [A(f"PTs{i}", [128, 32], BF16) for i in range(2)]
    PTn = A("PTn", [4, 32], BF16)
    ogs = A("ogs", [16, 256], BF16)
    ones_b = A("ones_b", [128, 8], BF16)
    zeros_b = A("zeros_b", [128, 128], BF16)
    ptb = A("ptb", [128, 4 * NPG], I32)
    pid = A("pid", [128, 1], I32)
    pidx = A("pidx", [128, 4 * NPG], I32)
    acc = sqb[0][:, 0:260]
    nsg = sqb[1][:, 0:256]
    junk2 = tnrm[1][0:16, 0:256]
    dsm = tnrm[1][0:16, 256:512]

    kb.op("pool", I("memset", Vtok[:, :, 256:257], 1.0), w=["Vtok"])
    kb.op("pool", I("memset", Vs[:, :, 256:257], 1.0), w=["Vs"])
    kb.op("dve", I("memset", ones_b[:], 1.0), w=["ones_b"])
    kb.op("dve", I("memset", zeros_b[:], 0.0), w=["zeros_b"])
    kb.dma("sp", I("dma_start", out=ptb[:], in_=pt_d[:].partition_broadcast(128)), "ld_pts", w=["ptb"])
    kb.op("pool", I("iota", pid[:], pattern=[[0, 1]], base=0, channel_multiplier=1), w=["pid"])
    kb.op("dve", I("tensor_scalar", out=pidx[:], in0=ptb[:], scalar1=128, scalar2=pid[:, 0:1], op0=ALU.mult, op1=ALU.add),
          r=["ptb", "pid"], w=["pidx"])
    ck_rows = ck_d.rearrange("a p f -> (a p) f")
    cv_rows = cv_d.rearrange("a p f -> (a p) f")

    nproj = [0]

    def pbank():
        pi = 2 + nproj[0] % 2
        nproj[0] += 1
        return P[pi], f"P{pi}"

    nstg = [0]
    tick_hook = [lambda: None]

    def load_head_weights(h, with_out):
        kb.dma("pool", I("dma_start", out=wqkv[:], in_=winc_d[h].rearrange("(c p) n -> p c n", p=128)), "ld_wqkv", w=["wqkv"])
        if with_out:
            kb.dma("pool", I("dma_start", out=woh[:], in_=woutc_d[h * 256:(h + 1) * 256, :].rearrange("(c p) n -> p c n", p=128)),
                   "ld_woh", w=["woh"])

    def project(h, t0, n):
        is_s = t0 >= S
        for j in range(2):
            pb, pk = pbank()
            for c in range(NCH):
                kb.op("pe", I("matmul", pb[:, 0:n], lhsT=wqkv[:, c, j * 128:(j + 1) * 128], rhs=hT[:, c, t0:t0 + n],
                              start=(c == 0), stop=(c == NCH - 1)), r=["wqkv", "hT"], w=[pk])
            if is_s:
                kb.op("act", I("activation", out=QTs[:, h * 2 + j, :], in_=pb[:, 0:16], func=AF.Identity, scale=DK_SCALE), r=[pk], w=["QTs"])
            else:
                kb.op("act", I("activation", out=QT[:, j, t0:t0 + n], in_=pb[:, 0:n], func=AF.Identity, scale=DK_SCALE), r=[pk], w=["QT"])
                tick_hook[0]()
        for j in range(2):
            pb, pk = pbank()
            for c in range(NCH):
                kb.op("pe", I("matmul", pb[:, 0:n], lhsT=wqkv[:, c, 256 + j * 128:256 + (j + 1) * 128], rhs=hT[:, c, t0:t0 + n],
                              start=(c == 0), stop=(c == NCH - 1)), r=["wqkv", "hT"], w=[pk])
            if is_s:
                kb.op("dve", I("tensor_copy", KTs[:, h * 2 + j, :], pb[:, 0:16]), r=[pk], w=["KTs"])
            else:
                kb.op("dve", I("tensor_copy", KT[:, j, t0:t0 + n], pb[:, 0:n]), r=[pk], w=["KT"])
                tick_hook[0]()
        subs = [(S + 4 * i, 4, i) for i in range(4)] if is_s else [(t0 + s * 128, 128, None) for s in range(n // 128)]
        for (c0, rows, seq) in subs:
            pb, pk = pbank()
            for c in range(NCH):
                kb.op("pe", I("matmul", pb[0:rows, 0:512], lhsT=hT[:, c, c0:c0 + rows], rhs=wqkv[:, c, 256:768],
                              start=(c == 0), stop=(c == NCH - 1)), r=["wqkv", "hT"], w=[pk])
            si = nstg[0] % 2
            nstg[0] += 1
            kb.op("act", I("copy", stg[si][0:rows, :], pb[0:rows, 0:512]), r=[pk], w=[f"stg{si}"])
            if seq is None:
                kdst = kp_d[c0:c0 + rows, h * 256:(h + 1) * 256]
                vdst = vp_d[c0:c0 + rows, h * 256:(h + 1) * 256]
                kb.op("dve", I("tensor_copy", Vtok[:, c0 // 128, 0:256], stg[si][:, 256:512]), r=[f"stg{si}"], w=["Vtok"])
            else:
                r0 = c0 - S
                kdst = ks_d[r0:r0 + rows, h * 256:(h + 1) * 256]
                vdst = vs_d[r0:r0 + rows, h * 256:(h + 1) * 256]
                kb.op("dve", I("tensor_copy", Vs[0:4, seq * 4 + h, 0:256], stg[si][0:4, 256:512]), r=[f"stg{si}"], w=["Vs"])
            key = f"kv_{h}_{c0}"
            kb.dma("sp", I("dma_start", out=kdst, in_=stg[si][0:rows, 0:256]), f"st_k{si}", r=[f"stg{si}"], w=[key + "k"])
            kb.dma("sp", I("dma_start", out=vdst, in_=stg[si][0:rows, 256:512]), f"st_v{si}", r=[f"stg{si}"], w=[key + "v"])
            out_keys.extend([key + "k", key + "v"])
            if seq is None:
                tick_hook[0]()

    def sample_gen():
        pages = [(i, p) for i in range(4) for p in range(NPG)]
        NP = len(pages)

        def stageA(n):
            sl, s2 = n % NSL, n % 2
            for hj in range(8):
                kb.op("pe", I("transpose", Pb[0][:, hj * 128:(hj + 1) * 128], Kpg[sl][:, hj * 128:(hj + 1) * 128], identb[:]),
                      r=[f"Kpg{sl}", "identb"], w=["P0"])
            if s2 == 0:
                kb.op("dve", I("tensor_copy", KTpg[s2][:].rearrange("p a b -> p (a b)"), Pb[0][:, 0:1024]), r=["P0"], w=[f"KTpg{s2}"])
            else:
                kb.op("act", I("copy", KTpg[s2][:].rearrange("p a b -> p (a b)"), Pb[0][:, 0:1024]), r=["P0"], w=[f"KTpg{s2}"])

        def stageB(n):
            i, p = pages[n]
            s2 = n % 2
            for hj in range(8):
                kb.op("pe", I("matmul", P[1][:, hj * 4:(hj + 1) * 4], lhsT=KTpg[s2][:, hj, :], rhs=QTs[:, hj, 4 * i:4 * i + 4], start=True, stop=True),
                      r=[f"KTpg{s2}", "QTs"], w=["P1"])
            kb.op("act", I("activation", out=PTs[s2][:], in_=P[1][:, 0:32], func=AF.Exp), r=["P1"], w=[f"PTs{s2}"])

        def stageC(n):
            i, p = pages[n]
            sl, s2 = n % NSL, n % 2
            if p == 0:
                kb.op("pool", I("memset", acc, 0.0), w=["acc"])
            for h in range(4):
                kb.op("pe", I("matmul", P[1][32 * h:32 * h + 8, 64:320], lhsT=PTs[s2][:, h * 8:(h + 1) * 8], rhs=Vpg[sl][:, h * 256:(h + 1) * 256],
                              start=True, stop=False, tile_position=(0, 32 * h)), r=[f"PTs{s2}", f"Vpg{sl}"], w=["P1"])
                kb.op("pe", I("matmul", P[1][32 * h:32 * h + 8, 320:321], lhsT=PTs[s2][:, h * 8:(h + 1) * 8], rhs=ones_b[:, 0:1],
                              start=False, stop=True, tile_position=(0, 32 * h)), r=[f"PTs{s2}", "ones_b"], w=["P1"])
            kb.op("dve", I("tensor_tensor", out=acc[:, 0:257], in0=P[1][:, 64:321], in1=acc[:, 0:257], op=ALU.add), r=["P1", "acc"], w=["acc"])
            if p == NPG - 1:
                finish_seq(i)

        def finish_seq(i):
            for hj in range(8):
                kb.op("pe", I("matmul", P[1][0:4, hj * 4:(hj + 1) * 4], lhsT=KTs[:, hj, 4 * i:4 * i + 4], rhs=QTs[:, hj, 4 * i:4 * i + 4],
                              start=True, stop=True), r=["KTs", "QTs"], w=["P1"])
            kb.op("act", I("activation", out=PTn[:], in_=P[1][0:4, 0:32], func=AF.Exp), r=["P1"], w=["PTn"])
            kb.op("dve", I("tensor_tensor", out=PTn[:].rearrange("p (a b) -> p a b", a=8), in0=PTn[:].rearrange("p (a b) -> p a b", a=8),
                           in1=maskb[0:4, 0:4].unsqueeze(1).to_broadcast([4, 8, 4]), op=ALU.mult), r=["PTn", "maskb"], w=["PTn"])
            for h in range(4):
                kb.op("pe", I("matmul", P[1][32 * h:32 * h + 8, 64:321], lhsT=PTn[0:4, h * 8:(h + 1) * 8], rhs=Vs[0:4, i * 4 + h, 0:257],
                              start=True, stop=True, tile_position=(0, 32 * h)), r=["PTn", "Vs"], w=["P1"])
            kb.op("dve", I("tensor_tensor", out=acc[:, 0:257], in0=P[1][:, 64:321], in1=acc[:, 0:257], op=ALU.add), r=["P1", "acc"], w=["acc"])
            kb.op("dve", I("tensor_scalar", out=ssm[:, 0:1], in0=acc[:, 256:257], scalar1=1e-30, scalar2=None, op0=ALU.add), r=["acc"], w=["ssm0"])
            kb.op("dve", I("reciprocal", ssm[:, 1:2], ssm[:, 0:1]), r=["ssm0"], w=["ssm1"])
            kb.op("dve", I("tensor_tensor", out=ssm[:, 2:3], in0=ssm[:, 1:2], in1=sgn8[:, 0:1], op=ALU.mult), r=["ssm1", "sgn8"], w=["ssm2"])
            kb.op("dve", I("tensor_scalar", out=nsg, in0=acc[:, 0:256], scalar1=ssm[:, 2:3], scalar2=None, op0=ALU.mult), r=["acc", "ssm2"], w=["nsg"])
            kb.op("pe", I("matmul", P[0][0:16, 0:256], lhsT=cf[:, CF_SEL2:CF_SEL2 + 16], rhs=nsg, start=True, stop=True), r=["nsg", "cf"], w=["P0"])
            kb.op("act", I("copy", dsm, P[0][0:16, 0:256]), r=["P0"], w=["dsm"])
            kb.op("act", I("activation", out=junk2, in_=dsm, func=AF.Square, accum_out=ssm[0:16, 3:4]), r=["dsm"], w=["junk2", "ssm3"])
            kb.op("act", I("activation", out=ssm[0:16, 4:5], in_=ssm[0:16, 3:4], func=AF.Sqrt, scale=1.0 / 256, bias=EPS), r=["ssm3"], w=["ssm4"])
            kb.op("dve", I("reciprocal", ssm[0:16, 5:6], ssm[0:16, 4:5]), r=["ssm4"], w=["ssm5"])
            kb.op("dve", I("scalar_tensor_tensor", out=ogs[:], in0=dsm, scalar=ssm[0:16, 5:6], in1=wsub[0:16, :], op0=ALU.mult, op1=ALU.mult),
                  r=["dsm", "ssm5", "wsub"], w=["ogs"])
            for half in range(2):
                kb.op("pe", I("transpose", Pb[0][:, half * 16:(half + 1) * 16], ogs[:, half * 128:(half + 1) * 128], identb[0:16, 0:16]),
                      r=["ogs", "identb"], w=["P0"])
            for half in range(2):
                kb.op("dve", I("tensor_copy", oTs[:].rearrange("p (h two) t -> p two h t", two=2)[:, half, :, 4 * i:4 * i + 4],
                               Pb[0][:, half * 16:(half + 1) * 16].rearrange("p (h q) -> p h q", h=4)), r=["P0"], w=["oTs"])

        for (c0_, c1_) in ((64, 192), (192, 320), (320, 321)):
            kb.op("pe", I("matmul", P[1][:, c0_:c1_], lhsT=zeros_b[:], rhs=identb[:, 0:c1_ - c0_], start=True, stop=True),
                  r=["zeros_b", "identb"], w=["P1"])
        kdma = lambda n: kb.dma("pool", I("indirect_dma_start", out=Kpg[n % NSL][:], out_offset=None, in_=ck_rows,
                                          in_offset=bass.IndirectOffsetOnAxis(ap=pidx[:, pages[n][0] * NPG + pages[n][1]:pages[n][0] * NPG + pages[n][1] + 1], axis=0)),
                                f"ld_kpg{n % NSL}", r=["pidx"], w=[f"Kpg{n % NSL}"])
        vdma = lambda n: kb.dma("pool", I("indirect_dma_start", out=Vpg[n % NSL][:], out_offset=None, in_=cv_rows,
                                          in_offset=bass.IndirectOffsetOnAxis(ap=pidx[:, pages[n][0] * NPG + pages[n][1]:pages[n][0] * NPG + pages[n][1] + 1], axis=0)),
                                f"ld_vpg{n % NSL}", r=["pidx"], w=[f"Vpg{n % NSL}"])
        for n in range(min(NSL - 1, NP)):
            kdma(n)
        for n in range(min(NSL - 2, NP)):
            vdma(n)
        for sstep in range(NP + 2):
            if sstep < NP:
                stageA(sstep)
            if 0 <= sstep - 1 < NP:
                stageB(sstep - 1)
            if 0 <= sstep - 2 < NP:
                stageC(sstep - 2)
            if sstep + NSL - 1 < NP:
                kdma(sstep + NSL - 1)
            if sstep + NSL - 2 < NP:
                vdma(sstep + NSL - 2)
            yield

    for h in range(4):
        load_head_weights(h, False)
        project(h, S, 16)
    gen = sample_gen()
    n_points = 4 * ((S // 512) * 8 + sum(4 * qt + 4 for qt in range(NQT)) * 2 + (S // 512) * 8)
    n_steps = 4 * NPG + 2
    pace = {"pts": 0, "done": 0}

    def tick():
        pace["pts"] += 1
        want = min(n_steps, (pace["pts"] * n_steps + n_points - 1) // n_points)
        while pace["done"] < want:
            next(gen, None)
            pace["done"] += 1

    tick_hook[0] = tick

    for h in range(4):
        load_head_weights(h, True)
        for (t0, n) in cfg.tiles:
            if t0 < S:
                project(h, t0, n)
        nst = [0]
        for qt in range(NQT):
            for j in range(2):
                nk = 4 * qt + 4
                bufs = {}

                def scores(kbi):
                    n0 = max(0, kbi - 4 * qt)
                    sb = 2 + nst[0] % 2
                    ps = nst[0] % 2
                    nst[0] += 1
                    bufs[kbi] = ps
                    q0 = qt * 512 + n0 * 128
                    kb.op("pe", I("matmul", P[sb][:, n0 * 128:512], lhsT=KT[:, j, kbi * 128:(kbi + 1) * 128], rhs=QT[:, j, q0:(qt + 1) * 512],
                                  start=True, stop=True), r=["KT", "QT"], w=[f"P{sb}"])
                    kb.op("act", I("activation", out=PT[ps][:, n0 * 128:512], in_=P[sb][:, n0 * 128:512], func=AF.Exp),
                          r=[f"P{sb}"], w=[f"PT{ps}"])
                    if kbi >= 4 * qt:
                        kb.op("dve", I("tensor_tensor", out=PT[ps][:, n0 * 128:(n0 + 1) * 128], in0=PT[ps][:, n0 * 128:(n0 + 1) * 128],
                                       in1=maskb[:], op=ALU.mult), r=[f"PT{ps}", "maskb"], w=[f"PT{ps}"])

                scores(0)
                for kbi in range(nk):
                    n0 = max(0, kbi - 4 * qt)
                    if kbi + 1 < nk:
                        scores(kbi + 1)
                    ps = bufs[kbi]
                    for qs in range(n0, 4):
                        kb.op("pe", I("matmul", P[4 + qs][:, 0:257], lhsT=PT[ps][:, qs * 128:(qs + 1) * 128], rhs=Vtok[:, kbi, 0:257],
                                      start=(kbi == 0), stop=(kbi == 4 * qt + qs)), r=[f"PT{ps}", "Vtok"], w=[f"P{4 + qs}"])
                    tick()
                for qs in range(4):
                    ob, ok = P[4 + qs], f"P{4 + qs}"
                    tok0 = qt * 512 + qs * 128
                    if j == 0:
                        kb.op("dve", I("reciprocal", small[:, 0:1], ob[:, 256:257]), r=[ok], w=["sm0"])
                        kb.op("act", I("activation", out=n1(qs), in_=ob[:, 0:256], func=AF.Identity, scale=small[:, 0:1]),
                              r=[ok, "sm0"], w=[f"n1{qs}"])
                    else:
                        kb.op("dve", I("reciprocal", small[:, 1:2], ob[:, 256:257]), r=[ok], w=["sm1"])
                        kb.op("dve", I("tensor_tensor", out=small[:, 2:3], in0=small[:, 1:2], in1=neglam[:, 0:1], op=ALU.mult),
                              r=["sm1", "neglam"], w=["sm2"])
                        kb.op("dve", I("scalar_tensor_tensor", out=dd, in0=ob[:, 0:256], scalar=small[:, 2:3], in1=n1(qs),
                                       op0=ALU.mult, op1=ALU.add), r=[ok, "sm2", f"n1{qs}"], w=["dd"])
                        kb.op("act", I("activation", out=junk, in_=dd, func=AF.Square, accum_out=small[:, 3:4]), r=["dd"], w=["junk1", "sm3"])
                        kb.op("act", I("activation", out=small[:, 4:5], in_=small[:, 3:4], func=AF.Sqrt, scale=1.0 / 256, bias=EPS),
                              r=["sm3"], w=["sm4"])
                        kb.op("dve", I("reciprocal", small[:, 5:6], small[:, 4:5]), r=["sm4"], w=["sm5"])
                        kb.op("dve", I("scalar_tensor_tensor", out=og[:], in0=dd, scalar=small[:, 5:6], in1=wsub[:], op0=ALU.mult, op1=ALU.mult),
                              r=["dd", "sm5", "wsub"], w=["og"])
                        pb, pk = pbank()
                        pbb = pb[:].bitcast(BF16)
                        for half in range(2):
                            kb.op("pe", I("transpose", pbb[:, half * 128:(half + 1) * 128], og[:, half * 128:(half + 1) * 128], identb[:]),
                                  r=["og", "identb"], w=[pk])
                        kb.op("act", I("copy", oTh[:, :, tok0:tok0 + 128], pbb[:, 0:256].rearrange("p (c t) -> p c t", c=2)), r=[pk], w=["oTh"])
        for (t0, n) in cfg.tiles:
            if t0 >= S:
                continue
            for cc in range(NCH):
                pb, pk = pbank()
                for c2 in range(2):
                    kb.op("pe", I("matmul", pb[:, 0:n], lhsT=woh[:, c2, cc * 128:(cc + 1) * 128], rhs=oTh[:, c2, t0:t0 + n],
                                  start=(c2 == 0), stop=(c2 == 1)), r=["woh", "oTh"], w=[pk])
                env["resid_add"](pb, pk, cc, t0, n, 1, 2)
                tick()
    for _ in gen:
        pass

    for h in range(4):
        kb.dma("pool", I("dma_start", out=woh[:], in_=woutc_d[h * 256:(h + 1) * 256, :].rearrange("(c p) n -> p c n", p=128)),
               "ld_woh", w=["woh"])
        for cc in range(NCH):
            pb, pk = pbank()
            for c2 in range(2):
                kb.op("pe", I("matmul", pb[:, 0:16], lhsT=woh[:, c2, cc * 128:(cc + 1) * 128], rhs=oTs[:, 2 * h + c2, :],
                              start=(c2 == 0), stop=(c2 == 1)), r=["woh", "oTs"], w=[pk])
            env["resid_add"](pb, pk, cc, S, 16, 1, 2)
    kb.barrier()
    ar.reset(m0)
```

```python
import math
import numpy as np
import concourse.bass as bass
import concourse.mybir as mybir
from concourse.bass_utils import run_bass_kernel_spmd

F32 = mybir.dt.float32
BF16 = mybir.dt.bfloat16
I32 = mybir.dt.int32
AF = mybir.ActivationFunctionType
ALU = mybir.AluOpType

ENGS = ("pe", "act", "dve", "pool", "sp")
D = 1024
NCH = 8
DFF = 4096
EPS = 1e-6
LAM_INIT = 0.8 - 0.6 * math.exp(-0.3 * 1)
DK_SCALE = 128.0 ** -0.5


class Cfg:
    def __init__(self, S=2048, NPG=64, NPHYS=2560, n_cores=8, debug=False):
        self.S = S
        self.NPG = NPG
        self.NPHYS = NPHYS
        self.n_cores = n_cores
        self.NT = S + 16
        self.POS0 = NPG * 128
        self.debug = debug
        self.tiles = [(t0, 512) for t0 in range(0, S, 512)] + [(S, 16)]


class KB:
    def __init__(self, nc):
        self.nc = nc
        self.q = {e: [] for e in ENGS}
        self.sems = {e: nc.alloc_semaphore("sem_" + e) for e in ENGS}
        self.cnt = {e: 0 for e in ENGS}
        self.seen = {e: {} for e in ENGS}
        self.lastw = {}
        self.readers = {}

    def dsem(self, name):
        if name not in self.sems:
            self.sems[name] = self.nc.alloc_semaphore("dsem_" + name)
            self.cnt[name] = 0
        return name

    def _deps(self, eng, r, w, skip_sem=None):
        deps = {}

        def need(x):
            if x is None:
                return
            sk, v = x
            if v > deps.get(sk, 0):
                deps[sk] = v
        for b in r:
            need(self.lastw.get(b))
        for b in w:
            need(self.lastw.get(b))
            for rd in self.readers.get(b, ()):
                need(rd)
        waits = []
        for sk, v in deps.items():
            if sk == skip_sem:
                continue
            if self.seen[eng].get(sk, 0) >= v:
                continue
            self.seen[eng][sk] = v
            waits.append((sk, v))
        return waits

    def _commit(self, me, r, w):
        for b in w:
            self.lastw[b] = me
            self.readers[b] = []
        for b in r:
            self.readers.setdefault(b, []).append(me)

    @staticmethod
    def _is_psum(k):
        return len(k) == 2 and k[0] == "P" and k[1].isdigit()

    def op(self, eng, fn, r=(), w=()):
        w = tuple(w) + tuple(k for k in r if self._is_psum(k))
        r = tuple(k for k in r if not self._is_psum(k))
        waits = self._deps(eng, r, w, skip_sem="pe" if eng == "pe" else None)
        self.cnt[eng] += 1
        me = (eng, self.cnt[eng])
        self.q[eng].append((waits, fn, eng, 1))
        self._commit(me, r, w)

    def dma(self, eng, fn, sem, r=(), w=(), group=False):
        r = tuple(r); w = tuple(w)
        self.dsem(sem)
        waits = self._deps(eng, r, w, skip_sem=sem if group else None)
        self.cnt[sem] += 16
        me = (sem, self.cnt[sem])
        self.q[eng].append((waits, fn, sem, 16))
        self._commit(me, r, w)

    def barrier(self):
        for e in ENGS:
            waits = []
            for sk, v in self.cnt.items():
                if v > self.seen[e].get(sk, 0):
                    self.seen[e][sk] = v
                    waits.append((sk, v))
            if waits:
                self.q[e].append((waits, None, None, 0))

    def emit(self):
        nc = self.nc
        with nc.Block() as block:
            def mk(ename):
                def body(e):
                    for waits, fn, incsem, incv in self.q[ename]:
                        for sk, v in waits:
                            e.wait_ge(self.sems[sk], v)
                        if fn is not None:
                            fn(e).then_inc(self.sems[incsem], incv)
                return body
            block.tensor(mk("pe"))
            block.scalar(mk("act"))
            block.vector(mk("dve"))
            block.gpsimd(mk("pool"))
            block.sync(mk("sp"))


def I(name, *a, **k):
    return lambda e: getattr(e, name)(*a, **k)


class Arena:
    def __init__(self, nc):
        self.nc = nc
        self.off = (int(nc.sbuf_base) + 63) // 64 * 64
        self.limit = int(nc.sbuf_top)
        self.n = 0

    def alloc(self, name, shape, dtype):
        esz = 2 if dtype == BF16 else 4
        nbytes = int(np.prod(shape[1:])) * esz
        self.off = (self.off + 31) // 32 * 32
        assert self.off + nbytes <= self.limit, f"SBUF overflow allocating {name}: {self.off}+{nbytes}>{self.limit}"
        self.n += 1
        t = self.nc.alloc_sbuf_tensor_at(f"{name}_{self.n}", list(shape), dtype, offset=self.off)
        self.off += nbytes
        return t

    def mark(self):
        return self.off

    def reset(self, m):
        self.off = m


CF_IDENT, CF_MASK, CF_M0, CF_M1, CF_SEL2 = 0, 128, 256, 257, 258
NCF = 274
CF_DT, CF_G1, CF_KD128, CF_KD4 = 0, 512, 1024, 1028
NCFB = 1032


def make_consts(cfg):
    cf = np.zeros((128, NCF), np.float64)
    cfb = np.zeros((128, NCFB), np.float64)
    cf[:, CF_IDENT:CF_IDENT + 128] = np.eye(128)
    s = np.arange(128)[:, None]
    t = np.arange(128)[None, :]
    cf[:, CF_MASK:CF_MASK + 128] = (s <= t)
    for h in range(4):
        g = 1.0 - 2.0 ** (-5.0 - h)
        cfb[:, CF_DT + h * 128:CF_DT + (h + 1) * 128] = np.where(t >= s, g ** np.maximum(t - s, 0), 0.0)
        cfb[:, CF_G1 + h * 128:CF_G1 + (h + 1) * 128] = g ** (t + 1.0)
        cfb[:, CF_KD128 + h] = g ** (127.0 - s[:, 0])
        cfb[:4, CF_KD4 + h] = g ** (3.0 - s[:4, 0])
        cf[32 * h:32 * h + 4, CF_M0] = 1.0
        cf[32 * h + 4:32 * h + 8, CF_M1] = 1.0
        for j in range(2):
            for q in range(4):
                cf[32 * h + j * 4 + q, CF_SEL2 + h * 4 + q] = 1.0
    d = 128
    inv = 1.0 / (10000.0 ** np.linspace(0.0, 1.0, d // 2, dtype=np.float32).astype(np.float64))
    pos = np.concatenate([np.arange(cfg.S), np.tile(cfg.POS0 + np.arange(4), 4)]).astype(np.float64)
    ang = np.repeat(inv, 2)[:, None] * pos[None, :]
    rope = np.stack([np.cos(ang), np.sin(ang)]).astype(np.float32)
    return cf.astype(np.float32), cfb.astype(np.float32), rope


def build_program(cfg):
    nc = bass.Bass("TRN2", target_bir_lowering=False)
    S, NT, NPG = cfg.S, cfg.NT, cfg.NPG
    kb = KB(nc)

    def din(name, shape, dt=F32):
        return nc.dram_tensor(name, list(shape), dt, kind="ExternalInput").ap()

    def dout(name, shape, dt=F32):
        return nc.dram_tensor(name, list(shape), dt, kind="ExternalOutput").ap()

    xp_d = din("xp", [S, D]); xs_d = din("xs", [16, D])
    sth_d = din("st_h", [4, 4, 128, 128]); str_d = din("st_r", [4, 4, 128, 128])
    ck_d = din("ck", [cfg.NPHYS, 128, 1024]); cv_d = din("cv", [cfg.NPHYS, 128, 1024])
    pt_d = din("pt", [1, 4 * NPG], I32)
    cvec_d = din("cvec", [5, D])
    wada_d = din("w_ada", [2, D, 6 * D]); bada_d = din("b_ada", [96, 128])
    normw_d = din("norm_w", [32, 128])
    winab_d = din("w_in_ab", [8, D, 512]); woutab_d = din("w_out_ab", [D, D])
    lbl_d = din("lb_logits", [8, 128]); hnw_d = din("hgrn_norm_w", [1, 128])
    winc_d = din("w_in_c", [4, D, 768]); woutc_d = din("w_out_c", [D, D])
    dlam_d = din("diff_lambda", [1, 512]); subln_d = din("subln_w", [1, 256])
    wup_d = din("w_up", [2, D, DFF]); wdn_d = din("w_down", [2, DFF, D])
    fnw_d = din("final_norm_w", [1, D])
    cf_d = din("cf", [128, NCF]); cfb_d = din("cfb", [128, NCFB]); rope_d = din("rope", [2, 128, NT])

    yp_d = dout("y_p", [S, D]); ys_d = dout("y_s", [16, D])
    hgp_d = dout("hg_p", [4, 128, 128]); rtp_d = dout("rt_p", [4, 128, 128])
    kp_d = dout("k_p", [S, 1024]); vp_d = dout("v_p", [S, 1024])
    hgs_d = dout("hg_s", [4, 4, 128, 128]); rts_d = dout("rt_s", [4, 4, 128, 128])
    ks_d = dout("k_s", [16, 1024]); vs_d = dout("v_s", [16, 1024])
    out_keys = []

    ar = Arena(nc)
    A = ar.alloc
    xT = A("xT", [128, NCH, NT], F32)
    hT = A("hT", [128, NCH, NT], BF16)
    cf = A("cf", [128, NCF], F32)
    identb = A("identb", [128, 128], BF16)
    maskb = A("maskb", [128, 128], BF16)
    ones_f = A("ones_f", [128, 128], F32)
    zeros_f = A("zeros_f", [128, 64], F32)
    modT = A("modT", [128, 2 * 48 * 5], F32)
    wmT = A("wmT", [128, 4 * 8 * 5], F32)
    nwT = A("nwT", [128, 32], F32)
    bT = A("bT", [128, 96], F32)
    scT = A("scT", [128, NCH, 5], BF16)
    lbv = A("lbv", [128, 12], F32)
    na = A("na", [128, 1], F32)
    neglam = A("neglam", [128, 1], F32)
    sgn8 = A("sgn8", [128, 1], F32)
    wsub = A("wsub", [128, 256], F32)
    ident = cf[:, CF_IDENT:CF_IDENT + 128]
    maskf = cf[:, CF_MASK:CF_MASK + 128]
    sqb = [A(f"sqb{i}", [128, 512], F32) for i in range(2)]
    rstd = A("rstd", [128, 512], F32)
    tnrm = [A(f"tnrm{i}", [128, 512], F32) for i in range(2)]
    phase_base = ar.mark()

    P = [nc.alloc_psum_tensor(f"P{i}", [128, 512], F32) for i in range(8)]
    Pb = [p[:].bitcast(BF16) for p in P]

    def mod_ap(l, j, c, b0, nb=1):
        o = ((l * 6 + j) * 8 + c) * 5 + b0
        return modT[:, o:o + nb]

    def wm_ap(l, i, c, b0):
        o = ((l * 2 + i) * 8 + c) * 5 + b0
        return wmT[:, o:o + 1]

    def bsegs(t0, n):
        if t0 < S:
            return [(t0, n, 0)]
        return [(S + 4 * i, 4, 1 + i) for i in range(4)]

    setup_m = ar.mark()
    c5 = A("c5", [5, D], F32)
    s5 = A("s5", [5, D], F32)
    ldrow = A("ldrow", [128, 128], F32)
    lam_t = A("lam_t", [1, 520], F32)
    wada_sl = [A(f"wada{i}", [128, NCH, 512], BF16) for i in range(2)]

    kb.dma("sp", I("dma_start", out=cf[:], in_=cf_d[:]), "ld_cf", w=["cf"])
    kb.dma("sp", I("dma_start", out=c5[:], in_=cvec_d[:]), "ld_c5", w=["c5"])
    kb.op("dve", I("memset", ones_f[:], 1.0), w=["ones_f"])
    kb.op("dve", I("memset", zeros_f[:], 0.0), w=["zeros_f"])
    kb.op("dve", I("tensor_copy", identb[:], ident), r=["cf"], w=["identb"])
    kb.op("dve", I("tensor_copy", maskb[:], maskf), r=["cf"], w=["maskb"])

    kb.op("act", I("activation", out=s5[:], in_=c5[:], func=AF.Silu), r=["c5"], w=["s5"])
    for c in range(NCH):
        kb.op("pe", I("transpose", P[0][:, c * 5:(c + 1) * 5], s5[:, c * 128:(c + 1) * 128], cf[0:5, 0:5]),
              r=["s5", "cf"], w=["P0"])
    kb.op("dve", I("tensor_copy", scT[:].rearrange("p c b -> p (c b)"), P[0][:, 0:40]), r=["P0"], w=["scT"])

    def load_cols(src_ap, nrows, dst_ap, key, tag):
        kb.dma("sp", I("dma_start", out=ldrow[0:nrows, :], in_=src_ap), "ld_row", w=["ldrow"])
        kb.op("pe", I("transpose", P[1][:, 0:nrows], ldrow[0:nrows, :], cf[0:nrows, 0:nrows]),
              r=["ldrow", "cf"], w=["P1"])
        kb.op("dve", I("tensor_copy", dst_ap, P[1][:, 0:nrows]), r=["P1"], w=[key])

    load_cols(bada_d[:], 96, bT[:], "bT", "b")
    load_cols(normw_d[:], 32, nwT[:], "nwT", "n")
    lbt = A("lbt", [128, 8], F32)
    load_cols(lbl_d[:], 8, lbt[:], "lbt", "l")
    load_cols(hnw_d[:], 1, na[:], "na", "h")
    kb.op("dve", I("tensor_tensor", out=lbt[:, 0:4], in0=lbt[:, 0:4], in1=lbt[:, 4:8], op=ALU.subtract),
          r=["lbt"], w=["lbt"])
    kb.op("act", I("activation", out=lbv[:, 0:4], in_=lbt[:, 0:4], func=AF.Sigmoid), r=["lbt"], w=["lbv"])
    kb.op("dve", I("tensor_scalar", out=lbv[:, 4:8], in0=lbv[:, 0:4], scalar1=-1.0, scalar2=1.0,
                                           op0=ALU.mult, op1=ALU.add), r=["lbv"], w=["lbv"])
    kb.op("dve", I("tensor_scalar", out=lbv[:, 8:12], in0=lbv[:, 0:4], scalar1=-1.0, scalar2=None,
                                           op0=ALU.add), r=["lbv"], w=["lbv"])

    kb.dma("sp", I("dma_start", out=lam_t[:, 0:512], in_=dlam_d[:]), "ld_lam", w=["lam_t"])
    kb.op("dve", I("tensor_tensor", out=lam_t[:, 0:128], in0=lam_t[:, 0:128], in1=lam_t[:, 128:256], op=ALU.mult),
          r=["lam_t"], w=["lam_t"])
    kb.op("dve", I("tensor_tensor", out=lam_t[:, 256:384], in0=lam_t[:, 256:384], in1=lam_t[:, 384:512], op=ALU.mult),
          r=["lam_t"], w=["lam_t"])
    kb.op("dve", I("reduce_sum", out=lam_t[:, 512:513], in_=lam_t[:, 0:128], axis=mybir.AxisListType.X),
          r=["lam_t"], w=["lam_t"])
    kb.op("dve", I("reduce_sum", out=lam_t[:, 513:514], in_=lam_t[:, 256:384], axis=mybir.AxisListType.X),
          r=["lam_t"], w=["lam_t"])
    kb.op("act", I("activation", out=lam_t[:, 514:516], in_=lam_t[:, 512:514], func=AF.Exp), r=["lam_t"], w=["lam_t"])
    kb.op("dve", I("tensor_tensor", out=lam_t[:, 516:517], in0=lam_t[:, 515:516], in1=lam_t[:, 514:515], op=ALU.subtract),
          r=["lam_t"], w=["lam_t"])
    kb.op("dve", I("tensor_scalar", out=lam_t[:, 518:519], in0=lam_t[:, 516:517], scalar1=-LAM_INIT,
                                           scalar2=None, op0=ALU.add), r=["lam_t"], w=["lam_t"])
    kb.op("pe", I("matmul", P[1][:, 0:1], lhsT=ones_f[0:1, :], rhs=lam_t[:, 518:519], start=True, stop=True),
          r=["lam_t", "ones_f"], w=["P1"])
    kb.op("dve", I("tensor_copy", neglam[:], P[1][:, 0:1]), r=["P1"], w=["neglam"])
    kb.op("dve", I("scalar_tensor_tensor", out=sgn8[:, :], in0=cf[:, CF_M1:CF_M1 + 1], scalar=neglam[:, 0:1],
                                                  in1=cf[:, CF_M0:CF_M0 + 1], op0=ALU.mult, op1=ALU.add),
          r=["neglam", "cf"], w=["sgn8"])
    kb.dma("sp", I("dma_start", out=wsub[:], in_=subln_d[:].partition_broadcast(128)), "ld_bc", w=["wsub"])
    kb.op("dve", I("tensor_scalar", out=wsub[:], in0=wsub[:], scalar1=1.0 - LAM_INIT, scalar2=None, op0=ALU.mult),
          r=["wsub"], w=["wsub"])

    for l in range(2):
        for g in range(12):
            sl = (l * 12 + g) % 2
            kb.dma("pool", I("dma_start",
                out=wada_sl[sl][:], in_=wada_d[l, :, g * 512:(g + 1) * 512].rearrange("(c p) n -> p c n", p=128)),
                f"ld_wada{sl}", w=[f"wada{sl}"])
            pb = P[2 + sl]
            for blk in range(4):
                for c in range(NCH):
                    kb.op("pe", I("matmul",
                        pb[:, blk * 5:(blk + 1) * 5], lhsT=wada_sl[sl][:, c, blk * 128:(blk + 1) * 128], rhs=scT[:, c, :],
                        start=(c == 0), stop=(c == NCH - 1)), r=[f"wada{sl}", "scT"], w=[f"P{2 + sl}"])
            for blk in range(4):
                bi = l * 48 + g * 4 + blk
                kb.op("dve", I("tensor_scalar",
                    out=modT[:, bi * 5:(bi + 1) * 5], in0=pb[:, blk * 5:(blk + 1) * 5], scalar1=bT[:, bi:bi + 1], scalar2=None,
                    op0=ALU.add), r=[f"P{2 + sl}", "bT"], w=["modT"])
    for l in range(2):
        for i in range(2):
            for c in range(NCH):
                o = ((l * 2 + i) * 8 + c)
                kb.op("dve", I("tensor_scalar",
                    out=wmT[:, o * 5:(o + 1) * 5], in0=mod_ap(l, 1 + 3 * i, c, 0, 5), scalar1=1.0, scalar2=nwT[:, o:o + 1],
                    op0=ALU.add, op1=ALU.mult), r=["modT", "nwT"], w=["wmT"])

    xld = [A(f"xld{i}", [128, D], F32) for i in range(2)]
    n_xt = S // 128
    for it in range(n_xt + 1):
        sl = it % 2
        rows = 128 if it < n_xt else 16
        src = xp_d[it * 128:(it + 1) * 128, :] if it < n_xt else xs_d[:]
        kb.dma("sp", I("dma_start", out=xld[sl][0:rows, :], in_=src),
               f"ld_x{sl}", w=[f"xld{sl}"])
        for half in range(2):
            pb = P[4 + 2 * sl + half]
            pk = f"P{4 + 2 * sl + half}"
            for cc in range(4):
                c = half * 4 + cc
                kb.op("pe", I("transpose",
                    pb[:, cc * 128:cc * 128 + rows], xld[sl][0:rows, c * 128:(c + 1) * 128], cf[0:rows, 0:rows]),
                    r=[f"xld{sl}", "cf"], w=[pk])
            t0 = it * 128
            eng = "act" if half == 0 else "dve"
            if eng == "act":
                kb.op("act", I("copy",
                    xT[:, half * 4:(half + 1) * 4, t0:t0 + rows],
                    pb[:].rearrange("p (c t) -> p c t", c=4)[:, :, 0:rows]), r=[pk], w=["xT"])
            else:
                kb.op("dve", I("tensor_copy",
                    xT[:, half * 4:(half + 1) * 4, t0:t0 + rows],
                    pb[:].rearrange("p (c t) -> p c t", c=4)[:, :, 0:rows]), r=[pk], w=["xT"])
    kb.barrier()
    ar.reset(setup_m)

    def prenorm(l, i):
        for ti in range(len(cfg.tiles)):
            prenorm_tile(l, i, ti)

    def prenorm_tile(l, i, ti):
        jsh = 3 * i
        for (t0, n) in [cfg.tiles[ti]]:
            pk = f"P{ti % 2}"
            pb = P[ti % 2]
            for c in range(NCH):
                sq = sqb[c % 2]
                kb.op("act", I("activation", out=sq[:, 0:n], in_=xT[:, c, t0:t0 + n], func=AF.Square),
                      r=["xT"], w=[f"sqb{c % 2}"])
                kb.op("pe", I("matmul", pb[:, 0:n], lhsT=ones_f[:], rhs=sq[:, 0:n],
                                                                      start=(c == 0), stop=(c == NCH - 1)),
                      r=[f"sqb{c % 2}", "ones_f"], w=[pk])
            kb.op("act", I("activation", out=rstd[:, 0:n], in_=pb[:, 0:n], func=AF.Sqrt, scale=1.0 / D, bias=EPS),
                  r=[pk], w=["rstd"])
            kb.op("dve", I("reciprocal", rstd[:, 0:n], rstd[:, 0:n]), r=["rstd"], w=["rstd"])
            for c in range(NCH):
                tn = tnrm[c % 2]
                kb.op("dve", I("tensor_tensor", out=tn[:, 0:n], in0=xT[:, c, t0:t0 + n], in1=rstd[:, 0:n],
                                                                              op=ALU.mult), r=["xT", "rstd"], w=[f"tnrm{c % 2}"])
                for (s0, sn, b) in bsegs(t0, n):
                    kb.op("act", I("activation",
                        out=hT[:, c, s0:s0 + sn], in_=tn[:, s0 - t0:s0 - t0 + sn], func=AF.Identity,
                        scale=wm_ap(l, i, c, b), bias=mod_ap(l, jsh, c, b)),
                        r=[f"tnrm{c % 2}", "wmT", "modT"], w=[f"hT{ti}"])

    def resid_add(pb, pk, cc, t0, n, l, jg):
        for (s0, sn, b) in bsegs(t0, n):
            kb.op("dve", I("scalar_tensor_tensor",
                out=xT[:, cc, s0:s0 + sn], in0=pb[:, s0 - t0:s0 - t0 + sn], scalar=mod_ap(l, jg, cc, b),
                in1=xT[:, cc, s0:s0 + sn], op0=ALU.mult, op1=ALU.add), r=[pk, "modT", "xT"], w=["xT"])

    def mlp(l):
        m = ar.mark()
        wup = [A(f"wup{i}", [128, NCH, 1024], BF16) for i in range(2)]
        wdn = [A(f"wdn{i}", [128, NCH, 1024], BF16) for i in range(2)]
        uT = [A(f"uT{i}", [128, NCH, 512], BF16) for i in range(2)]
        rl = [A(f"rl{i}", [128, 512], BF16) for i in range(2)]
        ntl = len(cfg.tiles)
        prenorm_tile(l, 1, 0)
        it = 0
        nup = 0
        ndn = 0
        for g in range(4):
            sl = g % 2
            kb.dma("pool", I("dma_start",
                out=wup[sl][:], in_=wup_d[l, :, g * 1024:(g + 1) * 1024].rearrange("(c p) n -> p c n", p=128)),
                f"ld_wup{sl}", w=[f"wup{sl}"])
            kb.dma("pool", I("dma_start",
                out=wdn[sl][:], in_=wdn_d[l, g * 1024:(g + 1) * 1024, :].rearrange("(f p) n -> p f n", p=128)),
                f"ld_wdn{sl}", w=[f"wdn{sl}"])
            for ti, (t0, n) in enumerate(cfg.tiles):
                if g == 0 and ti + 1 < ntl:
                    prenorm_tile(l, 1, ti + 1)
                us = it % 2
                it += 1
                for f in range(NCH):
                    pi = nup % 4
                    nup += 1
                    pb, pk = P[pi], f"P{pi}"
                    for c in range(NCH):
                        kb.op("pe", I("matmul",
                            pb[:, 0:n], lhsT=wup[sl][:, c, f * 128:(f + 1) * 128], rhs=hT[:, c, t0:t0 + n],
                            start=(c == 0), stop=(c == NCH - 1)), r=[f"wup{sl}", f"hT{ti}"], w=[pk])
                    rs = f % 2
                    kb.op("act", I("activation", out=rl[rs][:, 0:n], in_=pb[:, 0:n], func=AF.Relu),
                          r=[pk], w=[f"rl{rs}"])
                    kb.op("dve", I("tensor_tensor",
                        out=uT[us][:, f, 0:n], in0=pb[:, 0:n], in1=rl[rs][:, 0:n], op=ALU.mult),
                        r=[pk, f"rl{rs}"], w=[f"uT{us}"])
                for cc in range(NCH):
                    pi = 4 + ndn % 4
                    ndn += 1
                    pb, pk = P[pi], f"P{pi}"
                    for f in range(NCH):
                        kb.op("pe", I("matmul",
                            pb[:, 0:n], lhsT=wdn[sl][:, f, cc * 128:(cc + 1) * 128], rhs=uT[us][:, f, 0:n],
                            start=(f == 0), stop=(f == NCH - 1)), r=[f"wdn{sl}", f"uT{us}"], w=[pk])
                    resid_add(pb, pk, cc, t0, n, l, 5)
        kb.barrier()
        ar.reset(m)

    def final_out():
        m = ar.mark()
        ybuf = [A(f"ybuf{i}", [128, D], F32) for i in range(2)]
        finw = A("finw", [128, D], F32)
        kb.dma("sp", I("dma_start", out=finw[:], in_=fnw_d[:].partition_broadcast(128)), "ld_bc2", w=["finw"])
        ssq = A("ssq", [128, 8], F32)
        junk = A("junk", [128, 512], F32)
        n_t = S // 128
        for it in range(n_t + 1):
            sl = it % 2
            rows = 128 if it < n_t else 16
            t0 = it * 128
            for half in range(2):
                pi = 2 * sl + half
                pb, pk = P[pi], f"P{pi}"
                for cc in range(4):
                    c = half * 4 + cc
                    kb.op("pe", I("transpose",
                        pb[0:rows, cc * 128:(cc + 1) * 128], xT[:, c, t0:t0 + rows], ident), r=["xT", "cf"], w=[pk])
                kb.op("act", I("activation",
                    out=junk[0:rows, :], in_=pb[0:rows, :], func=AF.Square, accum_out=ssq[0:rows, 2 * sl + half:2 * sl + half + 1]),
                    r=[pk], w=["junk", f"ssq{sl}{half}"])
            kb.op("dve", I("tensor_tensor",
                out=ssq[0:rows, 4 + sl:5 + sl], in0=ssq[0:rows, 2 * sl:2 * sl + 1], in1=ssq[0:rows, 2 * sl + 1:2 * sl + 2], op=ALU.add),
                r=[f"ssq{sl}0", f"ssq{sl}1"], w=[f"ssqs{sl}"])
            kb.op("act", I("activation",
                out=ssq[0:rows, 6 + sl:7 + sl], in_=ssq[0:rows, 4 + sl:5 + sl], func=AF.Sqrt, scale=1.0 / D, bias=EPS),
                r=[f"ssqs{sl}"], w=[f"ssqr{sl}"])
            kb.op("dve", I("reciprocal", ssq[0:rows, 6 + sl:7 + sl], ssq[0:rows, 6 + sl:7 + sl]),
                  r=[f"ssqr{sl}"], w=[f"ssqr{sl}"])
            for half in range(2):
                pi = 2 * sl + half
                pb, pk = P[pi], f"P{pi}"
                kb.op("dve", I("scalar_tensor_tensor",
                    out=ybuf[sl][0:rows, half * 512:(half + 1) * 512], in0=pb[0:rows, :], scalar=ssq[0:rows, 6 + sl:7 + sl],
                    in1=finw[0:rows, half * 512:(half + 1) * 512], op0=ALU.mult, op1=ALU.mult),
                    r=[pk, f"ssqr{sl}", "finw"], w=[f"ybuf{sl}"])
            dst = yp_d[t0:t0 + 128, :] if it < n_t else ys_d[:]
            key = f"y{it}"
            kb.dma("sp", I("dma_start", out=dst, in_=ybuf[sl][0:rows, :]),
                   f"st_y{sl}", r=[f"ybuf{sl}"], w=[key])
            out_keys.append(key)
        ar.reset(m)

    from_layers(cfg, nc, kb, ar, A, P, Pb, locals())
    return nc


def from_layers(cfg, nc, kb, ar, A, P, Pb, env):
    stages = getattr(cfg, "stages", ("mix0", "mlp0", "mix1", "mlp1"))
    if "mix0" in stages:
        layer0_mixer(cfg, nc, kb, ar, A, P, Pb, env)
    if "mlp0" in stages:
        env["mlp"](0)
    if "mix1" in stages:
        layer1_mixer(cfg, nc, kb, ar, A, P, Pb, env)
    if "mlp1" in stages:
        env["mlp"](1)
    env["final_out"]()
    waits = kb._deps("sp", tuple(env["out_keys"]), ())
    kb.q["sp"].append((waits, None, None, 0))
    kb.emit()


def prep_shared(cfg, inp):
    f = lambda a: np.ascontiguousarray(np.asarray(a))
    w4 = np.asarray(inp["w_in_ab"])[0].reshape(D, 8, 4, 128)
    units = []
    for u in range(4):
        units.append(np.concatenate([w4[:, 0, u], w4[:, 1, u], w4[:, 3, u], w4[:, 2, u]], axis=1))
    for u in range(4):
        units.append(np.concatenate([w4[:, 4, u], w4[:, 5, u], w4[:, 7, u], w4[:, 6, u]], axis=1))
    wc = np.asarray(inp["w_in_c"])[0]
    heads = []
    for h in range(4):
        heads.append(np.concatenate([wc[:, h * 256:(h + 1) * 256], wc[:, 1024 + h * 256:1024 + (h + 1) * 256],
                                     wc[:, 2048 + h * 256:2048 + (h + 1) * 256]], axis=1))
    cf, cfb, rope = make_consts(cfg)
    return {
        "ck": f(np.asarray(inp["cache_k"])[0].reshape(cfg.NPHYS, 128, 1024)),
        "cv": f(np.asarray(inp["cache_v"])[0].reshape(cfg.NPHYS, 128, 1024)),
        "w_ada": f(inp["w_ada"]), "b_ada": f(np.asarray(inp["b_ada"]).reshape(96, 128)),
        "norm_w": f(np.asarray(inp["norm_w"]).reshape(32, 128)),
        "w_in_ab": f(np.stack(units)), "w_out_ab": f(np.asarray(inp["w_out_ab"])[0]),
        "lb_logits": f(np.asarray(inp["hgrn_lb_logits"]).reshape(8, 128)),
        "hgrn_norm_w": f(np.asarray(inp["hgrn_norm_w"]).reshape(1, 128)),
        "w_in_c": f(np.stack(heads)), "w_out_c": f(np.asarray(inp["w_out_c"])[0]),
        "diff_lambda": f(np.asarray(inp["diff_lambda"]).reshape(1, 512)),
        "subln_w": f(np.asarray(inp["diff_subln_w"]).reshape(1, 256)),
        "w_up": f(inp["w_mlp_up"]), "w_down": f(inp["w_mlp_down"]),
        "final_norm_w": f(np.asarray(inp["final_norm_w"]).reshape(1, D)),
        "cf": cf, "cfb": cfb, "rope": rope,
    }


def prep_core(cfg, inp, shared, c):
    f = lambda a: np.ascontiguousarray(np.asarray(a))
    m = dict(shared)
    m["xp"] = f(np.asarray(inp["x_prompt"])[c])
    m["xs"] = f(np.asarray(inp["x_sample"])[4 * c:4 * c + 4].reshape(16, D))
    m["st_h"] = f(np.asarray(inp["state_hgrn"])[0, 4 * c:4 * c + 4])
    m["st_r"] = f(np.asarray(inp["state_ret"])[0, 4 * c:4 * c + 4])
    m["pt"] = f(np.asarray(inp["page_table"])[4 * c:4 * c + 4].reshape(1, 4 * cfg.NPG).astype(np.int32))
    m["cvec"] = f(np.concatenate([np.asarray(inp["c_prompt"])[c:c + 1], np.asarray(inp["c_sample"])[4 * c:4 * c + 4]], axis=0))
    return m


def assemble(cfg, res):
    n = cfg.n_cores
    S = cfg.S
    g = lambda k: [np.asarray(r[k]) for r in res]
    y_p = np.stack(g("y_p"))
    y_s = np.concatenate([a.reshape(4, 4, D) for a in g("y_s")], axis=0)
    hg_p = np.stack(g("hg_p"))[None]
    rt_p = np.stack(g("rt_p"))[None]
    k_p = np.stack([a.reshape(S // 128, 128, 4, 2, 128) for a in g("k_p")])[None]
    v_p = np.stack([a.reshape(S // 128, 128, 4, 256) for a in g("v_p")])[None]
    hg_s = np.concatenate(g("hg_s"), axis=0)[None]
    rt_s = np.concatenate(g("rt_s"), axis=0)[None]
    k_s = np.concatenate([a.reshape(4, 4, 4, 2, 128) for a in g("k_s")], axis=0)[None]
    v_s = np.concatenate([a.reshape(4, 4, 4, 256) for a in g("v_s")], axis=0)[None]
    return tuple(np.ascontiguousarray(a.astype(np.float32)) for a in (y_p, y_s, hg_p, rt_p, k_p, v_p, hg_s, rt_s, k_s, v_s))


_NC_CACHE = {}


def kernel(**inputs):
    cfg = Cfg()
    if "nc" not in _NC_CACHE:
        _NC_CACHE["nc"] = build_program(cfg)
    nc = _NC_CACHE["nc"]
    shared = prep_shared(cfg, inputs)
    in_maps = [prep_core(cfg, inputs, shared, c) for c in range(cfg.n_cores)]
    res = run_bass_kernel_spmd(nc, in_maps, core_ids=list(range(cfg.n_cores)))
    return assemble(cfg, res.results)


def layer0_mixer(cfg, nc, kb, ar, A, P, Pb, env):
    S, NT = cfg.S, cfg.NT
    xT, hT, cf, identb, ones_f, zeros_f, lbv, na = (env[k] for k in
                                                    ("xT", "hT", "cf", "identb", "ones_f", "zeros_f", "lbv", "na"))
    sqb, rstd, tnrm = env["sqb"], env["rstd"], env["tnrm"]
    maskf = env["maskf"]
    d = env
    winab_d, woutab_d, rope_d = d["winab_d"], d["woutab_d"], d["rope_d"]
    sth_d, str_d, hgp_d, rtp_d, hgs_d, rts_d = d["sth_d"], d["str_d"], d["hgp_d"], d["rtp_d"], d["hgs_d"], d["rts_d"]
    out_keys = d["out_keys"]
    m0 = ar.mark()
    oT = A("oT", [128, NCH, NT], BF16)
    cfb = A("cfb", [128, NCFB], F32)
    kb.dma("sp", I("dma_start", out=cfb[:], in_=d["cfb_d"][:]), "ld_cfb", w=["cfb"])
    m_w = ar.mark()
    wu = [A(f"wu{i}", [128, NCH, 512], BF16) for i in range(2)]
    wR = A("wR", [128, NCH, 256], BF16)
    f32t = {k: A(k, [128, 512], F32) for k in ("qs", "sg", "om", "rb")}
    bb2 = [A(f"bb{i}", [128, 512], F32) for i in range(2)]
    qTt2 = [A(f"qTt{i}", [128, 512], BF16) for i in range(2)]
    kTt2 = [A(f"kTt{i}", [128, 512], BF16) for i in range(2)]
    gs2 = [A(f"gs{i}", [128, 512], BF16) for i in range(2)]
    vTt2 = [A(f"vTt{i}", [128, 512], BF16) for i in range(2)]
    tcnt = [0]
    zeros128 = A("zeros128", [128, 128], F32)
    kb.op("pool", I("memset", zeros128[:], 0.0), w=["zeros128"])
    cosT = A("cosT", [128, 512], F32); sinT = A("sinT", [128, 512], F32)
    Am2 = [A(f"Am{i}", [128, 128], BF16) for i in range(2)]
    kvtok2 = [A(f"kvtok{i}", [128, 256], BF16) for i in range(2)]
    qd2 = [A(f"qd{i}", [128, 128], BF16) for i in range(2)]
    xcnt = [0]
    U = A("U", [128, 128], F32); Sbf = A("Sbf", [128, 128], BF16); Sfin = A("Sfin", [128, 128], F32)
    belast = A("belast", [128, 1], F32)
    qs, sg, om, rb = (f32t[k] for k in ("qs", "sg", "om", "rb"))

    env["prenorm_tile"](0, 0, 0)

    pjc = [0]

    def proj(ws_ap_fn, t0, n, pi_unused, rkeys):
        pi = pjc[0] % 2
        pjc[0] += 1
        pb, pk = P[pi], f"P{pi}"
        for c in range(NCH):
            kb.op("pe", I("matmul", pb[:, 0:n], lhsT=ws_ap_fn(c), rhs=hT[:, c, t0:t0 + n],
                                                start=(c == 0), stop=(c == NCH - 1)), r=list(rkeys) + [f"hT{t0 // 512}"], w=[pk])
        return pb, pk

    for u in range(8):
        ret = u >= 4
        h = u % 4
        sl = u % 2
        wk = f"wu{sl}"
        kb.dma("pool", I("dma_start", out=wu[sl][:], in_=winab_d[u].rearrange("(c p) n -> p c n", p=128)),
               f"ld_wu{sl}", w=[wk])
        if ret:
            wuv = wu[sl][:, :, 0:256].rearrange("p c (i two) -> p c i two", two=2)
            wRv = wR[:].rearrange("p c (i two) -> p c i two", two=2)
            kb.op("pool", I("tensor_scalar", out=wRv[:, :, :, 0], in0=wuv[:, :, :, 1], scalar1=-1.0, scalar2=None,
                                                                     op0=ALU.mult), r=[wk], w=["wR"])
            kb.op("pool", I("tensor_copy", wRv[:, :, :, 1], wuv[:, :, :, 0]), r=[wk], w=["wR"])
            g_h = 1.0 - 2.0 ** (-5.0 - h)
        st_in = str_d if ret else sth_d
        st_out_p = rtp_d if ret else hgp_d
        st_out_s = rts_d if ret else hgs_d
        for (t0, n) in cfg.tiles:
            is_s = t0 >= S
            W = wu[sl]
            if u == 0 and t0 // 512 + 1 < len(cfg.tiles):
                env["prenorm_tile"](0, 0, t0 // 512 + 1)
            tb = tcnt[0] % 2
            tcnt[0] += 1
            qTt, kTt, gs, vTt, bb = qTt2[tb], kTt2[tb], gs2[tb], vTt2[tb], bb2[tb]
            kq, kk, kg, kv, kbb = f"qTt{tb}", f"kTt{tb}", f"gs{tb}", f"vTt{tb}", f"bb{tb}"
            if not ret:
                pb, pk = proj(lambda c: W[:, c, 0:128], t0, n, 0, [wk])
                kb.op("act", I("activation", out=qs[:, 0:n], in_=pb[:, 0:n], func=AF.Silu), r=[pk], w=["qs"])
                pb, pk = proj(lambda c: W[:, c, 128:256], t0, n, 1, [wk])
                kb.op("act", I("activation", out=sg[:, 0:n], in_=pb[:, 0:n], func=AF.Sigmoid), r=[pk], w=["sg"])
                pb, pk = proj(lambda c: W[:, c, 256:384], t0, n, 2, [wk])
                kb.op("act", I("activation", out=gs[:, 0:n], in_=pb[:, 0:n], func=AF.Silu), r=[pk], w=[kg])
                pb, pk = proj(lambda c: W[:, c, 384:512], t0, n, 0, [wk])
                kb.op("act", I("copy", vTt[:, 0:n], pb[:, 0:n]), r=[pk], w=[kv])
                kb.op("dve", I("tensor_scalar", out=om[:, 0:n], in0=sg[:, 0:n], scalar1=lbv[:, 8 + h:9 + h], scalar2=lbv[:, 4 + h:5 + h],
                                                       op0=ALU.mult, op1=ALU.add), r=["sg", "lbv"], w=["om"])
                kb.op("dve", I("tensor_scalar", out=sg[:, 0:n], in0=sg[:, 0:n], scalar1=lbv[:, 4 + h:5 + h], scalar2=lbv[:, h:h + 1],
                                                       op0=ALU.mult, op1=ALU.add), r=["sg", "lbv"], w=["sg"])
                C = 4 if is_s else 128
                for c0 in range(0, n, C):
                    kb.op("dve", I("tensor_tensor_scan",
                        out=bb[:, c0:c0 + C], data0=sg[:, c0:c0 + C], data1=zeros128[:, 0:C], initial=1.0, op0=ALU.mult, op1=ALU.add),
                        r=["sg", "zeros128"], w=[kbb])
                kb.op("dve", I("reciprocal", rb[:, 0:n], bb[:, 0:n]), r=[kbb], w=["rb"])
                kb.op("dve", I("scalar_tensor_tensor", out=qTt[:, 0:n], in0=qs[:, 0:n], scalar=DK_SCALE, in1=bb[:, 0:n],
                                                              op0=ALU.mult, op1=ALU.mult), r=["qs", kbb], w=[kq])
                kb.op("dve", I("tensor_tensor", out=kTt[:, 0:n], in0=om[:, 0:n], in1=rb[:, 0:n], op=ALU.mult),
                      r=["om", "rb"], w=[kk])
            else:
                kb.dma("sp", I("dma_start", out=cosT[:, 0:n], in_=rope_d[0, :, t0:t0 + n]), "ld_cos", w=["cosT"])
                kb.dma("sp", I("dma_start", out=sinT[:, 0:n], in_=rope_d[1, :, t0:t0 + n]), "ld_sin", w=["sinT"])
                pa, pka = proj(lambda c: W[:, c, 0:128], t0, n, 0, [wk])
                pr, pkr = proj(lambda c: wR[:, c, 0:128], t0, n, 1, ["wR"])
                kb.op("dve", I("tensor_tensor", out=qs[:, 0:n], in0=pa[:, 0:n], in1=cosT[:, 0:n], op=ALU.mult),
                      r=[pka, "cosT"], w=["qs"])
                kb.op("dve", I("tensor_tensor", out=sg[:, 0:n], in0=pr[:, 0:n], in1=sinT[:, 0:n], op=ALU.mult),
                      r=[pkr, "sinT"], w=["sg"])
                kb.op("pool", I("tensor_tensor", out=qTt[:, 0:n], in0=qs[:, 0:n], in1=sg[:, 0:n], op=ALU.add),
                      r=["qs", "sg"], w=[kq])
                pa, pka = proj(lambda c: W[:, c, 128:256], t0, n, 2, [wk])
                pr, pkr = proj(lambda c: wR[:, c, 128:256], t0, n, 0, ["wR"])
                kb.op("dve", I("scalar_tensor_tensor", out=om[:, 0:n], in0=pa[:, 0:n], scalar=DK_SCALE, in1=cosT[:, 0:n],
                                                                     op0=ALU.mult, op1=ALU.mult), r=[pka, "cosT"], w=["om"])
                kb.op("dve", I("scalar_tensor_tensor", out=rb[:, 0:n], in0=pr[:, 0:n], scalar=DK_SCALE, in1=sinT[:, 0:n],
                                                                     op0=ALU.mult, op1=ALU.mult), r=[pkr, "sinT"], w=["rb"])
                kb.op("pool", I("tensor_tensor", out=kTt[:, 0:n], in0=om[:, 0:n], in1=rb[:, 0:n], op=ALU.add),
                      r=["om", "rb"], w=[kk])
                pb, pk = proj(lambda c: W[:, c, 256:384], t0, n, 1, [wk])
                kb.op("act", I("activation", out=gs[:, 0:n], in_=pb[:, 0:n], func=AF.Silu), r=[pk], w=[kg])
                pb, pk = proj(lambda c: W[:, c, 384:512], t0, n, 2, [wk])
                kb.op("act", I("copy", vTt[:, 0:n], pb[:, 0:n]), r=[pk], w=[kv])
                C = 4 if is_s else 128

            nchunks = n // C

            def stage_x(ci):
                c0 = ci * C
                b2 = xcnt[0] % 2
                xcnt[0] += 1
                xb[ci] = b2
                state_zero = (not is_s) and t0 == 0 and ci == 0
                pa, pka = P[2 + b2], f"P{2 + b2}"
                pt, pkt = Pb[4 + b2], f"P{4 + b2}"
                kb.op("pe", I("matmul", pa[0:C, 0:C], lhsT=kTt[:, c0:c0 + C], rhs=qTt[:, c0:c0 + C], start=True, stop=True),
                      r=[kk, kq], w=[pka])
                kb.op("pe", I("transpose", pt[0:C, 0:128], kTt[:, c0:c0 + C], identb[:]), r=[kk, "identb"], w=[pkt])
                kb.op("pe", I("transpose", pt[0:C, 128:256], vTt[:, c0:c0 + C], identb[:]), r=[kv, "identb"], w=[pkt])
                if not ret:
                    mk_ap = maskf[0:C, 0:C]
                else:
                    mk_ap = cfb[0:C, CF_DT + h * 128:CF_DT + h * 128 + C]
                kb.op("dve", I("tensor_tensor", out=Am2[b2][0:C, 0:C], in0=pa[0:C, 0:C], in1=mk_ap, op=ALU.mult),
                      r=[pka, "cf", "cfb"], w=[f"Am{b2}"])
                if not ret:
                    kb.op("act", I("copy", kvtok2[b2][0:C, :], pt[0:C, 0:256]), r=[pkt], w=[f"kvtok{b2}"])
                else:
                    kd = cfb[0:C, (CF_KD4 if is_s else CF_KD128) + h:(CF_KD4 if is_s else CF_KD128) + h + 1]
                    kb.op("act", I("activation", out=kvtok2[b2][0:C, 0:128], in_=pt[0:C, 0:128], func=AF.Identity, scale=kd),
                          r=[pkt, "cfb"], w=[f"kvtok{b2}"])
                    kb.op("act", I("copy", kvtok2[b2][0:C, 128:256], pt[0:C, 128:256]), r=[pkt], w=[f"kvtok{b2}"])
                    if not state_zero:
                        kb.op("pool", I("tensor_tensor", out=qd2[b2][:, 0:C], in0=qTt[:, c0:c0 + C], in1=cfb[:, CF_G1 + h * 128:CF_G1 + h * 128 + C],
                                        op=ALU.mult), r=[kq, "cfb"], w=[f"qd{b2}"])

            def stage_y(ci):
                c0 = ci * C
                b2 = xb[ci]
                Amb, kvb, qdb = Am2[b2], kvtok2[b2], qd2[b2]
                seq = ci if is_s else None
                state_zero = (not is_s) and t0 == 0 and ci == 0
                if is_s:
                    kb.dma("sp", I("dma_start", out=U[:], in_=st_in[seq, h]), "ld_U", w=["U"])
                    kb.op("dve", I("tensor_copy", Sbf[:], U[:]), r=["U"], w=["Sbf"])
                kb.op("pe", I("matmul", P[7][:, c0:c0 + C], lhsT=kvb[0:C, 128:256], rhs=Amb[0:C, 0:C],
                              start=True, stop=state_zero), r=[f"kvtok{b2}", f"Am{b2}"], w=["P7"])
                if not state_zero:
                    q_in = qdb[:, 0:C] if ret else qTt[:, c0:c0 + C]
                    kb.op("pe", I("matmul", P[7][:, c0:c0 + C], lhsT=Sbf[:], rhs=q_in, start=False, stop=True),
                          r=["Sbf", f"qd{b2}", kq], w=["P7"])
                kb.op("pe", I("matmul", P[6][:, 0:128], lhsT=kvb[0:C, 0:128], rhs=kvb[0:C, 128:256], start=True, stop=True),
                      r=[f"kvtok{b2}"], w=["P6"])
                if state_zero:
                    kb.op("dve", I("tensor_copy", U[:], P[6][:, 0:128]), r=["P6"], w=["U"])
                else:
                    if ret:
                        sc_prev = g_h ** C
                    elif is_s:
                        sc_prev = 1.0
                    elif ci == 0:
                        sc_prev = belast[:, 0:1]
                    else:
                        sc_prev = bb[:, c0 - 1:c0]
                    kb.op("dve", I("scalar_tensor_tensor", out=U[:], in0=U[:], scalar=sc_prev, in1=P[6][:, 0:128],
                                                                                 op0=ALU.mult, op1=ALU.add),
                          r=["U", "P6", kbb, "belast"], w=["U"])
                last_of_seq = is_s or (t0 + n == S and ci == nchunks - 1)
                be_cur = bb[:, c0 + C - 1:c0 + C]
                if not last_of_seq:
                    if ret:
                        kb.op("act", I("copy", Sbf[:], U[:]), r=["U"], w=["Sbf"])
                    else:
                        kb.op("dve", I("tensor_scalar", out=Sbf[:], in0=U[:], scalar1=be_cur, scalar2=None, op0=ALU.mult),
                              r=["U", kbb], w=["Sbf"])
                        if ci == nchunks - 1:
                            kb.op("dve", I("tensor_copy", belast[:], be_cur), r=[kbb], w=["belast"])
                else:
                    dst = st_out_s[seq, h] if is_s else st_out_p[h]
                    key = f"st_{u}_{seq}"
                    if ret:
                        kb.op("act", I("copy", Sfin[:], U[:]), r=["U"], w=["Sfin"])
                    else:
                        kb.op("dve", I("tensor_scalar", out=Sfin[:], in0=U[:], scalar1=be_cur, scalar2=None, op0=ALU.mult),
                              r=["U", kbb], w=["Sfin"])
                    kb.dma("sp", I("dma_start", out=dst, in_=Sfin[:]), "st_S", r=["Sfin"], w=[key])
                    out_keys.append(key)

            xb = {}
            stage_x(0)
            for ci in range(nchunks):
                if ci + 1 < nchunks:
                    stage_x(ci + 1)
                stage_y(ci)

            kb.op("act", I("activation", out=sqb[0][:, 0:n], in_=P[7][:, 0:n], func=AF.Square), r=["P7"], w=["sqb0"])
            pns = pjc[0] % 2
            pjc[0] += 1
            kb.op("pe", I("matmul", P[pns][:, 0:n], lhsT=ones_f[:], rhs=sqb[0][:, 0:n], start=True, stop=True),
                  r=["sqb0", "ones_f"], w=[f"P{pns}"])
            kb.op("act", I("activation", out=rstd[:, 0:n], in_=P[pns][:, 0:n], func=AF.Sqrt, scale=1.0 / 128, bias=EPS),
                  r=[f"P{pns}"], w=["rstd"])
            kb.op("dve", I("reciprocal", rstd[:, 0:n], rstd[:, 0:n]), r=["rstd"], w=["rstd"])
            kb.op("dve", I("tensor_tensor", out=tnrm[0][:, 0:n], in0=P[7][:, 0:n], in1=rstd[:, 0:n], op=ALU.mult),
                  r=["P7", "rstd"], w=["tnrm0"])
            nsc = 1.0 if ret else na[:, 0:1]
            kb.op("dve", I("scalar_tensor_tensor", out=oT[:, u, t0:t0 + n], in0=tnrm[0][:, 0:n], scalar=nsc, in1=gs[:, 0:n],
                                                                  op0=ALU.mult, op1=ALU.mult), r=["tnrm0", kg, "na"], w=["oT"])

    kb.barrier()
    m_end = ar.mark()
    ar.reset(m_w)
    wo = A("wo", [128, NCH, 1024], BF16)
    kb.dma("pool", I("dma_start", out=wo[:], in_=woutab_d[:].rearrange("(c p) n -> p c n", p=128)), "ld_wo", w=["wo"])
    npj = 0
    for (t0, n) in cfg.tiles:
        for cc in range(NCH):
            pi = npj % 4
            npj += 1
            pb, pk = P[pi], f"P{pi}"
            for c in range(NCH):
                kb.op("pe", I("matmul", pb[:, 0:n], lhsT=wo[:, c, cc * 128:(cc + 1) * 128], rhs=oT[:, c, t0:t0 + n],
                                                                  start=(c == 0), stop=(c == NCH - 1)), r=["wo", "oT"], w=[pk])
            env["resid_add"](pb, pk, cc, t0, n, 0, 2)
    kb.barrier()
    ar.reset(m0)


def layer1_mixer(cfg, nc, kb, ar, A, P, Pb, env):
    S, NT, NPG = cfg.S, cfg.NT, cfg.NPG
    xT, hT, cf, identb, maskb, neglam, sgn8, wsub = (env[k] for k in ("xT", "hT", "cf", "identb", "maskb", "neglam", "sgn8", "wsub"))
    sqb, rstd, tnrm = env["sqb"], env["rstd"], env["tnrm"]
    d = env
    winc_d, woutc_d, ck_d, cv_d, pt_d = d["winc_d"], d["woutc_d"], d["ck_d"], d["cv_d"], d["pt_d"]
    kp_d, vp_d, ks_d, vs_d = d["kp_d"], d["vp_d"], d["ks_d"], d["vs_d"]
    out_keys = d["out_keys"]
    NVT = S // 128
    NQT = S // 512
    VW = 264

    env["prenorm"](1, 0)
    kb.barrier()
    m0 = ar.mark()
    oTs = A("oTs", [128, NCH, 16], BF16)
    QTs = A("QTs", [128, 8, 16], BF16)
    KTs = A("KTs", [128, 8, 16], BF16)
    Vs = A("Vs", [4, 16, VW], BF16)
    small = A("small", [128, 8], F32)
    ssm = A("ssm", [128, 8], F32)
    oTh = A("oTh", [128, 2, S], BF16)
    wqkv = A("wqkv", [128, NCH, 768], BF16)
    woh = A("woh", [128, 2, 1024], BF16)
    QT = A("QT", [128, 2, S], BF16)
    KT = A("KT", [128, 2, S], BF16)
    Vtok = A("Vtok", [128, NVT, VW], BF16)
    stg = [A(f"stg{i}", [128, 512], F32) for i in range(2)]
    PT = [A(f"PT{i}", [128, 512], BF16) for i in range(2)]
    n1b = A("n1b", [128, 2, 256], F32)
    og = A("og", [128, 256], BF16)
    n1a = rstd[:].rearrange("p (a b) -> p a b", a=2)
    n1 = lambda qs: (n1a if qs < 2 else n1b)[:, qs % 2, :]
    dd = tnrm[0][:, 0:256]
    junk = tnrm[0][:, 256:512]
    NSL = 5
    Kpg = [A(f"Kpg{i}", [128, 1024], BF16) for i in range(NSL)]
    Vpg = [A(f"Vpg{i}", [128, 1024], BF16) for i in range(NSL)]
    KTpg = [A(f"KTpg{i}", [128, 8, 128], BF16) for i in range(2)]
    PTs = [A(f"PTs{i}", [128, 32], BF16) for i in range(2)]
    PTn = A("PTn", [4, 32], BF16)
    ogs = A("ogs", [16, 256], BF16)
    ones_b = A("ones_b", [128, 8], BF16)
    zeros_b = A("zeros_b", [128, 128], BF16)
    ptb = A("ptb", [128, 4 * NPG], I32)
    pid = A("pid", [128, 1], I32)
    pidx = A("pidx", [128, 4 * NPG], I32)
    acc = sqb[0][:, 0:260]
    nsg = sqb[1][:, 0:256]
    junk2 = tnrm[1][0:16, 0:256]
    dsm = tnrm[1][0:16, 256:512]

    kb.op("pool", I("memset", Vtok[:, :, 256:257], 1.0), w=["Vtok"])
    kb.op("pool", I("memset", Vs[:, :, 256:257], 1.0), w=["Vs"])
    kb.op("dve", I("memset", ones_b[:], 1.0), w=["ones_b"])
    kb.op("dve", I("memset", zeros_b[:], 0.0), w=["zeros_b"])
    kb.dma("sp", I("dma_start", out=ptb[:], in_=pt_d[:].partition_broadcast(128)), "ld_pts", w=["ptb"])
    kb.op("pool", I("iota", pid[:], pattern=[[0, 1]], base=0, channel_multiplier=1), w=["pid"])
    kb.op("dve", I("tensor_scalar", out=pidx[:], in0=ptb[:], scalar1=128, scalar2=pid[:, 0:1], op0=ALU.mult, op1=ALU.add),
          r=["ptb", "pid"], w=["pidx"])
    ck_rows = ck_d.rearrange("a p f -> (a p) f")
    cv_rows = cv_d.rearrange("a p f -> (a p) f")

    nproj = [0]

    def pbank():
        pi = 2 + nproj[0] % 2
        nproj[0] += 1
        return P[pi], f"P{pi}"

    nstg = [0]
    tick_hook = [lambda: None]

    def load_head_weights(h, with_out):
        kb.dma("pool", I("dma_start", out=wqkv[:], in_=winc_d[h].rearrange("(c p) n -> p c n", p=128)), "ld_wqkv", w=["wqkv"])
        if with_out:
            kb.dma("pool", I("dma_start", out=woh[:], in_=woutc_d[h * 256:(h + 1) * 256, :].rearrange("(c p) n -> p c n", p=128)),
                   "ld_woh", w=["woh"])

    def project(h, t0, n):
        is_s = t0 >= S
        for j in range(2):
            pb, pk = pbank()
            for c in range(NCH):
                kb.op("pe", I("matmul", pb[:, 0:n], lhsT=wqkv[:, c, j * 128:(j + 1) * 128], rhs=hT[:, c, t0:t0 + n],
                              start=(c == 0), stop=(c == NCH - 1)), r=["wqkv", f"hT{t0 // 512}"], w=[pk])
            if is_s:
                kb.op("act", I("activation", out=QTs[:, h * 2 + j, :], in_=pb[:, 0:16], func=AF.Identity, scale=DK_SCALE), r=[pk], w=["QTs"])
            else:
                kb.op("act", I("activation", out=QT[:, j, t0:t0 + n], in_=pb[:, 0:n], func=AF.Identity, scale=DK_SCALE), r=[pk], w=["QT"])
                tick_hook[0]()
        for j in range(2):
            pb, pk = pbank()
            for c in range(NCH):
                kb.op("pe", I("matmul", pb[:, 0:n], lhsT=wqkv[:, c, 256 + j * 128:256 + (j + 1) * 128], rhs=hT[:, c, t0:t0 + n],
                              start=(c == 0), stop=(c == NCH - 1)), r=["wqkv", f"hT{t0 // 512}"], w=[pk])
            if is_s:
                kb.op("dve", I("tensor_copy", KTs[:, h * 2 + j, :], pb[:, 0:16]), r=[pk], w=["KTs"])
            else:
                kb.op("dve", I("tensor_copy", KT[:, j, t0:t0 + n], pb[:, 0:n]), r=[pk], w=["KT"])
                tick_hook[0]()
        subs = [(S + 4 * i, 4, i) for i in range(4)] if is_s else [(t0 + s * 128, 128, None) for s in range(n // 128)]
        for (c0, rows, seq) in subs:
            pb, pk = pbank()
            for c in range(NCH):
                kb.op("pe", I("matmul", pb[0:rows, 0:512], lhsT=hT[:, c, c0:c0 + rows], rhs=wqkv[:, c, 256:768],
                              start=(c == 0), stop=(c == NCH - 1)), r=["wqkv", f"hT{t0 // 512}"], w=[pk])
            si = nstg[0] % 2
            nstg[0] += 1
            kb.op("act", I("copy", stg[si][0:rows, :], pb[0:rows, 0:512]), r=[pk], w=[f"stg{si}"])
            if seq is None:
                kdst = kp_d[c0:c0 + rows, h * 256:(h + 1) * 256]
                vdst = vp_d[c0:c0 + rows, h * 256:(h + 1) * 256]
                kb.op("dve", I("tensor_copy", Vtok[:, c0 // 128, 0:256], stg[si][:, 256:512]), r=[f"stg{si}"], w=["Vtok"])
            else:
                r0 = c0 - S
                kdst = ks_d[r0:r0 + rows, h * 256:(h + 1) * 256]
                vdst = vs_d[r0:r0 + rows, h * 256:(h + 1) * 256]
                kb.op("dve", I("tensor_copy", Vs[0:4, seq * 4 + h, 0:256], stg[si][0:4, 256:512]), r=[f"stg{si}"], w=["Vs"])
            key = f"kv_{h}_{c0}"
            kb.dma("sp", I("dma_start", out=kdst, in_=stg[si][0:rows, 0:256]), f"st_k{si}", r=[f"stg{si}"], w=[key + "k"])
            kb.dma("sp", I("dma_start", out=vdst, in_=stg[si][0:rows, 256:512]), f"st_v{si}", r=[f"stg{si}"], w=[key + "v"])
            out_keys.extend([key + "k", key + "v"])
            if seq is None:
                tick_hook[0]()

    def sample_gen():
        pages = [(i, p) for i in range(4) for p in range(NPG)]
        NP = len(pages)

        def stageA(n):
            sl, s2 = n % NSL, n % 2
            for hj in range(8):
                kb.op("pe", I("transpose", Pb[0][:, hj * 128:(hj + 1) * 128], Kpg[sl][:, hj * 128:(hj + 1) * 128], identb[:]),
                      r=[f"Kpg{sl}", "identb"], w=["P0"])
            if s2 == 0:
                kb.op("dve", I("tensor_copy", KTpg[s2][:].rearrange("p a b -> p (a b)"), Pb[0][:, 0:1024]), r=["P0"], w=[f"KTpg{s2}"])
            else:
                kb.op("act", I("copy", KTpg[s2][:].rearrange("p a b -> p (a b)"), Pb[0][:, 0:1024]), r=["P0"], w=[f"KTpg{s2}"])

        def stageB(n):
            i, p = pages[n]
            s2 = n % 2
            for hj in range(8):
                kb.op("pe", I("matmul", P[1][:, hj * 4:(hj + 1) * 4], lhsT=KTpg[s2][:, hj, :], rhs=QTs[:, hj, 4 * i:4 * i + 4], start=True, stop=True),
                      r=[f"KTpg{s2}", "QTs"], w=["P1"])
            kb.op("act", I("activation", out=PTs[s2][:], in_=P[1][:, 0:32], func=AF.Exp), r=["P1"], w=[f"PTs{s2}"])

        def stageC(n):
            i, p = pages[n]
            sl, s2 = n % NSL, n % 2
            if p == 0:
                kb.op("pool", I("memset", acc, 0.0), w=["acc"])
            for h in range(4):
                kb.op("pe", I("matmul", P[1][32 * h:32 * h + 8, 64:320], lhsT=PTs[s2][:, h * 8:(h + 1) * 8], rhs=Vpg[sl][:, h * 256:(h + 1) * 256],
                              start=True, stop=False, tile_position=(0, 32 * h)), r=[f"PTs{s2}", f"Vpg{sl}"], w=["P1"])
                kb.op("pe", I("matmul", P[1][32 * h:32 * h + 8, 320:321], lhsT=PTs[s2][:, h * 8:(h + 1) * 8], rhs=ones_b[:, 0:1],
                              start=False, stop=True, tile_position=(0, 32 * h)), r=[f"PTs{s2}", "ones_b"], w=["P1"])
            kb.op("dve", I("tensor_tensor", out=acc[:, 0:257], in0=P[1][:, 64:321], in1=acc[:, 0:257], op=ALU.add), r=["P1", "acc"], w=["acc"])
            if p == NPG - 1:
                finish_seq(i)

        def finish_seq(i):
            for hj in range(8):
                kb.op("pe", I("matmul", P[1][0:4, hj * 4:(hj + 1) * 4], lhsT=KTs[:, hj, 4 * i:4 * i + 4], rhs=QTs[:, hj, 4 * i:4 * i + 4],
                              start=True, stop=True), r=["KTs", "QTs"], w=["P1"])
            kb.op("act", I("activation", out=PTn[:], in_=P[1][0:4, 0:32], func=AF.Exp), r=["P1"], w=["PTn"])
            kb.op("dve", I("tensor_tensor", out=PTn[:].rearrange("p (a b) -> p a b", a=8), in0=PTn[:].rearrange("p (a b) -> p a b", a=8),
                           in1=maskb[0:4, 0:4].unsqueeze(1).to_broadcast([4, 8, 4]), op=ALU.mult), r=["PTn", "maskb"], w=["PTn"])
            for h in range(4):
                kb.op("pe", I("matmul", P[1][32 * h:32 * h + 8, 64:321], lhsT=PTn[0:4, h * 8:(h + 1) * 8], rhs=Vs[0:4, i * 4 + h, 0:257],
                              start=True, stop=True, tile_position=(0, 32 * h)), r=["PTn", "Vs"], w=["P1"])
            kb.op("dve", I("tensor_tensor", out=acc[:, 0:257], in0=P[1][:, 64:321], in1=acc[:, 0:257], op=ALU.add), r=["P1", "acc"], w=["acc"])
            kb.op("dve", I("tensor_scalar", out=ssm[:, 0:1], in0=acc[:, 256:257], scalar1=1e-30, scalar2=None, op0=ALU.add), r=["acc"], w=["ssm0"])
            kb.op("dve", I("reciprocal", ssm[:, 1:2], ssm[:, 0:1]), r=["ssm0"], w=["ssm1"])
            kb.op("dve", I("tensor_tensor", out=ssm[:, 2:3], in0=ssm[:, 1:2], in1=sgn8[:, 0:1], op=ALU.mult), r=["ssm1", "sgn8"], w=["ssm2"])
            kb.op("dve", I("tensor_scalar", out=nsg, in0=acc[:, 0:256], scalar1=ssm[:, 2:3], scalar2=None, op0=ALU.mult), r=["acc", "ssm2"], w=["nsg"])
            kb.op("pe", I("matmul", P[0][0:16, 0:256], lhsT=cf[:, CF_SEL2:CF_SEL2 + 16], rhs=nsg, start=True, stop=True), r=["nsg", "cf"], w=["P0"])
            kb.op("act", I("copy", dsm, P[0][0:16, 0:256]), r=["P0"], w=["dsm"])
            kb.op("act", I("activation", out=junk2, in_=dsm, func=AF.Square, accum_out=ssm[0:16, 3:4]), r=["dsm"], w=["junk2", "ssm3"])
            kb.op("act", I("activation", out=ssm[0:16, 4:5], in_=ssm[0:16, 3:4], func=AF.Sqrt, scale=1.0 / 256, bias=EPS), r=["ssm3"], w=["ssm4"])
            kb.op("dve", I("reciprocal", ssm[0:16, 5:6], ssm[0:16, 4:5]), r=["ssm4"], w=["ssm5"])
            kb.op("dve", I("scalar_tensor_tensor", out=ogs[:], in0=dsm, scalar=ssm[0:16, 5:6], in1=wsub[0:16, :], op0=ALU.mult, op1=ALU.mult),
                  r=["dsm", "ssm5", "wsub"], w=["ogs"])
            for half in range(2):
                kb.op("pe", I("transpose", Pb[0][:, half * 16:(half + 1) * 16], ogs[:, half * 128:(half + 1) * 128], identb[0:16, 0:16]),
                      r=["ogs", "identb"], w=["P0"])
            for half in range(2):
                kb.op("dve", I("tensor_copy", oTs[:].rearrange("p (h two) t -> p two h t", two=2)[:, half, :, 4 * i:4 * i + 4],
                               Pb[0][:, half * 16:(half + 1) * 16].rearrange("p (h q) -> p h q", h=4)), r=["P0"], w=["oTs"])

        for (c0_, c1_) in ((64, 192), (192, 320), (320, 321)):
            kb.op("pe", I("matmul", P[1][:, c0_:c1_], lhsT=zeros_b[:], rhs=identb[:, 0:c1_ - c0_], start=True, stop=True),
                  r=["zeros_b", "identb"], w=["P1"])
        kdma = lambda n: kb.dma("pool", I("indirect_dma_start", out=Kpg[n % NSL][:], out_offset=None, in_=ck_rows,
                                          in_offset=bass.IndirectOffsetOnAxis(ap=pidx[:, pages[n][0] * NPG + pages[n][1]:pages[n][0] * NPG + pages[n][1] + 1], axis=0)),
                                f"ld_kpg{n % NSL}", r=["pidx"], w=[f"Kpg{n % NSL}"])
        vdma = lambda n: kb.dma("pool", I("indirect_dma_start", out=Vpg[n % NSL][:], out_offset=None, in_=cv_rows,
                                          in_offset=bass.IndirectOffsetOnAxis(ap=pidx[:, pages[n][0] * NPG + pages[n][1]:pages[n][0] * NPG + pages[n][1] + 1], axis=0)),
                                f"ld_vpg{n % NSL}", r=["pidx"], w=[f"Vpg{n % NSL}"])
        for n in range(min(NSL - 1, NP)):
            kdma(n)
        for n in range(min(NSL - 2, NP)):
            vdma(n)
        for sstep in range(NP + 2):
            if sstep < NP:
                stageA(sstep)
            if 0 <= sstep - 1 < NP:
                stageB(sstep - 1)
            if 0 <= sstep - 2 < NP:
                stageC(sstep - 2)
            if sstep + NSL - 1 < NP:
                kdma(sstep + NSL - 1)
            if sstep + NSL - 2 < NP:
                vdma(sstep + NSL - 2)
            yield

    for h in range(4):
        load_head_weights(h, False)
        project(h, S, 16)
    gen = sample_gen()
    n_points = 4 * ((S // 512) * 8 + sum(4 * qt + 4 for qt in range(NQT)) * 2 + (S // 512) * 8)
    n_steps = 4 * NPG + 2
    pace = {"pts": 0, "done": 0}

    def tick():
        pace["pts"] += 1
        want = min(n_steps, (pace["pts"] * n_steps + n_points - 1) // n_points)
        while pace["done"] < want:
            next(gen, None)
            pace["done"] += 1

    tick_hook[0] = tick

    for h in range(4):
        load_head_weights(h, True)
        for (t0, n) in cfg.tiles:
            if t0 < S:
                project(h, t0, n)
        nst = [0]
        for qt in range(NQT):
            for j in range(2):
                nk = 4 * qt + 4
                bufs = {}

                def scores(kbi):
                    n0 = max(0, kbi - 4 * qt)
                    sb = 2 + nst[0] % 2
                    ps = nst[0] % 2
                    nst[0] += 1
                    bufs[kbi] = ps
                    q0 = qt * 512 + n0 * 128
                    kb.op("pe", I("matmul", P[sb][:, n0 * 128:512], lhsT=KT[:, j, kbi * 128:(kbi + 1) * 128], rhs=QT[:, j, q0:(qt + 1) * 512],
                                  start=True, stop=True), r=["KT", "QT"], w=[f"P{sb}"])
                    kb.op("act", I("activation", out=PT[ps][:, n0 * 128:512], in_=P[sb][:, n0 * 128:512], func=AF.Exp),
                          r=[f"P{sb}"], w=[f"PT{ps}"])
                    if kbi >= 4 * qt:
                        kb.op("dve", I("tensor_tensor", out=PT[ps][:, n0 * 128:(n0 + 1) * 128], in0=PT[ps][:, n0 * 128:(n0 + 1) * 128],
                                       in1=maskb[:], op=ALU.mult), r=[f"PT{ps}", "maskb"], w=[f"PT{ps}"])

                scores(0)
                for kbi in range(nk):
                    n0 = max(0, kbi - 4 * qt)
                    if kbi + 1 < nk:
                        scores(kbi + 1)
                    ps = bufs[kbi]
                    for qs in range(n0, 4):
                        kb.op("pe", I("matmul", P[4 + qs][:, 0:257], lhsT=PT[ps][:, qs * 128:(qs + 1) * 128], rhs=Vtok[:, kbi, 0:257],
                                      start=(kbi == 0), stop=(kbi == 4 * qt + qs)), r=[f"PT{ps}", "Vtok"], w=[f"P{4 + qs}"])
                    tick()
                for qs in range(4):
                    ob, ok = P[4 + qs], f"P{4 + qs}"
                    tok0 = qt * 512 + qs * 128
                    if j == 0:
                        kb.op("dve", I("reciprocal", small[:, 0:1], ob[:, 256:257]), r=[ok], w=["sm0"])
                        kb.op("act", I("activation", out=n1(qs), in_=ob[:, 0:256], func=AF.Identity, scale=small[:, 0:1]),
                              r=[ok, "sm0"], w=[f"n1{qs}"])
                    else:
                        kb.op("dve", I("reciprocal", small[:, 1:2], ob[:, 256:257]), r=[ok], w=["sm1"])
                        kb.op("dve", I("tensor_tensor", out=small[:, 2:3], in0=small[:, 1:2], in1=neglam[:, 0:1], op=ALU.mult),
                              r=["sm1", "neglam"], w=["sm2"])
                        kb.op("dve", I("scalar_tensor_tensor", out=dd, in0=ob[:, 0:256], scalar=small[:, 2:3], in1=n1(qs),
                                       op0=ALU.mult, op1=ALU.add), r=[ok, "sm2", f"n1{qs}"], w=["dd"])
                        kb.op("act", I("activation", out=junk, in_=dd, func=AF.Square, accum_out=small[:, 3:4]), r=["dd"], w=["junk1", "sm3"])
                        kb.op("act", I("activation", out=small[:, 4:5], in_=small[:, 3:4], func=AF.Sqrt, scale=1.0 / 256, bias=EPS),
                              r=["sm3"], w=["sm4"])
                        kb.op("dve", I("reciprocal", small[:, 5:6], small[:, 4:5]), r=["sm4"], w=["sm5"])
                        kb.op("dve", I("scalar_tensor_tensor", out=og[:], in0=dd, scalar=small[:, 5:6], in1=wsub[:], op0=ALU.mult, op1=ALU.mult),
                              r=["dd", "sm5", "wsub"], w=["og"])
                        pb, pk = pbank()
                        pbb = pb[:].bitcast(BF16)
                        for half in range(2):
                            kb.op("pe", I("transpose", pbb[:, half * 128:(half + 1) * 128], og[:, half * 128:(half + 1) * 128], identb[:]),
                                  r=["og", "identb"], w=[pk])
                        kb.op("act", I("copy", oTh[:, :, tok0:tok0 + 128], pbb[:, 0:256].rearrange("p (c t) -> p c t", c=2)), r=[pk], w=["oTh"])
        for (t0, n) in cfg.tiles:
            if t0 >= S:
                continue
            for cc in range(NCH):
                pb, pk = pbank()
                for c2 in range(2):
                    kb.op("pe", I("matmul", pb[:, 0:n], lhsT=woh[:, c2, cc * 128:(cc + 1) * 128], rhs=oTh[:, c2, t0:t0 + n],
                                  start=(c2 == 0), stop=(c2 == 1)), r=["woh", "oTh"], w=[pk])
                env["resid_add"](pb, pk, cc, t0, n, 1, 2)
                tick()
    for _ in gen:
        pass

    for h in range(4):
        kb.dma("pool", I("dma_start", out=woh[:], in_=woutc_d[h * 256:(h + 1) * 256, :].rearrange("(c p) n -> p c n", p=128)),
               "ld_woh", w=["woh"])
        for cc in range(NCH):
            pb, pk = pbank()
            for c2 in range(2):
                kb.op("pe", I("matmul", pb[:, 0:16], lhsT=woh[:, c2, cc * 128:(cc + 1) * 128], rhs=oTs[:, 2 * h + c2, :],
                              start=(c2 == 0), stop=(c2 == 1)), r=["woh", "oTs"], w=[pk])
            env["resid_add"](pb, pk, cc, S, 16, 1, 2)
    kb.barrier()
    ar.reset(m0)
```
